# Optimizing a Trainium2 kernel written in Bass

```python
import math
import jax, jax.numpy as jnp
from jax import lax
import numpy as np

D_MODEL = 2048
BATCH = 2
SEQ = 4096
DEPTH = 2
DEC_BATCH = 128
DEC_SEQ = 1
PAST_LEN = 2048
PAGE_SIZE = 128

N_AB = (DEPTH + 1) // 2
N_C = DEPTH // 2
MIX_A = D_MODEL // 2
A_GROUPS = 4
A_GROUP_DIM = MIX_A // A_GROUPS
CHUNK = 128
MIX_B = D_MODEL // 2
B_HEADS = 8
B_QK_DIM = 64
B_K_ROW = 2 * B_QK_DIM
B_V_DIM = MIX_B // B_HEADS
QK_W = B_HEADS * 2 * B_QK_DIM
B_BLOCK = 128
ATTN_SCALE = 1.0 / math.sqrt(B_QK_DIM)
AB_IN = 3 * MIX_A + 2 * QK_W + 2 * MIX_B
C_WIDTH = D_MODEL
C_WINDOWS = (2, 4, 8, 16)
C_GROUPS = len(C_WINDOWS)
C_GROUP_DIM = C_WIDTH // C_GROUPS
C_HIST = max(C_WINDOWS) - 1
C_IN = 2 * C_WIDTH
ALPHA = (2 * DEPTH) ** 0.25
BETA = (8 * DEPTH) ** -0.25
LN_EPS = 1e-5
NEG = -1e30

kernel_name = "hybrid_gmlp_diffattn_pool_decode_step"


def layer_norm(x, g, b):
    xf = x.astype(jnp.float32)
    mu = jnp.mean(xf, axis=-1, keepdims=True)
    var = jnp.mean(jnp.square(xf - mu), axis=-1, keepdims=True)
    return ((xf - mu) * lax.rsqrt(var + LN_EPS) * g.astype(jnp.float32) + b.astype(jnp.float32)).astype(x.dtype)


def rms_norm(x, g):
    xf = x.astype(jnp.float32)
    ms = jnp.mean(jnp.square(xf), axis=-1, keepdims=True)
    return (xf * lax.rsqrt(ms + LN_EPS) * g.astype(jnp.float32)).astype(x.dtype)


def chunk_spatial_gate(u, v, w_s, b_s):
    bsz, t, _ = v.shape
    n_chunks = -(-t // CHUNK)
    pad = n_chunks * CHUNK - t
    vb = jnp.pad(v, ((0, 0), (0, pad), (0, 0))).reshape(bsz, n_chunks, CHUNK, A_GROUPS, A_GROUP_DIM)
    mask = jnp.tril(jnp.ones((CHUNK, CHUNK), dtype=bool))
    w = w_s * mask[None].astype(w_s.dtype)
    mixed = jnp.einsum('gij,bcjgd->bcigd', w, vb) + b_s.T[None, None, :, :, None]
    mixed = mixed.reshape(bsz, n_chunks * CHUNK, MIX_A)[:, :t]
    return u * mixed


def diff_weights(s, mask, lam):
    s = jnp.where(mask, s, NEG)
    p = jax.nn.softmax(s, axis=-1)
    return p[:, 0] - lam * p[:, 1]


def prompt_diff_attention(q, k, v, lam):
    bsz, t = q.shape[:2]
    nb = t // B_BLOCK
    q_blocks = q.reshape(bsz, nb, B_BLOCK, B_HEADS, 2, B_QK_DIM).transpose(1, 0, 2, 3, 4, 5)
    k_pos = jnp.arange(t)

    def block(args):
        qb, i = args
        q_pos = i * B_BLOCK + jnp.arange(B_BLOCK)
        s = jnp.einsum('bqhsd,bkhsd->bshqk', qb, k).astype(jnp.float32)
        a = diff_weights(s, k_pos[None, :] <= q_pos[:, None], lam)
        return jnp.einsum('bhqk,bkhd->bqhd', a.astype(v.dtype), v)

    o = lax.map(block, (q_blocks, jnp.arange(nb)))
    return o.transpose(1, 0, 2, 3, 4).reshape(bsz, t, B_HEADS, B_V_DIM)


def sample_diff_attention(q, k_new, v_new, k_past, v_past, lam):
    bsz, t = q.shape[:2]
    p = k_past.shape[1]
    kp = k_past.reshape(bsz, p, B_HEADS, 2, B_QK_DIM)
    s_past = jnp.einsum('bqhsd,bkhsd->bshqk', q, kp)
    s_new = jnp.einsum('bqhsd,bkhsd->bshqk', q, k_new)
    s = jnp.concatenate([s_past, s_new], axis=-1).astype(jnp.float32)
    mask = jnp.concatenate([jnp.ones((t, p), dtype=bool), jnp.tril(jnp.ones((t, t), dtype=bool))], axis=-1)
    a = diff_weights(s, mask, lam).astype(v_new.dtype)
    return (jnp.einsum('bhqk,bkhd->bqhd', a[..., :p], v_past)
            + jnp.einsum('bhqk,bkhd->bqhd', a[..., p:], v_new))


def ab_mixer(x, w_in, a_ln_g, a_ln_b, a_w_s, a_b_s, lam, lam_init, subln_g, w_out, k_past, v_past):
    bsz, t, _ = x.shape
    h = jnp.einsum('btd,de->bte', x, w_in)
    a_u, a_v, a_gate, q, k, v, b_gate = jnp.split(
        h, [MIX_A, 2 * MIX_A, 3 * MIX_A, 3 * MIX_A + QK_W, 3 * MIX_A + 2 * QK_W, 3 * MIX_A + 2 * QK_W + MIX_B], axis=-1)
    a_v = layer_norm(a_v, a_ln_g, a_ln_b)
    a_out = chunk_spatial_gate(a_u, a_v, a_w_s, a_b_s) * jax.nn.silu(a_gate)
    q = q.reshape(bsz, t, B_HEADS, 2, B_QK_DIM) * ATTN_SCALE
    k_rows = k.reshape(bsz, t, B_HEADS, B_K_ROW)
    v_rows = v.reshape(bsz, t, B_HEADS, B_V_DIM)
    k_new = k_rows.reshape(bsz, t, B_HEADS, 2, B_QK_DIM)
    if k_past is None:
        o = prompt_diff_attention(q, k_new, v_rows, lam)
    else:
        o = sample_diff_attention(q, k_new, v_rows, k_past, v_past, lam)
    o = rms_norm(o, subln_g) * (1.0 - lam_init)
    b_out = o.reshape(bsz, t, MIX_B) * jax.nn.silu(b_gate)
    out = jnp.einsum('bte,ed->btd', jnp.concatenate([a_out, b_out], axis=-1), w_out)
    return out, k_rows, v_rows, a_v


def pool_mixer(x, hist, start_pos, w_in, w_grp, b_grp, scale, w_out):
    bsz, t, _ = x.shape
    h = jnp.einsum('btd,de->bte', x, w_in)
    hp, gate = jnp.split(h, 2, axis=-1)
    ext = jnp.concatenate([hist.astype(hp.dtype), hp], axis=1)
    cs = jnp.concatenate([jnp.zeros((bsz, 1, C_WIDTH), jnp.float32),
                          jnp.cumsum(ext.astype(jnp.float32), axis=1)], axis=1)
    pos = start_pos + jnp.arange(t)
    means = []
    for g, w in enumerate(C_WINDOWS):
        sl = slice(g * C_GROUP_DIM, (g + 1) * C_GROUP_DIM)
        win_sum = cs[:, C_HIST + 1:C_HIST + 1 + t, sl] - cs[:, C_HIST + 1 - w:C_HIST + 1 - w + t, sl]
        count = jnp.minimum(pos + 1, w).astype(jnp.float32)
        means.append(win_sum / count[None, :, None])
    pooled = (jnp.concatenate(means, axis=-1) - hp.astype(jnp.float32)).astype(hp.dtype)
    mixed = jnp.einsum('btgc,gce->btge', pooled.reshape(bsz, t, C_GROUPS, C_GROUP_DIM), w_grp)
    mixed = mixed.reshape(bsz, t, C_WIDTH) + b_grp
    out = jnp.einsum('btc,cd->btd', mixed * scale * jax.nn.silu(gate), w_out)
    return out, ext[:, t:]


def setup_inputs(seed: int = 0) -> dict:
    key = jax.random.key(seed)
    ks = jax.random.split(key, 32)
    n_pages = PAST_LEN // PAGE_SIZE
    n_pool = (DEC_BATCH * n_pages * 5) // 4

    def nrm(k, shape, s):
        return jax.random.normal(k, shape, jnp.float32) * s

    x_prompt = nrm(ks[0], (BATCH, SEQ, D_MODEL), 1.0)
    x_sample = nrm(ks[1], (DEC_BATCH, DEC_SEQ, D_MODEL), 1.0)
    cache_k = nrm(ks[2], (N_AB, n_pool, PAGE_SIZE, B_HEADS, B_K_ROW), 1.0)
    cache_v = nrm(ks[3], (N_AB, n_pool, PAGE_SIZE, B_HEADS, B_V_DIM), 1.0)
    state_pool = nrm(ks[4], (N_C, DEC_BATCH, C_HIST, C_WIDTH), 1.0)
    page_table = jax.random.permutation(ks[5], n_pool)[:DEC_BATCH * n_pages].reshape(DEC_BATCH, n_pages).astype(jnp.int32)
    ln_g = 1.0 + nrm(ks[6], (DEPTH, D_MODEL), 0.02)
    ln_b = nrm(ks[7], (DEPTH, D_MODEL), 0.02)
    w_in_ab = nrm(ks[8], (N_AB, D_MODEL, AB_IN), D_MODEL ** -0.5)
    a_ln_g = 1.0 + nrm(ks[9], (N_AB, MIX_A), 0.02)
    a_ln_b = nrm(ks[10], (N_AB, MIX_A), 0.02)
    a_w_s = nrm(ks[11], (N_AB, A_GROUPS, CHUNK, CHUNK), CHUNK ** -0.5)
    a_b_s = 1.0 + nrm(ks[12], (N_AB, A_GROUPS, CHUNK), 0.02)
    b_lq1 = nrm(ks[13], (N_AB, B_QK_DIM), 0.1)
    b_lk1 = nrm(ks[14], (N_AB, B_QK_DIM), 0.1)
    b_lq2 = nrm(ks[15], (N_AB, B_QK_DIM), 0.1)
    b_lk2 = nrm(ks[16], (N_AB, B_QK_DIM), 0.1)
    b_subln_g = 1.0 + nrm(ks[17], (N_AB, B_V_DIM), 0.02)
    w_out_ab = nrm(ks[18], (N_AB, MIX_A + MIX_B, D_MODEL), BETA * (MIX_A + MIX_B) ** -0.5)
    w_in_c = nrm(ks[19], (N_C, D_MODEL, C_IN), D_MODEL ** -0.5)
    c_w_grp = nrm(ks[20], (N_C, C_GROUPS, C_GROUP_DIM, C_GROUP_DIM), C_GROUP_DIM ** -0.5)
    c_b_grp = nrm(ks[21], (N_C, C_WIDTH), 0.02)
    c_scale = 1.0 + nrm(ks[22], (N_C, C_WIDTH), 0.02)
    w_out_c = nrm(ks[23], (N_C, C_WIDTH, D_MODEL), BETA * C_WIDTH ** -0.5)
    return {"x_prompt": x_prompt, "x_sample": x_sample, "cache_k": cache_k, "cache_v": cache_v,
            "state_pool": state_pool, "page_table": page_table, "ln_g": ln_g, "ln_b": ln_b,
            "w_in_ab": w_in_ab, "a_ln_g": a_ln_g, "a_ln_b": a_ln_b, "a_w_s": a_w_s, "a_b_s": a_b_s,
            "b_lq1": b_lq1, "b_lk1": b_lk1, "b_lq2": b_lq2, "b_lk2": b_lk2, "b_subln_g": b_subln_g,
            "w_out_ab": w_out_ab, "w_in_c": w_in_c, "c_w_grp": c_w_grp, "c_b_grp": c_b_grp,
            "c_scale": c_scale, "w_out_c": w_out_c}


def reference(x_prompt, x_sample, cache_k, cache_v, state_pool, page_table, ln_g, ln_b,
              w_in_ab, a_ln_g, a_ln_b, a_w_s, a_b_s, b_lq1, b_lk1, b_lq2, b_lk2, b_subln_g,
              w_out_ab, w_in_c, c_w_grp, c_b_grp, c_scale, w_out_c):
    n_pages = page_table.shape[1]
    past_len = n_pages * cache_k.shape[2]
    dec_b = x_sample.shape[0]
    yp, ys = x_prompt, x_sample
    k_p_list, v_p_list, k_s_list, v_s_list, cv_s_list = [], [], [], [], []
    pool_p_list, pool_s_list = [], []
    for layer in range(DEPTH):
        i = layer // 2
        if layer % 2 == 0:
            lam_init = 0.8 - 0.6 * math.exp(-0.3 * layer)
            lam = (jnp.exp(jnp.sum(b_lq1[i].astype(jnp.float32) * b_lk1[i].astype(jnp.float32)))
                   - jnp.exp(jnp.sum(b_lq2[i].astype(jnp.float32) * b_lk2[i].astype(jnp.float32)))
                   + lam_init)
            out_p, k_p, v_p, _ = ab_mixer(yp, w_in_ab[i], a_ln_g[i], a_ln_b[i], a_w_s[i], a_b_s[i],
                                          lam, lam_init, b_subln_g[i], w_out_ab[i], None, None)
            k_past = cache_k[i, page_table].reshape(dec_b, past_len, B_HEADS, B_K_ROW)
            v_past = cache_v[i, page_table].reshape(dec_b, past_len, B_HEADS, B_V_DIM)
            out_s, k_s, v_s, av_s = ab_mixer(ys, w_in_ab[i], a_ln_g[i], a_ln_b[i], a_w_s[i], a_b_s[i],
                                             lam, lam_init, b_subln_g[i], w_out_ab[i], k_past, v_past)
            k_p_list.append(k_p)
            v_p_list.append(v_p)
            k_s_list.append(k_s)
            v_s_list.append(v_s)
            cv_s_list.append(av_s)
        else:
            hist_p = jnp.zeros((yp.shape[0], C_HIST, C_WIDTH), yp.dtype)
            out_p, hp_new = pool_mixer(yp, hist_p, 0, w_in_c[i], c_w_grp[i], c_b_grp[i], c_scale[i], w_out_c[i])
            out_s, hs_new = pool_mixer(ys, state_pool[i], past_len, w_in_c[i], c_w_grp[i], c_b_grp[i],
                                       c_scale[i], w_out_c[i])
            pool_p_list.append(hp_new)
            pool_s_list.append(hs_new)
        yp = layer_norm(ALPHA * yp + out_p, ln_g[layer], ln_b[layer])
        ys = layer_norm(ALPHA * ys + out_s, ln_g[layer], ln_b[layer])
    k_prompt_new = jnp.stack(k_p_list)
    v_prompt_new = jnp.stack(v_p_list)
    k_sample_new = jnp.stack(k_s_list)
    v_sample_new = jnp.stack(v_s_list)
    chunk_v_sample = jnp.stack(cv_s_list)
    pool_prompt_new = jnp.stack(pool_p_list)
    pool_sample_new = jnp.stack(pool_s_list)
    return (yp, ys, k_prompt_new, v_prompt_new, k_sample_new, v_sample_new, chunk_v_sample,
            pool_prompt_new, pool_sample_new)
```

```python
import math
import numpy as np
import ml_dtypes
import concourse.bass as bass
import concourse.mybir as mybir
from concourse.bass_utils import run_bass_kernel_spmd

F32 = mybir.dt.float32
BF16 = mybir.dt.bfloat16
I32 = mybir.dt.int32
ALU = mybir.AluOpType
AF = mybir.ActivationFunctionType
AX = mybir.AxisListType

NCORES = 8
D = 2048
TO = 1024
NS = 16
NT = TO + NS
NH = 120
NX = NT + NH
ALPHA = 4 ** 0.25
LN_EPS = 1e-5
LAM_INIT = 0.8 - 0.6 * math.exp(0.0)
NEG = -1e30
CHUNKS_P = [(0, 512), (512, 512)]
CHUNKS_PS = [(0, 512), (512, 512), (1024, 16)]
TILES = [(i * 128, 128) for i in range(8)] + [(1024, 16)]


class Buf:
    def __init__(self, t, multi=False):
        self.t = t
        self.multi = multi
        self.lw = []
        self.rd = []

    def __getitem__(self, k):
        return self.t[k]


class Sched:
    ENG = ["pe", "act", "dve", "pool", "sp"]

    def __init__(self, nc, esem, dsem):
        self.nc = nc
        self.stream = {e: [] for e in self.ENG}
        self.tick = {e: 0 for e in self.ENG}
        self.seen = {e: {} for e in self.ENG}
        self.esem = esem
        self.dsem = dsem
        self.dcount = {q: 0 for q in dsem}
        self.dval = {}
        self.dlast = {}
        self.ccn = 0
        self.enabled = True

    def _need(self, eng, dep):
        if dep is None:
            return
        if dep[0] == "eng":
            _, name, tick = dep
            if name == eng and name == "pe":
                return
            key = ("eng", name)
            if self.seen[eng].get(key, 0) >= tick:
                return
            self.seen[eng][key] = tick
            self.stream[eng].append(("wait", self.esem[name], tick))
        else:
            _, key, val = dep
            if self.seen[eng].get(key, 0) >= val:
                return
            self.seen[eng][key] = val
            q, k = key
            self.stream[eng].append(("wait", self.dsem[q][k], val))

    def _deps(self, eng, reads, writes):
        best = {}
        for b in reads:
            for d in b.lw:
                k = d[:2]
                if k not in best or best[k][2] < d[2]:
                    best[k] = d
        for b in writes:
            for d in b.rd:
                k = d[:2]
                if k not in best or best[k][2] < d[2]:
                    best[k] = d
            if not b.multi:
                for d in b.lw:
                    k = d[:2]
                    if k not in best or best[k][2] < d[2]:
                        best[k] = d
        for d in best.values():
            self._need(eng, d)

    def _mark(self, dep, reads, writes):
        for b in reads:
            b.rd.append(dep)
        for b in writes:
            if b.multi:
                b.lw.append(dep)
            else:
                b.lw = [dep]
                b.rd = []

    def op(self, eng, fn, reads=(), writes=()):
        if not self.enabled:
            return None
        self._deps(eng, reads, writes)
        self.tick[eng] += 1
        dep = ("eng", eng, self.tick[eng])
        self.stream[eng].append(("ins", fn, self.esem[eng], 1))
        self._mark(dep, reads, writes)
        return dep

    def dma(self, q, fn, reads=(), writes=()):
        if not self.enabled:
            return None
        self._deps(q, reads, writes)
        k = self.dcount[q] % len(self.dsem[q])
        self.dcount[q] += 1
        key = (q, k)
        self._need(q, self.dlast.get(key))
        val = self.dval.get(key, 0) + 16
        self.dval[key] = val
        dep = ("dma", key, val)
        self.dlast[key] = dep
        self.stream[q].append(("ins", fn, self.dsem[q][k], 16))
        self._mark(dep, reads, writes)
        return dep

    def barrier(self, final=False):
        if not self.enabled:
            return
        deps = [("eng", e, self.tick[e]) for e in ("pe", "act", "dve", "pool") if self.tick[e] > 0]
        deps += [d for k, d in self.dlast.items() if final or k[0] != "cc"]
        for e in self.ENG:
            for d in deps:
                if d[0] == "eng" and d[1] == e:
                    continue
                self._need(e, d)

    def replay(self, eng, e):
        for item in self.stream[eng]:
            if item[0] == "wait":
                e.wait_ge(item[1], item[2])
            else:
                ins = item[1](e)
                ins.then_inc(item[2], item[3])

    def mm(self, mms, reads, writes):
        mms = list(mms)

        def fn(e):
            ins = None
            for (o, l, r, st, sp) in mms:
                ins = e.matmul(o, l, r, start=st, stop=sp)
            return ins
        return self.op("pe", fn, reads, writes)

    def tr(self, trs, reads, writes):
        trs = list(trs)

        def fn(e):
            ins = None
            for (o, i, idn) in trs:
                ins = e.transpose(o, i, idn)
            return ins
        return self.op("pe", fn, reads, writes)

    def act(self, out, in_, func, reads, writes, scale=None, bias=None, accum=None):
        kw = {}
        if scale is not None:
            kw["scale"] = scale
        if bias is not None:
            kw["bias"] = bias
        if accum is not None:
            kw["accum_out"] = accum
        return self.op("act", lambda e: e.activation(out, in_, func, **kw), reads, writes)

    def copy(self, eng, out, in_, reads, writes):
        if eng == "act":
            return self.op("act", lambda e: e.copy(out, in_), reads, writes)
        return self.op(eng, lambda e: e.tensor_copy(out, in_), reads, writes)

    def tt(self, eng, out, a, b, op, reads, writes):
        return self.op(eng, lambda e: e.tensor_tensor(out, a, b, op), reads, writes)

    def ts(self, eng, out, a, s1, s2, op0, op1, reads, writes):
        if op1 is None:
            return self.op(eng, lambda e: e.tensor_scalar(out, a, s1, None, op0), reads, writes)
        return self.op(eng, lambda e: e.tensor_scalar(out, a, s1, s2, op0, op1), reads, writes)

    def stt(self, eng, out, a, s, b, op0, op1, reads, writes):
        return self.op(eng, lambda e: e.scalar_tensor_tensor(out, a, s, b, op0, op1), reads, writes)

    def red(self, eng, out, in_, op, reads, writes, axis=AX.X):
        return self.op(eng, lambda e: e.tensor_reduce(out, in_, axis, op), reads, writes)

    def memset(self, eng, ap, val, writes):
        return self.op(eng, lambda e: e.memset(ap, val), (), writes)

    def load(self, q, out, in_, reads, writes):
        return self.dma(q, lambda e: e.dma_start(out=out, in_=in_), reads, writes)


STOP = 99
NPOOL = 2560


def build_program():
    nc = bass.Bass("TRN2", target_bir_lowering=False)

    def din(name, shape, dt=F32):
        return nc.dram_tensor(name, list(shape), dt, kind="ExternalInput").ap()

    def dout(name, shape, dt=F32):
        return nc.dram_tensor(name, list(shape), dt, kind="ExternalOutput").ap()

    def dint(name, shape, dt):
        return nc.dram_tensor(name, list(shape), dt, kind="Internal").ap()

    xT = din("xT", [D, NT])
    xtok = din("xtok", [NT, D])
    cache_k = din("cache_k", [NPOOL * 128, 1024])
    cache_v = din("cache_v", [NPOOL * 128, 1024])
    ptab = din("ptab", [1, 256], I32)
    iota_i = din("iota_i", [128, 256], I32)
    hist = din("hist", [NS, 15, D])
    histT = din("histT", [D, NS, 15])
    w_in = din("w_in", [D, 7168])
    w_out = din("w_out", [D, D])
    w_in_c = din("w_in_c", [D, 4096])
    w_grp = din("w_grp", [D, 512])
    w_out_c = din("w_out_c", [D, D])
    a_ln_g = din("a_ln_g", [1, 1024])
    a_ln_b = din("a_ln_b", [1, 1024])
    wsT = din("wsT", [128, 4, 128])
    bs = din("bs", [1, 512])
    w00e = din("w00e", [1, 1024])
    b0e = din("b0e", [1, 1024])
    lqk = din("lqk", [1, 256])
    subln = din("subln", [1, 128])
    ln_g = din("ln_g", [2, D])
    ln_b = din("ln_b", [2, D])
    c_bT = din("c_bT", [128, 16])
    c_sT = din("c_sT", [128, 16])
    ident_d = din("ident", [128, 128])
    tril_d = din("tril", [128, 128])
    amask_d = din("amask", [128, 4, 128])
    rc_d = din("rc", [1, 64])
    sel_d = din("sel", [1, 8])
    pat_d = din("pat", [16, 16 * 8 * 16])
    patoe_d = din("patoe", [16, 2])

    y_o = dout("y", [NT, D])
    knew_o = dout("knew", [NT, 1024])
    vnew_o = dout("vnew", [NT, 1024])
    cv_o = dout("cv", [NS, 1024])
    poolp_o = dout("poolp", [15, D])
    pools_o = dout("pools", [NS, 15, D])

    kv_src = dint("kv_src", [2048, 1024], BF16)
    kv_dst = [dint(f"kv_dst{i}", [4 * 128, 1024], BF16) for i in range(16)]
    x1s = dint("x1s", [NT, D], F32)
    halo_src = dint("halo_src", [D, NH], BF16)
    halo_dst = dint("halo_dst", [4 * D, NH], BF16)
    sqkv = [dint(f"sqkv{i}", [NS, 1024], F32) for i in range(3)]

    from contextlib import ExitStack
    es = ExitStack()
    with es:
        esem = {e: es.enter_context(nc.semaphore("es_" + e)) for e in ("pe", "act", "dve", "pool")}
        dsem = {"sp": [es.enter_context(nc.semaphore(f"dsp{i}")) for i in range(24)],
                "pool": [es.enter_context(nc.semaphore(f"dpl{i}")) for i in range(16)]}
        ccsem = [es.enter_context(nc.semaphore(f"cc{i}")) for i in range(17)]
        S = Sched(nc, esem, dsem)

        free_list = [[16512, 229344]]
        nalloc = [0]

        def sb(name, shape, dt, grp, multi=False):
            esz = 2 if dt == BF16 else 4
            nb = esz
            for d_ in shape[1:]:
                nb *= d_
            nb = (nb + 63) // 64 * 64
            for fr in free_list:
                if fr[1] - fr[0] >= nb:
                    off = fr[0]
                    fr[0] += nb
                    break
            else:
                raise RuntimeError(f"SBUF arena full allocating {name} {shape} ({nb} B); free={free_list}")
            nalloc[0] += 1
            b = Buf(nc.alloc_sbuf_tensor_at(f"{name}_{nalloc[0]}", list(shape), dt, offset=off), multi)
            b.rng = (off, off + nb)
            grp.append(b)
            return b

        def free_group(grp):
            for b in grp:
                free_list.append(list(b.rng))
            del grp[:]
            free_list.sort()
            i = 0
            while i + 1 < len(free_list):
                if free_list[i][1] >= free_list[i + 1][0]:
                    free_list[i][1] = max(free_list[i][1], free_list[i + 1][1])
                    del free_list[i + 1]
                else:
                    i += 1
            free_list[:] = [f_ for f_ in free_list if f_[1] > f_[0]]

        class Grp(list):
            def __enter__(self):
                return self

            def __exit__(self, *a):
                free_group(self)
                return False

            def close(self):
                free_group(self)

        def psb(name, stack, dt=F32, n=512):
            return Buf(stack.enter_context(nc.psum_tensor(name, [128, n], dt)))

        class DB(Buf):
            def __init__(self):
                Buf.__init__(self, None, True)
        d_kvsrc, d_kvdst, d_x1s, d_hsrc, d_hdst, d_sqkv = DB(), DB(), DB(), DB(), DB(), DB()

        def collective(idx, src, dst, bsrc, bdst):
            if not S.enabled:
                return
            S._deps("pool", [bsrc], [bdst])
            sem = ccsem[idx]

            def fn(e):
                return e.collective_compute("AllGather", ALU.bypass,
                                            replica_groups=[[0, 1, 2, 3], [4, 5, 6, 7]],
                                            ins=[src], outs=[dst])
            S.stream["pool"].append(("ins", fn, sem, 1))
            S.dsem.setdefault("cc", ccsem)
            dep = ("dma", ("cc", idx), 1)
            S.dlast[("cc", idx)] = dep
            S._mark(dep, [bsrc], [bdst])

        pers = Grp()
        ident = sb("ident", [128, 128], F32, pers)
        identb = sb("identb", [128, 128], BF16, pers)
        lamc = sb("lamc", [128, 4], F32, pers)
        epsc = sb("epsc", [128, 1], F32, pers)
        S.memset("dve", epsc[:], LN_EPS, [epsc])
        S.load("sp", ident[:], ident_d, [], [ident])
        S.copy("dve", identb[:], ident[:], [ident], [identb])
        with Grp() as st0:
            lq = sb("lq", [128, 256], F32, st0)
            lt = sb("lt", [128, 128], F32, st0)
            ls = sb("ls", [128, 2], F32, st0)
            le = sb("le", [128, 2], F32, st0)
            S.load("sp", lq[:], lqk.partition_broadcast(128), [], [lq])
            lqv = lq[:].rearrange("p (a b c) -> p a b c", a=2, b=2)
            S.tt("dve", lt[:].rearrange("p (a c) -> p a c", a=2), lqv[:, :, 0, :], lqv[:, :, 1, :], ALU.mult, [lq], [lt])
            S.red("dve", ls[:], lt[:].rearrange("p (a c) -> p a c", a=2), ALU.add, [lt], [ls])
            S.act(le[:], ls[:], AF.Exp, [ls], [le])
            S.tt("dve", lamc[:, 2:3], le[:, 0:1], le[:, 1:2], ALU.subtract, [le], [lamc])
            S.ts("dve", lamc[:, 0:1], lamc[:, 2:3], LAM_INIT, None, ALU.add, None, [lamc], [lamc])
            S.ts("dve", lamc[:, 1:2], lamc[:, 0:1], -1.0, None, ALU.mult, None, [lamc], [lamc])
            S.barrier()

        ps_stack = ExitStack()
        es.enter_context(ps_stack)
        PS = [psb(f"ps{i}", ps_stack) for i in range(6)]
        PB = psb("psB", ps_stack, BF16, 1024)
        PM = psb("psM", ps_stack)
        psi = [0]

        def nps():
            b = PS[psi[0] % 6]
            psi[0] += 1
            return b

        st_ab = Grp()
        a_outT = sb("a_outT", [128, 8, NT], BF16, st_ab, multi=True)

        st_p12 = Grp()
        QT = sb("QT", [128, 8, TO], BF16, st_p12, multi=True)
        sbg = sb("sbg", [128, 9, 1024], BF16, st_p12, multi=True)
        ktok_s = sb("ktok_s", [NS, 1024], F32, st_p12, multi=True)
        vtok_s = sb("vtok_s", [NS, 1024], F32, st_p12, multi=True)
        qtok_s = sb("qtok_s", [NS, 1024], F32, st_p12, multi=True)

        S.enabled = STOP >= 0.1
        with Grp() as st1:
            xTb = [sb(f"xTb{i}", [128, 4, NT], BF16, st1) for i in range(4)]
            xTv = xT.rearrange("(k p) t -> p k t", p=128)
            for i in range(4):
                S.load("pool", xTb[i][:], xTv[:, 4 * i:4 * i + 4, :], [], [xTb[i]])
            W = [sb(f"W{i}", [128, 16, 512], BF16, st1) for i in range(3)]
            wcnt = [0]
            w_inv = w_in.rearrange("(k p) c -> p k c", p=128)

            def loadw(blk):
                b = W[wcnt[0] % len(W)]
                wcnt[0] += 1
                S.load("pool", b[:], w_inv[:, :, blk * 512:(blk + 1) * 512], [], [b])
                return b

            def fm(wb, c0, chunk, ps):
                t0, n = chunk
                S.mm([(ps[:, 0:n], wb[:, kt, c0:c0 + 128], xTb[kt // 4][:, kt % 4, t0:t0 + n],
                       kt == 0, kt == 15) for kt in range(16)], [wb] + xTb, [ps])

            def tm(wb, tile, ps):
                t0, n = tile
                S.mm([(ps[0:n, :], xTb[kt // 4][:, kt % 4, t0:t0 + n], wb[:, kt, :],
                       kt == 0, kt == 15) for kt in range(16)], [wb] + xTb, [ps])

            stg = [sb(f"stg{i}", [128, 512], F32, st1) for i in range(3)]
            stgb = [sb(f"stgb{i}", [128, 1024], BF16, st1) for i in range(3)]
            sc = [0, 0]

            def nstg():
                b = stg[sc[0] % 3]
                sc[0] += 1
                return b

            def nstgb():
                b = stgb[sc[1] % 3]
                sc[1] += 1
                return b

            kvs_k = kv_src[0:1024, :].rearrange("(h p) t -> p h t", p=128)
            seqA = [8, 9, 10, 11, 6, 7, 12, 13]
            wbufA = {}

            def getw(pos):
                for p_ in range(pos + 3):
                    if p_ < len(seqA) and p_ not in wbufA:
                        wbufA[p_] = loadw(seqA[p_])
                return wbufA[pos]
            getw(0)
            for bi in range(2):
                wb = getw(bi)
                for hl in range(4):
                    h = 4 * bi + hl
                    kb = nstgb()
                    for ci, ch in enumerate(CHUNKS_P):
                        ps = nps()
                        fm(wb, hl * 128, ch, ps)
                        S.copy("act", kb[:, ch[0]:ch[0] + 512], ps[:, 0:512], [ps], [kb])
                    S.load("sp", kvs_k[:, h, :], kb[:, :], [kb], [d_kvsrc])
                for ti, tile in enumerate(TILES):
                    ps = nps()
                    tm(wb, tile, ps)
                    n = tile[1]
                    if ti < 8:
                        sg_ = nstg()
                        S.copy("dve", sg_[0:n, :], ps[0:n, :], [ps], [sg_])
                        S.load("sp", knew_o[tile[0]:tile[0] + n, bi * 512:(bi + 1) * 512], sg_[0:n, :], [sg_], [])
                    else:
                        S.copy("dve", ktok_s[:, bi * 512:(bi + 1) * 512], ps[0:n, :], [ps], [ktok_s])
            S.enabled = STOP >= 0.3
            for bi in range(2):
                wb = getw(2 + bi)
                for ti, tile in enumerate(TILES):
                    ps = nps()
                    tm(wb, tile, ps)
                    n = tile[1]
                    if ti < 8:
                        sg_ = nstg()
                        S.copy("dve", sg_[0:n, :], ps[0:n, :], [ps], [sg_])
                        S.load("sp", vnew_o[tile[0]:tile[0] + n, bi * 512:(bi + 1) * 512], sg_[0:n, :], [sg_], [])
                        vb = nstgb()
                        S.copy("act", vb[:, 0:512], sg_[:, :], [sg_], [vb])
                        S.load("sp", kv_src[1024 + tile[0]:1024 + tile[0] + 128, bi * 512:(bi + 1) * 512],
                               vb[:, 0:512], [vb], [d_kvsrc])
                    else:
                        S.copy("dve", vtok_s[:, bi * 512:(bi + 1) * 512], ps[0:n, :], [ps], [vtok_s])
            S.load("sp", knew_o[TO:NT, :], ktok_s[:, :], [ktok_s], [])
            S.load("sp", vnew_o[TO:NT, :], vtok_s[:, :], [vtok_s], [])
            S.load("sp", sqkv[1], ktok_s[:, :], [ktok_s], [d_sqkv])
            S.load("sp", sqkv[2], vtok_s[:, :], [vtok_s], [d_sqkv])
            S.enabled = STOP >= 0.4
            for ci_ in range(16):
                collective(1 + ci_, kv_src[ci_ * 128:(ci_ + 1) * 128, :], kv_dst[ci_], d_kvsrc, d_kvdst)

            S.enabled = STOP >= 0.5
            for bi in range(2):
                wb = getw(4 + bi)
                for hl in range(4):
                    h = 4 * bi + hl
                    for ch in CHUNKS_P:
                        ps = nps()
                        fm(wb, hl * 128, ch, ps)
                        S.act(QT[:, h, ch[0]:ch[0] + 512], ps[:, 0:512], AF.Copy, [ps], [QT], scale=0.125)
                ps = nps()
                tm(wb, TILES[8], ps)
                S.act(qtok_s[:, bi * 512:(bi + 1) * 512], ps[0:NS, :], AF.Copy, [ps], [qtok_s], scale=0.125)
            S.load("sp", sqkv[0], qtok_s[:, :], [qtok_s], [d_sqkv])

            for bi in range(2):
                wb = getw(6 + bi)
                for ti, tile in enumerate(TILES):
                    ps = nps()
                    tm(wb, tile, ps)
                    n = tile[1]
                    S.act(sbg[0:n, ti, bi * 512:(bi + 1) * 512], ps[0:n, :], AF.Silu, [ps], [sbg])

            S.enabled = STOP >= 0.7
            S.barrier()
            free_group([W.pop(2)] + stg + stgb)
            vn = sb("vn", [128, 8, 1024], BF16, st1, multi=True)
            cvs = sb("cvs", [NS, 1024], F32, st1)
            algb = sb("algb", [128, 2, 1024], F32, st1)
            S.load("sp", algb[:, 0, :], a_ln_g.partition_broadcast(128), [], [algb])
            S.load("sp", algb[:, 1, :], a_ln_b.partition_broadcast(128), [], [algb])
            wa = [loadw(2), loadw(3)]
            lnst = [sb(f"lnst{i}", [128, 16], F32, st1) for i in range(2)]
            lnmv = [sb(f"lnmv{i}", [128, 4], F32, st1) for i in range(2)]
            lnt = [sb(f"lnt{i}", [128, 1024], F32, st1) for i in range(2)]
            for ti, tile in enumerate(TILES):
                n = tile[1]
                pA, pB = nps(), nps()
                tm(wa[0], tile, pA)
                tm(wa[1], tile, pB)
                stt_, mv, t32 = lnst[ti % 2], lnmv[ti % 2], lnt[ti % 2]
                S.op("dve", lambda e, o=stt_[0:n, 0:6], i=pA[0:n, :]: e.bn_stats(o, i), [pA], [stt_])
                S.op("dve", lambda e, o=stt_[0:n, 6:12], i=pB[0:n, :]: e.bn_stats(o, i), [pB], [stt_])
                S.op("dve", lambda e, o=mv[0:n, 0:2], i=stt_[0:n, 0:12]: e.bn_aggr(o, i), [stt_], [mv])
                S.act(mv[0:n, 2:3], mv[0:n, 1:2], AF.Sqrt, [mv], [mv], scale=1.0, bias=epsc[0:n, 0:1])
                S.op("dve", lambda e, o=mv[0:n, 2:3]: e.reciprocal(o, o), [mv], [mv])
                S.ts("dve", t32[0:n, 0:512], pA[0:n, :], mv[0:n, 0:1], mv[0:n, 2:3], ALU.subtract, ALU.mult, [pA, mv], [t32])
                S.ts("dve", t32[0:n, 512:1024], pB[0:n, :], mv[0:n, 0:1], mv[0:n, 2:3], ALU.subtract, ALU.mult, [pB, mv], [t32])
                S.tt("pool", t32[0:n, :], t32[0:n, :], algb[0:n, 0, :], ALU.mult, [t32, algb], [t32])
                if ti < 8:
                    S.tt("pool", vn[:, ti, :], t32[:, :], algb[:, 1, :], ALU.add, [t32, algb], [vn])
                else:
                    S.tt("pool", cvs[:, :], t32[0:n, :], algb[0:n, 1, :], ALU.add, [t32, algb], [cvs])
            S.load("sp", cv_o[:, :], cvs[:, :], [cvs], [])

            S.enabled = STOP >= 0.9
            wsm = sb("wsm", [128, 4, 128], F32, st1)
            wsb = sb("wsb", [128, 4, 128], BF16, st1)
            trl = sb("trl", [128, 128], F32, st1)
            bsb = sb("bsb", [128, 4, 4, 128], F32, st1)
            S.load("sp", wsm[:], wsT, [], [wsm])
            S.load("sp", trl[:], tril_d, [], [trl])
            for r in range(4):
                S.load("sp", bsb[:, :, r, :], bs.rearrange("o (g i) -> o g i", g=4).partition_broadcast(128), [], [bsb])
            for g in range(4):
                S.tt("dve", wsb[:, g, :], wsm[:, g, :], trl[:, :], ALU.mult, [wsm, trl], [wsb])
            w0b = sb("w0b", [NS, 2, 1024], F32, st1)
            S.load("sp", w0b[:, 0, :], w00e.partition_broadcast(NS), [], [w0b])
            S.load("sp", w0b[:, 1, :], b0e.partition_broadcast(NS), [], [w0b])
            ms = sb("ms", [NS, 1024], F32, st1)
            S.tt("dve", ms[:, :], cvs[:, :], w0b[:, 0, :], ALU.mult, [cvs, w0b], [ms])
            S.tt("dve", ms[:, :], ms[:, :], w0b[:, 1, :], ALU.add, [ms, w0b], [ms])
            aos = sb("aos", [NS, 1024], BF16, st1, multi=True)
            sgt = [sb(f"sgt{i}", [128, 512], BF16, st1) for i in range(2)]
            t1 = [sb(f"t1_{i}", [128, 512], F32, st1) for i in range(2)]
            cnt = 0
            for bi in range(2):
                wu = loadw(bi)
                wg = loadw(4 + bi)
                for fl in range(4):
                    ft = 4 * bi + fl
                    g = ft // 2
                    for ci, ch in enumerate(CHUNKS_P):
                        pU, pG, pM = nps(), nps(), nps()
                        fm(wu, fl * 128, ch, pU)
                        fm(wg, fl * 128, ch, pG)
                        S.mm([(pM[:, i * 128:(i + 1) * 128], vn[:, 4 * ci + i, ft * 128:(ft + 1) * 128],
                               wsb[:, g, :], True, True) for i in range(4)], [vn, wsb], [pM])
                        sg_, tt_ = sgt[cnt % 2], t1[cnt % 2]
                        cnt += 1
                        S.act(sg_[:, :], pG[:, :], AF.Silu, [pG], [sg_])
                        S.tt("dve", tt_[:, :], pM[:, :], bsb[:, g, :, :].rearrange("p r i -> p (r i)"), ALU.add, [pM, bsb], [tt_])
                        S.tt("dve", tt_[:, :], tt_[:, :], pU[:, :], ALU.mult, [tt_, pU], [tt_])
                        S.tt("pool", a_outT[:, ft, ch[0]:ch[0] + 512], tt_[:, :], sg_[:, :], ALU.mult, [tt_, sg_], [a_outT])
                pU, pG = nps(), nps()
                tm(wu, TILES[8], pU)
                tm(wg, TILES[8], pG)
                sg_, tt_ = sgt[cnt % 2], t1[cnt % 2]
                cnt += 1
                S.act(sg_[0:NS, :], pG[0:NS, :], AF.Silu, [pG], [sg_])
                S.tt("dve", tt_[0:NS, :], pU[0:NS, :], ms[:, bi * 512:(bi + 1) * 512], ALU.mult, [pU, ms], [tt_])
                S.tt("dve", aos[:, bi * 512:(bi + 1) * 512], tt_[0:NS, :], sg_[0:NS, :], ALU.mult, [tt_, sg_], [aos])
            pT = PB
            S.tr([(pT[:, ft * NS:(ft + 1) * NS], aos[:, ft * 128:(ft + 1) * 128], identb[0:NS, 0:NS]) for ft in range(8)],
                 [aos, identb], [pT])
            S.copy("dve", a_outT[:, :, TO:NT], pT[:, 0:8 * NS].rearrange("p (f s) -> p f s", f=8), [pT], [a_outT])
            S.barrier()

        S.enabled = STOP >= 2
        o_samp = sb("o_samp", [NS, 1024], F32, st_p12)
        with Grp() as st2:
            ptb = sb("ptb", [128, 256], I32, st2)
            iob = sb("iob", [128, 256], I32, st2)
            idx = sb("idx", [128, 256], I32, st2)
            S.load("sp", ptb[:], ptab.partition_broadcast(128), [], [ptb])
            S.load("sp", iob[:], iota_i, [], [iob])
            S.op("dve", lambda e: e.tensor_single_scalar(idx[:], ptb[:], 7, ALU.logical_shift_left), [ptb], [idx])
            S.tt("dve", idx[:], idx[:], iob[:], ALU.bitwise_or, [idx, iob], [idx])
            pat = sb("pat", [16, 16, 8, 16], F32, st2)
            patoe = sb("patoe", [16, 2], F32, st2)
            lamv = sb("lamv", [16, 2], F32, st2)
            ones = sb("ones", [128, 1], F32, st2)
            S.load("sp", pat[:].rearrange("p a b c -> p (a b c)"), pat_d, [], [pat])
            S.load("sp", patoe[:], patoe_d, [], [patoe])
            S.memset("dve", ones[:], 1.0, [ones])
            S.ts("dve", lamv[:, 1:2], patoe[:, 1:2], lamc[0:16, 0:1], None, ALU.mult, None, [patoe, lamc], [lamv])
            S.tt("dve", lamv[:, 0:1], patoe[:, 0:1], lamv[:, 1:2], ALU.subtract, [patoe, lamv], [lamv])
            KB = [sb(f"KB{i}", [128, 1024], F32, st2) for i in range(3)]
            VB = [sb(f"VB{i}", [128, 1024], F32, st2) for i in range(3)]
            tmpb = [sb(f"tmpb{i}", [128, 1024], F32, st2) for i in range(2)]
            qbc = [sb(f"qbc{i}", [128, 1024], F32, st2) for i in range(2)]
            ksf = [sb(f"ksf{i}", [1, 1024], F32, st2) for i in range(2)]
            vsf = [sb(f"vsf{i}", [1, 1024], F32, st2) for i in range(2)]
            Sb = [sb(f"Sb{i}", [128, 16, 17], F32, st2) for i in range(2)]
            Eb = [sb(f"Eb{i}", [128, 16, 17], F32, st2) for i in range(2)]
            esum = [sb(f"esum{i}", [128, 16], F32, st2) for i in range(2)]
            rinv = [sb(f"rinv{i}", [16, 2], F32, st2) for i in range(2)]
            csel = [sb(f"csel{i}", [16, 8, 16], F32, st2) for i in range(2)]
            ofs = [sb(f"ofs{i}", [16, 1024], F32, st2) for i in range(2)]
            for i in range(2):
                S.memset("dve", Sb[i][:, :, 16:17], NEG, [Sb[i]])
            pO = [PS[0], PS[1]]
            pF = [[PS[2], PS[3]], [PS[4], PS[5]]]
            pL = [PM, PM]
            kc = 0
            vc = 0

            def k_page(s, j):
                nonlocal kc
                qb, Sx = qbc[s % 2], Sb[s % 2]
                kb = KB[kc % 3]
                tb = tmpb[kc % 2]
                kc += 1
                col = s * 16 + j
                S.dma("pool", lambda e, o=kb[:, :], c=col: e.indirect_dma_start(
                    out=o, out_offset=None, in_=cache_k,
                    in_offset=bass.IndirectOffsetOnAxis(ap=idx[:, c:c + 1], axis=0)), [idx], [kb])
                S.tt("dve", tb[:, :], kb[:, :], qb[:, :], ALU.mult, [kb, qb], [tb])
                S.red("dve", Sx[:, :, j], tb[:, :].rearrange("p (a d) -> p a d", d=64), ALU.add, [tb], [Sx])

            def k_finish(s):
                nonlocal kc
                qb, kf, Sx, Ex = qbc[s % 2], ksf[s % 2], Sb[s % 2], Eb[s % 2]
                tb = tmpb[kc % 2]
                kc += 1
                S.tt("dve", tb[0:1, :], kf[0:1, :], qb[0:1, :], ALU.mult, [kf, qb], [tb])
                S.red("dve", Sx[0:1, :, 16], tb[0:1, :].rearrange("p (a d) -> p a d", d=64), ALU.add, [tb], [Sx])
                S.act(Ex[:], Sx[:], AF.Exp, [Sx], [Ex])
                es_ = esum[s % 2]
                S.red("dve", es_[:, :], Ex[:], ALU.add, [Ex], [es_])
                pl = pL[s % 2]
                S.mm([(pl[0:16, 0:1], es_[:, :], ones[:, 0:1], True, True)], [es_, ones], [pl])
                ri = rinv[s % 2]
                S.op("dve", lambda e, o=ri[:, 0:1], i=pl[0:16, 0:1]: e.reciprocal(o, i), [pl], [ri])
                S.tt("dve", ri[:, 1:2], ri[:, 0:1], lamv[:, 0:1], ALU.mult, [ri, lamv], [ri])
                S.ts("dve", csel[s % 2][:], pat[:, s, :, :], ri[:, 1:2], None, ALU.mult, None, [pat, ri], [csel[s % 2]])

            def v_page(s, j):
                nonlocal vc
                Ex, pf = Eb[s % 2], pF[s % 2]
                vb = VB[vc % 3]
                vc += 1
                col = s * 16 + j
                S.dma("pool", lambda e, o=vb[:, :], c=col: e.indirect_dma_start(
                    out=o, out_offset=None, in_=cache_v,
                    in_offset=bass.IndirectOffsetOnAxis(ap=idx[:, c:c + 1], axis=0)), [idx], [vb])
                S.mm([(pf[hf][0:16, :], Ex[:, :, j], vb[:, hf * 512:(hf + 1) * 512], j == 0, False)
                      for hf in range(2)], [Ex, vb], pf)

            def v_finish(s):
                Ex, pf, vf = Eb[s % 2], pF[s % 2], vsf[s % 2]
                S.mm([(pf[hf][0:16, :], Ex[0:1, :, 16], vf[0:1, hf * 512:(hf + 1) * 512], False, True)
                      for hf in range(2)], [Ex, vf], pf)
                cs, of_ = csel[s % 2], ofs[s % 2]
                for hf in range(2):
                    S.copy("act", of_[:, hf * 512:(hf + 1) * 512], pf[hf][0:16, :], [pf[hf]], [of_])
                S.mm([(pO[h // 4][0:16, (h % 4) * 128:(h % 4 + 1) * 128], cs[:, h, :], of_[:, h * 128:(h + 1) * 128],
                       s == 0 and h % 4 == 0, s == NS - 1) for h in range(8)], [cs, of_], pO)

            for s in range(NS + 1):
                if s < NS:
                    S.load("sp", qbc[s % 2][:], sqkv[0][s:s + 1, :].partition_broadcast(128), [d_sqkv], [qbc[s % 2]])
                    S.load("sp", ksf[s % 2][:], sqkv[1][s:s + 1, :], [d_sqkv], [ksf[s % 2]])
                for j in range(16):
                    if s < NS:
                        k_page(s, j)
                    if s >= 1:
                        v_page(s - 1, j)
                if s >= 1:
                    v_finish(s - 1)
                if s < NS:
                    S.load("sp", vsf[s % 2][:], sqkv[2][s:s + 1, :], [d_sqkv], [vsf[s % 2]])
                    k_finish(s)
            for hf in range(2):
                S.copy("act", o_samp[:, hf * 512:(hf + 1) * 512], pO[hf][0:16, :], [pO[hf]], [o_samp])
            S.barrier()

        S.enabled = STOP >= 3
        b_outT = sb("b_outT", [128, 8, NT], BF16, st_ab, multi=True)
        with Grp() as st3:
            KTh = [sb(f"KTh{i}", [128, 4, 1024], BF16, st3) for i in range(2)]
            Vh = [sb(f"Vh{i}", [128, 32, 129], BF16, st3) for i in range(2)]
            for i in range(2):
                S.memset("dve", Vh[i][:, :, 128:129], 1.0, [Vh[i]])
            amk = sb("amk", [128, 4, 128], F32, st3)
            S.load("sp", amk[:], amask_d, [], [amk])
            sgb = sb("sgb", [128, 128], F32, st3)
            S.load("sp", sgb[:], subln.partition_broadcast(128), [], [sgb])
            S.ts("dve", sgb[:], sgb[:], 1.0 - LAM_INIT, None, ALU.mult, None, [sgb], [sgb])
            PT = [sb(f"PT{i}", [128, 512], BF16, st3) for i in range(3)]
            Osb2 = [sb(f"Osb{i}", [128, 2, 8, 129], F32, st3, multi=True) for i in range(2)]
            rcp2 = [sb(f"rcp{i}", [128, 2, 8], F32, st3) for i in range(2)]
            bo = sb("bo", [128, 9, 1024], BF16, st3, multi=True)
            od = [sb(f"od{i}", [128, 128], F32, st3) for i in range(2)]
            sq_ = [sb(f"sq{i}", [128, 128], F32, st3) for i in range(2)]
            ssb = [sb(f"ssb{i}", [128, 2], F32, st3) for i in range(2)]
            pST = [PS[0], PS[1], PS[2]]
            pOA = [PS[3], PS[4], PS[5]]
            stc = 0
            ptc = 0
            ec = 0

            def epilogue(src_ap_fn, n, ti, h, reads):
                nonlocal ec
                o_, q_, s_ = od[ec % 2], sq_[ec % 2], ssb[ec % 2]
                ec += 1
                src_ap_fn(o_)
                S.act(q_[0:n, :], o_[0:n, :], AF.Square, [o_], [q_, s_], accum=s_[0:n, 0:1])
                S.act(s_[0:n, 1:2], s_[0:n, 0:1], AF.Sqrt, [s_], [s_], scale=1.0 / 128.0, bias=epsc[0:n, 0:1])
                S.op("dve", lambda e, o=s_[0:n, 1:2]: e.reciprocal(o, o), [s_], [s_])
                S.stt("dve", o_[0:n, :], o_[0:n, :], s_[0:n, 1:2], sgb[0:n, :], ALU.mult, ALU.mult, [o_, s_, sgb], [o_])
                S.tt("dve", bo[0:n, ti, h * 128:(h + 1) * 128], o_[0:n, :], sbg[0:n, ti, h * 128:(h + 1) * 128],
                     ALU.mult, [o_, sbg], [bo])

            for h in range(8):
                kt_, vh_ = KTh[h % 2], Vh[h % 2]
                S.load("sp", kt_[:], kv_dst[h].rearrange("(r x) c -> x r c", r=4), [d_kvdst], [kt_])
                for m_ in range(8):
                    for r in range(4):
                        S.load("sp", vh_[:, r * 8 + m_, 0:128],
                               kv_dst[8 + m_][r * 128:(r + 1) * 128, h * 128:(h + 1) * 128], [d_kvdst], [vh_])
                Osb, rcp = Osb2[h % 2], rcp2[h % 2]
                for mp in range(2):
                    rows = slice(mp * 64, mp * 64 + 64)
                    steps = []
                    for mq in range(8):
                        for r in range(4):
                            c0 = mq * 128
                            while c0 < TO:
                                n = min(512, TO - c0)
                                steps.append((mq, r, c0, n))
                                c0 += n

                    def qk(st):
                        nonlocal stc
                        mq, r, c0, n = st
                        pst = pST[stc % 3]
                        stc += 1
                        S.mm([(pst[:, 0:n], kt_[rows, r, mq * 128:(mq + 1) * 128], QT[rows, h, c0:c0 + n], True, True)],
                             [kt_, QT], [pst])
                        return pst

                    def rest(st, pst):
                        nonlocal ptc
                        mq, r, c0, n = st
                        blk = r * 8 + mq
                        pt = PT[ptc % 3]
                        ptc += 1
                        S.act(pt[:, 0:n], pst[:, 0:n], AF.Exp, [pst], [pt])
                        if c0 == mq * 128:
                            S.tt("dve", pt[:, 0:128], pt[:, 0:128], amk[:, r, :], ALU.mult, [pt, amk], [pt])
                        mms = []
                        for i in range(n // 128):
                            m = (c0 // 128) + i
                            ob = pOA[m // 3]
                            oc = (m % 3) * 129
                            mms.append((ob[:, oc:oc + 129], pt[:, i * 128:(i + 1) * 128], vh_[:, blk, :],
                                        (mq == 0 and r == 0 and m % 3 == 0), (mq == m and r == 3)))
                        S.mm(mms, [pt, vh_], pOA)

                    pend = None
                    for st in steps:
                        pst = qk(st)
                        if pend is not None:
                            rest(*pend)
                        pend = (st, pst)
                    rest(*pend)
                    for bk in range(3):
                        nm = 3 if bk < 2 else 2
                        S.copy("act", Osb[:, mp, 3 * bk:3 * bk + nm, :].rearrange("p m c -> p (m c)"),
                               pOA[bk][:, 0:nm * 129], [pOA[bk]], [Osb])
                S.op("dve", lambda e, r_=rcp[:, :, :], o_=Osb[:, :, :, 128]: e.reciprocal(r_, o_), [Osb], [rcp])
                S.ts("dve", rcp[:, 1, :], rcp[:, 1, :], lamc[:, 1:2], None, ALU.mult, None, [rcp, lamc], [rcp])
                for m in range(8):
                    def src(o_, m=m):
                        S.ts("dve", o_[:, :], Osb[:, 0, m, 0:128], rcp[:, 0, m:m + 1], None, ALU.mult, None, [Osb, rcp], [o_])
                        S.stt("dve", o_[:, :], Osb[:, 1, m, 0:128], rcp[:, 1, m:m + 1], o_[:, :], ALU.mult, ALU.add,
                              [Osb, rcp, o_], [o_])
                    epilogue(src, 128, m, h, [])
            for h in range(8):
                def src(o_, h=h):
                    S.copy("dve", o_[0:NS, :], o_samp[:, h * 128:(h + 1) * 128], [o_samp], [o_])
                epilogue(src, NS, 8, h, [])
            pTb = [PB, PB]
            for m in range(8):
                p = pTb[m % 2]
                S.tr([(p[:, h * 128:(h + 1) * 128], bo[:, m, h * 128:(h + 1) * 128], identb[:, :]) for h in range(8)],
                     [bo, identb], [p])
                S.copy("dve" if m % 2 else "act", b_outT[:, :, m * 128:(m + 1) * 128],
                       p[:, :].rearrange("p (h t) -> p h t", h=8), [p], [b_outT])
            p = pTb[0]
            S.tr([(p[:, h * NS:(h + 1) * NS], bo[0:NS, 8, h * 128:(h + 1) * 128], identb[0:NS, 0:NS]) for h in range(8)],
                 [bo, identb], [p])
            S.copy("dve", b_outT[:, :, TO:NT], p[:, 0:8 * NS].rearrange("p (h s) -> p h s", h=8), [p], [b_outT])
            S.barrier()
        st_p12.close()

        S.enabled = STOP >= 4
        def out_proj_ln(stk, layer, lhs_fn, lhs_bufs, w_dram, xres_dram, xres_dep, sink):
            Wo = [sb(f"Wo{layer}_{i}", [128, 16, 512], BF16, stk) for i in range(4)]
            wv_ = w_dram.rearrange("(k p) c -> p k c", p=128)
            for cb in range(4):
                S.load("pool", Wo[cb][:], wv_[:, :, cb * 512:(cb + 1) * 512], [], [Wo[cb]])
            gb = sb(f"gb{layer}", [128, 2, D], F32, stk)
            S.load("sp", gb[:, 0, :], ln_g[layer:layer + 1, :].partition_broadcast(128), [], [gb])
            S.load("sp", gb[:, 1, :], ln_b[layer:layer + 1, :].partition_broadcast(128), [], [gb])
            xr = [sb(f"xr{layer}_{i}", [128, D], F32, stk) for i in range(2)]
            y0 = [sb(f"y0{layer}_{i}", [128, D], F32, stk) for i in range(2)]
            st_ = [sb(f"lst{layer}_{i}", [128, 24], F32, stk) for i in range(2)]
            mv_ = [sb(f"lmv{layer}_{i}", [128, 4], F32, stk) for i in range(2)]
            for ti, (t0, n) in enumerate(TILES):
                x_, y_, s_, m_ = xr[ti % 2], y0[ti % 2], st_[ti % 2], mv_[ti % 2]
                S.load("sp", x_[0:n, :], xres_dram[t0:t0 + n, :], xres_dep, [x_])
                for cb in range(4):
                    ps = nps()
                    S.mm([(ps[0:n, :], lhs_fn(kt, t0, n), Wo[cb][:, kt, :], kt == 0, kt == 15) for kt in range(16)],
                         lhs_bufs + [Wo[cb]], [ps])
                    S.stt("dve", y_[0:n, cb * 512:(cb + 1) * 512], x_[0:n, cb * 512:(cb + 1) * 512], ALPHA, ps[0:n, :],
                          ALU.mult, ALU.add, [x_, ps], [y_])
                    S.op("dve", lambda e, o=s_[0:n, cb * 6:cb * 6 + 6], i=y_[0:n, cb * 512:(cb + 1) * 512]: e.bn_stats(o, i),
                         [y_], [s_])
                S.op("dve", lambda e, o=m_[0:n, 0:2], i=s_[0:n, 0:24]: e.bn_aggr(o, i), [s_], [m_])
                S.act(m_[0:n, 2:3], m_[0:n, 1:2], AF.Sqrt, [m_], [m_], scale=1.0, bias=epsc[0:n, 0:1])
                S.op("dve", lambda e, o=m_[0:n, 2:3]: e.reciprocal(o, o), [m_], [m_])
                S.ts("dve", y_[0:n, :], y_[0:n, :], m_[0:n, 0:1], m_[0:n, 2:3], ALU.subtract, ALU.mult, [y_, m_], [y_])
                S.tt("pool", y_[0:n, :], y_[0:n, :], gb[0:n, 0, :], ALU.mult, [y_, gb], [y_])
                S.tt("pool", y_[0:n, :], y_[0:n, :], gb[0:n, 1, :], ALU.add, [y_, gb], [y_])
                sink(ti, t0, n, y_)

        st_x1 = Grp()
        x1T = sb("x1T", [128, 16, NX], BF16, st_x1, multi=True)

        with Grp() as st4:
            x1b = [sb(f"x1b{i}", [128, D], BF16, st4) for i in range(2)]
            pTc = [PB, PB]
            pcnt = [0]

            def lhs0(kt, t0, n):
                return a_outT[:, kt, t0:t0 + n] if kt < 8 else b_outT[:, kt - 8, t0:t0 + n]

            def sink0(ti, t0, n, y_):
                S.load("sp", x1s[t0:t0 + n, :], y_[0:n, :], [y_], [d_x1s])
                xb = x1b[ti % 2]
                S.copy("act", xb[0:n, :], y_[0:n, :], [y_], [xb])
                for half in range(2):
                    p = pTc[pcnt[0] % 2]
                    pcnt[0] += 1
                    S.tr([(p[:, k * 128:k * 128 + n], xb[0:n, (half * 8 + k) * 128:(half * 8 + k + 1) * 128], identb[0:n, 0:n])
                          for k in range(8)], [xb, identb], [p])
                    S.copy("dve" if half else "act", x1T[:, half * 8:half * 8 + 8, t0:t0 + n],
                           p[:, :].rearrange("p (k t) -> p k t", k=8)[:, :, 0:n], [p], [x1T])

            out_proj_ln(st4, 0, lhs0, [a_outT, b_outT], w_out, xtok, [], sink0)
            hsv = halo_src.rearrange("(k p) (m c) -> p k m c", p=128, m=8)
            for m in range(8):
                S.load("sp", hsv[:, :, m, :], x1T[:, :, m * 128 + 113:m * 128 + 128], [x1T], [d_hsrc])
            collective(0, halo_src, halo_dst, d_hsrc, d_hdst)
            S.barrier()
        st_ab.close()

        S.enabled = STOP >= 5
        with Grp() as st5:
            with Grp() as st5h:
                Hb = sb("Hb", [128, 16, 4, NH], BF16, st5h)
                hdv = halo_dst.rearrange("(r k p) c -> p k r c", r=4, p=128)
                for r in range(4):
                    S.load("sp", Hb[:, :, r, :], hdv[:, :, r, :], [d_hdst], [Hb])
                selb = sb("selb", [128, 8], F32, st5h)
                S.load("sp", selb[:], sel_d.partition_broadcast(128), [], [selb])
                hA = sb("hA", [128, 16, NH], F32, st5h)
                hB = sb("hB", [128, 16, NH], F32, st5h)
                for (acc, off) in ((hA, 0), (hB, 4)):
                    S.ts("dve", acc[:], Hb[:, :, 0, :], selb[:, off:off + 1], None, ALU.mult, None, [Hb, selb], [acc])
                    for r in range(1, 4):
                        S.stt("dve", acc[:], Hb[:, :, r, :], selb[:, off + r:off + r + 1], acc[:], ALU.mult, ALU.add,
                              [Hb, selb, acc], [acc])
                S.copy("dve", x1T[:, :, NT:NT + 15], hA[:, :, 0:15], [hA], [x1T])
                S.tt("dve", x1T[:, :, NT + 15:NX], hA[:, :, 15:NH], hB[:, :, 0:NH - 15], ALU.add, [hA, hB], [x1T])
                S.barrier()

            pooledT = sb("pooledT", [128, 16, NT], BF16, st5, multi=True)
            sgT = sb("sgT", [128, 16, NT], BF16, st5, multi=True)
            with Grp() as st5a:
                Wc = [sb(f"Wc{i}", [128, 16, 512], BF16, st5a) for i in range(3)]
                wcc = [0]
                wcv = w_in_c.rearrange("(k p) c -> p k c", p=128)

                def loadwc(blk):
                    b = Wc[wcc[0] % 3]
                    wcc[0] += 1
                    S.load("pool", b[:], wcv[:, :, blk * 512:(blk + 1) * 512], [], [b])
                    return b
                Wg = sb("Wg", [128, 16, 512], BF16, st5a)
                S.load("pool", Wg[:], w_grp.rearrange("(k p) c -> p k c", p=128), [], [Wg])
                cbs = sb("cbs", [128, 2, 16], F32, st5a)
                S.load("sp", cbs[:, 0, :], c_bT, [], [cbs])
                S.load("sp", cbs[:, 1, :], c_sT, [], [cbs])
                rcb = sb("rcb", [128, 4, 16], F32, st5a)
                S.load("sp", rcb[:].rearrange("p g c -> p (g c)"), rc_d.partition_broadcast(128), [], [rcb])
                hpx = [sb(f"hpx{i}", [128, 8, 143], F32, st5a) for i in range(2)]
                wnA = [sb(f"wnA{i}", [128, 8, 143], F32, st5a) for i in range(1)] * 2
                wnB = [sb(f"wnB{i}", [128, 8, 143], F32, st5a) for i in range(1)] * 2
                hps = [sb(f"hps{i}", [128, NS, 16], F32, st5a) for i in range(2)]
                wns = [sb(f"wns{i}", [128, NS], F32, st5a) for i in range(2)]
                fx = [sb(f"fx{i}", [128, 16], F32, st5a) for i in range(2)]
                stg5 = [sb(f"stg5_{i}", [128, 512], F32, st5a) for i in range(2)]
                s5 = [0]
                hTv = histT.rearrange("(k p) s c -> p k s c", p=128)

                def fm1(wb, c0, chunk, ps):
                    t0, n = chunk
                    S.mm([(ps[:, 0:n], wb[:, kt, c0:c0 + 128], x1T[:, kt, t0:t0 + n], kt == 0, kt == 15)
                          for kt in range(16)], [wb, x1T], [ps])

                def tm1(wb, tile, ps):
                    t0, n = tile
                    S.mm([(ps[0:n, :], x1T[:, kt, t0:t0 + n], wb[:, kt, :], kt == 0, kt == 15)
                          for kt in range(16)], [wb, x1T], [ps])

                S.load("sp", pools_o[:, 0:14, :], hist[:, 1:15, :], [], [])
                wbufC = {}

                def getwc(pos):
                    for p_ in range(pos + 3):
                        if p_ < 8 and p_ not in wbufC:
                            wbufC[p_] = loadwc(p_)
                    return wbufC[pos]
                for b in range(4):
                    wb = getwc(b)
                    w = 2 ** (b + 1)
                    ps = nps()
                    tm1(wb, TILES[7], ps)
                    sg_ = stg5[s5[0] % 2]
                    s5[0] += 1
                    S.copy("act", sg_[:, :], ps[:, :], [ps], [sg_])
                    S.load("sp", poolp_o[:, b * 512:(b + 1) * 512], sg_[113:128, :], [sg_], [])
                    ps = nps()
                    tm1(wb, TILES[8], ps)
                    sg_ = stg5[s5[0] % 2]
                    s5[0] += 1
                    S.copy("act", sg_[0:NS, :], ps[0:NS, :], [ps], [sg_])
                    S.load("sp", pools_o[:, 14, b * 512:(b + 1) * 512], sg_[0:NS, :], [sg_], [])
                    for cl in range(4):
                        ct = 4 * b + cl
                        hx, wA, wB, hs, ws_, fx_ = hpx[ct % 2], wnA[ct % 2], wnB[ct % 2], hps[ct % 2], wns[ct % 2], fx[ct % 2]
                        for ci, ch in enumerate(CHUNKS_P):
                            ps = nps()
                            fm1(wb, cl * 128, ch, ps)
                            S.copy("act", hx[:, 4 * ci:4 * ci + 4, 15:143], ps[:, :].rearrange("p (m t) -> p m t", m=4), [ps], [hx])
                        ps = nps()
                        fm1(wb, cl * 128, (NT, NH), ps)
                        S.copy("act", hx[:, :, 0:15], ps[:, 0:NH].rearrange("p (m c) -> p m c", m=8), [ps], [hx])
                        ps = nps()
                        fm1(wb, cl * 128, (TO, NS), ps)
                        S.load("sp", hs[:, :, 0:15], hTv[:, ct, :, :], [], [hs])
                        S.copy("act", hs[:, :, 15], ps[:, 0:NS], [ps], [hs])
                        cur, lo, step = hx, 0, 1
                        bufs = [wA, wB]
                        bi_ = 0
                        while step < w:
                            nxt = bufs[bi_ % 2]
                            bi_ += 1
                            nlo = lo + step
                            S.tt("dve", nxt[:, :, nlo:143], cur[:, :, nlo:143], cur[:, :, nlo - step:143 - step], ALU.add,
                                 [cur], [nxt])
                            cur, lo, step = nxt, nlo, step * 2
                        S.stt("dve", pooledT[:, ct, 0:TO].rearrange("p (m t) -> p m t", m=8), cur[:, :, 15:143], 1.0 / w,
                              hx[:, :, 15:143], ALU.mult, ALU.subtract, [cur, hx], [pooledT])
                        S.tt("dve", fx_[:, :], cur[:, 0, 15:31], rcb[:, b, :], ALU.mult, [cur, rcb], [fx_])
                        S.tt("dve", pooledT[:, ct, 0:16], fx_[:, :], hx[:, 0, 15:31], ALU.subtract, [fx_, hx], [pooledT])
                        S.red("dve", ws_[:, :], hs[:, :, 16 - w:16], ALU.add, [hs], [ws_])
                        S.stt("dve", pooledT[:, ct, TO:NT], ws_[:, :], 1.0 / w, hs[:, :, 15], ALU.mult, ALU.subtract,
                              [ws_, hs], [pooledT])
                for b in range(4, 8):
                    wb = getwc(b)
                    for cl in range(4):
                        ct = 4 * (b - 4) + cl
                        for ch in CHUNKS_PS:
                            ps = nps()
                            fm1(wb, cl * 128, ch, ps)
                            S.act(sgT[:, ct, ch[0]:ch[0] + ch[1]], ps[:, 0:ch[1]], AF.Silu, [ps], [sgT])
                t5 = [sb(f"t5_{i}", [128, 512], F32, st5a) for i in range(2)]
                c5 = 0
                for g in range(4):
                    for el in range(4):
                        et = 4 * g + el
                        for ch in CHUNKS_PS:
                            t0, n = ch
                            ps = nps()
                            S.mm([(ps[:, 0:n], Wg[:, 4 * g + kl, el * 128:(el + 1) * 128], pooledT[:, 4 * g + kl, t0:t0 + n],
                                   kl == 0, kl == 3) for kl in range(4)], [Wg, pooledT], [ps])
                            tt_ = t5[c5 % 2]
                            c5 += 1
                            S.ts("dve", tt_[:, 0:n], ps[:, 0:n], cbs[:, 0, et:et + 1], cbs[:, 1, et:et + 1], ALU.add, ALU.mult,
                                 [ps, cbs], [tt_])
                            S.tt("pool", sgT[:, et, t0:t0 + n], tt_[:, 0:n], sgT[:, et, t0:t0 + n], ALU.mult, [tt_, sgT], [sgT])
                S.barrier()

            free_group([pooledT])
            st_x1.close()
            with Grp() as st5b:
                def lhs1(kt, t0, n):
                    return sgT[:, kt, t0:t0 + n]

                def sink1(ti, t0, n, y_):
                    S.load("sp", y_o[t0:t0 + n, :], y_[0:n, :], [y_], [])

                out_proj_ln(st5b, 1, lhs1, [sgT], w_out_c, x1s, [d_x1s], sink1)
                S.barrier()
        st_x1.close()

        S.enabled = True
        S.barrier(final=True)
        import sys as _sys
        print("ticks", S.tick, "dma", S.dcount, file=_sys.stderr)

        with nc.Block() as block:
            @block.tensor
            def _(e):
                S.replay("pe", e)

            @block.scalar
            def _(e):
                S.replay("act", e)

            @block.vector
            def _(e):
                S.replay("dve", e)

            @block.gpsimd
            def _(e):
                S.replay("pool", e)

            @block.sync
            def _(e):
                S.replay("sp", e)
    return nc


_PROG = None


def _own_tokens(qi):
    return np.concatenate([np.arange((4 * m + qi) * 128, (4 * m + qi + 1) * 128) for m in range(8)])


def kernel(x_prompt, x_sample, cache_k, cache_v, state_pool, page_table, ln_g, ln_b,
           w_in_ab, a_ln_g, a_ln_b, a_w_s, a_b_s, b_lq1, b_lk1, b_lq2, b_lk2, b_subln_g,
           w_out_ab, w_in_c, c_w_grp, c_b_grp, c_scale, w_out_c):
    global _PROG
    f = lambda a: np.ascontiguousarray(np.asarray(a))
    x_prompt, x_sample = np.asarray(x_prompt), np.asarray(x_sample)
    global NPOOL
    NPOOL = np.asarray(cache_k).shape[1]
    ck = f(cache_k).reshape(NPOOL * 128, 1024)
    cvv = f(cache_v).reshape(NPOOL * 128, 1024)
    page_table = np.asarray(page_table).astype(np.int32)
    state_pool = np.asarray(state_pool)
    a_w_s, a_b_s = np.asarray(a_w_s), np.asarray(a_b_s)

    shared = {
        "cache_k": ck, "cache_v": cvv,
        "w_in": f(np.asarray(w_in_ab)[0]), "w_out": f(np.asarray(w_out_ab)[0]),
        "w_in_c": f(np.asarray(w_in_c)[0]), "w_grp": f(np.asarray(c_w_grp)[0].reshape(2048, 512)),
        "w_out_c": f(np.asarray(w_out_c)[0]),
        "a_ln_g": f(np.asarray(a_ln_g)[0:1]), "a_ln_b": f(np.asarray(a_ln_b)[0:1]),
        "wsT": f(a_w_s[0].transpose(2, 0, 1)),
        "bs": f(a_b_s[0].reshape(1, 512)),
        "w00e": f(np.repeat(a_w_s[0, :, 0, 0], 256)[None, :]),
        "b0e": f(np.repeat(a_b_s[0, :, 0], 256)[None, :]),
        "lqk": f(np.concatenate([np.asarray(b_lq1)[0], np.asarray(b_lk1)[0],
                                 np.asarray(b_lq2)[0], np.asarray(b_lk2)[0]])[None, :]),
        "subln": f(np.asarray(b_subln_g)[0:1]),
        "ln_g": f(ln_g), "ln_b": f(ln_b),
        "c_bT": f(np.asarray(c_b_grp)[0].reshape(16, 128).T),
        "c_sT": f(np.asarray(c_scale)[0].reshape(16, 128).T),
        "ident": np.eye(128, dtype=np.float32),
        "tril": np.triu(np.ones((128, 128), np.float32)),
        "iota_i": np.broadcast_to(np.arange(128, dtype=np.int32)[:, None], (128, 256)).copy(),
    }
    pat = np.zeros((16, 16, 8, 16), np.float32)
    for hm in range(16):
        for s in range(16):
            pat[hm, s, hm // 2, s] = 1.0
    shared["pat"] = pat.reshape(16, -1)
    patoe = np.zeros((16, 2), np.float32)
    patoe[0::2, 0] = 1.0
    patoe[1::2, 1] = 1.0
    shared["patoe"] = patoe

    in_maps = []
    for c in range(NCORES):
        b, qi = c // 4, c % 4
        tok = _own_tokens(qi)
        xp = x_prompt[b][tok]
        xs = x_sample[16 * c:16 * c + 16, 0]
        xt = np.concatenate([xp, xs], 0)
        amask = np.zeros((128, 4, 128), np.float32)
        for r in range(4):
            if r < qi:
                amask[:, r, :] = 1.0
            elif r == qi:
                amask[:, r, :] = np.triu(np.ones((128, 128), np.float32))
        rc = np.zeros((4, 16), np.float32)
        for g, w in enumerate((2, 4, 8, 16)):
            for t in range(16):
                cnt = min(t + 1, w) if qi == 0 else w
                rc[g, t] = 1.0 / cnt
        sel = np.zeros((1, 8), np.float32)
        if qi > 0:
            sel[0, qi - 1] = 1.0
        else:
            sel[0, 4 + 3] = 1.0
        m = dict(shared)
        m.update({
            "xT": f(xt.T), "xtok": f(xt),
            "ptab": f(page_table[16 * c:16 * c + 16].reshape(1, 256)),
            "hist": f(state_pool[0, 16 * c:16 * c + 16]),
            "histT": f(state_pool[0, 16 * c:16 * c + 16].transpose(2, 0, 1)),
            "amask": amask, "rc": rc.reshape(1, 64), "sel": sel,
        })
        in_maps.append(m)

    if _PROG is None:
        _PROG = build_program()
    res = run_bass_kernel_spmd(_PROG, in_maps, core_ids=list(range(NCORES)))
    R = res.results

    y_prompt = np.zeros((2, 4096, 2048), np.float32)
    y_sample = np.zeros((128, 1, 2048), np.float32)
    k_p = np.zeros((1, 2, 4096, 8, 128), np.float32)
    v_p = np.zeros((1, 2, 4096, 8, 128), np.float32)
    k_s = np.zeros((1, 128, 1, 8, 128), np.float32)
    v_s = np.zeros((1, 128, 1, 8, 128), np.float32)
    cv_s = np.zeros((1, 128, 1, 1024), np.float32)
    pool_p = np.zeros((1, 2, 15, 2048), np.float32)
    pool_s = np.zeros((1, 128, 15, 2048), np.float32)
    for c in range(NCORES):
        b, qi = c // 4, c % 4
        tok = _own_tokens(qi)
        r = R[c]
        y_prompt[b, tok] = r["y"][:TO]
        y_sample[16 * c:16 * c + 16, 0] = r["y"][TO:]
        k_p[0, b, tok] = r["knew"][:TO].reshape(TO, 8, 128)
        v_p[0, b, tok] = r["vnew"][:TO].reshape(TO, 8, 128)
        k_s[0, 16 * c:16 * c + 16, 0] = r["knew"][TO:].reshape(NS, 8, 128)
        v_s[0, 16 * c:16 * c + 16, 0] = r["vnew"][TO:].reshape(NS, 8, 128)
        cv_s[0, 16 * c:16 * c + 16, 0] = r["cv"]
        if qi == 3:
            pool_p[0, b] = r["poolp"]
        pool_s[0, 16 * c:16 * c + 16] = r["pools"]
    return (y_prompt, y_sample, k_p, v_p, k_s, v_s, cv_s, pool_p, pool_s)
```

```python
import math
import numpy as np
import ml_dtypes
import concourse.bass as bass
import concourse.mybir as mybir
from concourse.bass_utils import run_bass_kernel_spmd

F32 = mybir.dt.float32
BF16 = mybir.dt.bfloat16
I32 = mybir.dt.int32
ALU = mybir.AluOpType
AF = mybir.ActivationFunctionType
AX = mybir.AxisListType

NCORES = 8
D = 2048
TO = 1024
NS = 16
NT = TO + NS
NH = 120
NX = NT + NH
ALPHA = 4 ** 0.25
LN_EPS = 1e-5
LAM_INIT = 0.8 - 0.6 * math.exp(0.0)
NEG = -1e30
CHUNKS_P = [(0, 512), (512, 512)]
CHUNKS_PS = [(0, 512), (512, 512), (1024, 16)]
TILES = [(i * 128, 128) for i in range(8)] + [(1024, 16)]


class Buf:
    def __init__(self, t, multi=False):
        self.t = t
        self.multi = multi
        self.lw = []
        self.rd = []

    def __getitem__(self, k):
        return self.t[k]


class Sched:
    ENG = ["pe", "act", "dve", "pool", "sp"]

    def __init__(self, nc, esem, dsem):
        self.nc = nc
        self.stream = {e: [] for e in self.ENG}
        self.tick = {e: 0 for e in self.ENG}
        self.seen = {e: {} for e in self.ENG}
        self.esem = esem
        self.dsem = dsem
        self.dcount = {q: 0 for q in dsem}
        self.dval = {}
        self.dlast = {}
        self.ccn = 0
        self.enabled = True

    def _need(self, eng, dep):
        if dep is None:
            return
        if dep[0] == "eng":
            _, name, tick = dep
            if name == eng and name == "pe":
                return
            key = ("eng", name)
            if self.seen[eng].get(key, 0) >= tick:
                return
            self.seen[eng][key] = tick
            self.stream[eng].append(("wait", self.esem[name], tick))
        else:
            _, key, val = dep
            if self.seen[eng].get(key, 0) >= val:
                return
            self.seen[eng][key] = val
            q, k = key
            self.stream[eng].append(("wait", self.dsem[q][k], val))

    def _deps(self, eng, reads, writes):
        best = {}
        for b in reads:
            for d in b.lw:
                k = d[:2]
                if k not in best or best[k][2] < d[2]:
                    best[k] = d
        for b in writes:
            for d in b.rd:
                k = d[:2]
                if k not in best or best[k][2] < d[2]:
                    best[k] = d
            if not b.multi:
                for d in b.lw:
                    k = d[:2]
                    if k not in best or best[k][2] < d[2]:
                        best[k] = d
        for d in best.values():
            self._need(eng, d)

    def _mark(self, dep, reads, writes):
        for b in reads:
            b.rd.append(dep)
        for b in writes:
            if b.multi:
                b.lw.append(dep)
            else:
                b.lw = [dep]
                b.rd = []

    def op(self, eng, fn, reads=(), writes=()):
        if not self.enabled:
            return None
        self._deps(eng, reads, writes)
        self.tick[eng] += 1
        dep = ("eng", eng, self.tick[eng])
        self.stream[eng].append(("ins", fn, self.esem[eng], 1))
        self._mark(dep, reads, writes)
        return dep

    def dma(self, q, fn, reads=(), writes=()):
        if not self.enabled:
            return None
        self._deps(q, reads, writes)
        k = self.dcount[q] % len(self.dsem[q])
        self.dcount[q] += 1
        key = (q, k)
        self._need(q, self.dlast.get(key))
        val = self.dval.get(key, 0) + 16
        self.dval[key] = val
        dep = ("dma", key, val)
        self.dlast[key] = dep
        self.stream[q].append(("ins", fn, self.dsem[q][k], 16))
        self._mark(dep, reads, writes)
        return dep

    def barrier(self, final=False):
        if not self.enabled:
            return
        deps = [("eng", e, self.tick[e]) for e in ("pe", "act", "dve", "pool") if self.tick[e] > 0]
        deps += [d for k, d in self.dlast.items() if final or k[0] != "cc"]
        for e in self.ENG:
            for d in deps:
                if d[0] == "eng" and d[1] == e:
                    continue
                self._need(e, d)

    def replay(self, eng, e):
        for item in self.stream[eng]:
            if item[0] == "wait":
                e.wait_ge(item[1], item[2])
            else:
                ins = item[1](e)
                ins.then_inc(item[2], item[3])

    def mm(self, mms, reads, writes):
        mms = list(mms)

        def fn(e):
            ins = None
            for (o, l, r, st, sp) in mms:
                ins = e.matmul(o, l, r, start=st, stop=sp)
            return ins
        return self.op("pe", fn, reads, writes)

    def tr(self, trs, reads, writes):
        trs = list(trs)

        def fn(e):
            ins = None
            for (o, i, idn) in trs:
                ins = e.transpose(o, i, idn)
            return ins
        return self.op("pe", fn, reads, writes)

    def act(self, out, in_, func, reads, writes, scale=None, bias=None, accum=None):
        kw = {}
        if scale is not None:
            kw["scale"] = scale
        if bias is not None:
            kw["bias"] = bias
        if accum is not None:
            kw["accum_out"] = accum
        return self.op("act", lambda e: e.activation(out, in_, func, **kw), reads, writes)

    def copy(self, eng, out, in_, reads, writes):
        if eng == "act":
            return self.op("act", lambda e: e.copy(out, in_), reads, writes)
        return self.op(eng, lambda e: e.tensor_copy(out, in_), reads, writes)

    def tt(self, eng, out, a, b, op, reads, writes):
        return self.op(eng, lambda e: e.tensor_tensor(out, a, b, op), reads, writes)

    def ts(self, eng, out, a, s1, s2, op0, op1, reads, writes):
        if op1 is None:
            return self.op(eng, lambda e: e.tensor_scalar(out, a, s1, None, op0), reads, writes)
        return self.op(eng, lambda e: e.tensor_scalar(out, a, s1, s2, op0, op1), reads, writes)

    def stt(self, eng, out, a, s, b, op0, op1, reads, writes):
        return self.op(eng, lambda e: e.scalar_tensor_tensor(out, a, s, b, op0, op1), reads, writes)

    def red(self, eng, out, in_, op, reads, writes, axis=AX.X):
        return self.op(eng, lambda e: e.tensor_reduce(out, in_, axis, op), reads, writes)

    def memset(self, eng, ap, val, writes):
        return self.op(eng, lambda e: e.memset(ap, val), (), writes)

    def load(self, q, out, in_, reads, writes):
        return self.dma(q, lambda e: e.dma_start(out=out, in_=in_), reads, writes)


STOP = 99
NPOOL = 2560


def build_program():
    nc = bass.Bass("TRN2", target_bir_lowering=False)

    def din(name, shape, dt=F32):
        return nc.dram_tensor(name, list(shape), dt, kind="ExternalInput").ap()

    def dout(name, shape, dt=F32):
        return nc.dram_tensor(name, list(shape), dt, kind="ExternalOutput").ap()

    def dint(name, shape, dt):
        return nc.dram_tensor(name, list(shape), dt, kind="Internal").ap()

    xT = din("xT", [D, NT])
    xtok = din("xtok", [NT, D])
    cache_k = din("cache_k", [NPOOL * 128, 1024])
    cache_v = din("cache_v", [NPOOL * 128, 1024])
    ptab = din("ptab", [1, 256], I32)
    iota_i = din("iota_i", [128, 256], I32)
    hist = din("hist", [NS, 15, D])
    histT = din("histT", [D, NS, 15])
    w_in = din("w_in", [D, 7168])
    w_out = din("w_out", [D, D])
    w_in_c = din("w_in_c", [D, 4096])
    w_grp = din("w_grp", [D, 512])
    w_out_c = din("w_out_c", [D, D])
    a_ln_g = din("a_ln_g", [1, 1024])
    a_ln_b = din("a_ln_b", [1, 1024])
    wsT = din("wsT", [128, 4, 128])
    bs = din("bs", [1, 512])
    w00e = din("w00e", [1, 1024])
    b0e = din("b0e", [1, 1024])
    lqk = din("lqk", [1, 256])
    subln = din("subln", [1, 128])
    ln_g = din("ln_g", [2, D])
    ln_b = din("ln_b", [2, D])
    c_bT = din("c_bT", [128, 16])
    c_sT = din("c_sT", [128, 16])
    ident_d = din("ident", [128, 128])
    tril_d = din("tril", [128, 128])
    amask_d = din("amask", [128, 4, 128])
    rc_d = din("rc", [1, 64])
    sel_d = din("sel", [1, 8])
    pat_d = din("pat", [16, 16 * 8 * 16])
    patoe_d = din("patoe", [16, 2])

    y_o = dout("y", [NT, D])
    knew_o = dout("knew", [NT, 1024])
    vnew_o = dout("vnew", [NT, 1024])
    cv_o = dout("cv", [NS, 1024])
    poolp_o = dout("poolp", [15, D])
    pools_o = dout("pools", [NS, 15, D])

    kv_src = dint("kv_src", [2048, 1024], BF16)
    kv_dst = [dint(f"kv_dst{i}", [4 * 128, 1024], BF16) for i in range(16)]
    x1s = dint("x1s", [NT, D], F32)
    halo_src = dint("halo_src", [D, NH], BF16)
    halo_dst = dint("halo_dst", [4 * D, NH], BF16)
    sqkv = [dint(f"sqkv{i}", [NS, 1024], F32) for i in range(3)]

    from contextlib import ExitStack
    es = ExitStack()
    with es:
        esem = {e: es.enter_context(nc.semaphore("es_" + e)) for e in ("pe", "act", "dve", "pool")}
        dsem = {"sp": [es.enter_context(nc.semaphore(f"dsp{i}")) for i in range(24)],
                "pool": [es.enter_context(nc.semaphore(f"dpl{i}")) for i in range(16)]}
        ccsem = [es.enter_context(nc.semaphore(f"cc{i}")) for i in range(17)]
        S = Sched(nc, esem, dsem)

        free_list = [[16512, 229344]]
        nalloc = [0]

        def sb(name, shape, dt, grp, multi=False):
            esz = 2 if dt == BF16 else 4
            nb = esz
            for d_ in shape[1:]:
                nb *= d_
            nb = (nb + 63) // 64 * 64
            for fr in free_list:
                if fr[1] - fr[0] >= nb:
                    off = fr[0]
                    fr[0] += nb
                    break
            else:
                raise RuntimeError(f"SBUF arena full allocating {name} {shape} ({nb} B); free={free_list}")
            nalloc[0] += 1
            b = Buf(nc.alloc_sbuf_tensor_at(f"{name}_{nalloc[0]}", list(shape), dt, offset=off), multi)
            b.rng = (off, off + nb)
            grp.append(b)
            return b

        def free_group(grp):
            for b in grp:
                free_list.append(list(b.rng))
            del grp[:]
            free_list.sort()
            i = 0
            while i + 1 < len(free_list):
                if free_list[i][1] >= free_list[i + 1][0]:
                    free_list[i][1] = max(free_list[i][1], free_list[i + 1][1])
                    del free_list[i + 1]
                else:
                    i += 1
            free_list[:] = [f_ for f_ in free_list if f_[1] > f_[0]]

        class Grp(list):
            def __enter__(self):
                return self

            def __exit__(self, *a):
                free_group(self)
                return False

            def close(self):
                free_group(self)

        def psb(name, stack, dt=F32, n=512):
            return Buf(stack.enter_context(nc.psum_tensor(name, [128, n], dt)))

        class DB(Buf):
            def __init__(self):
                Buf.__init__(self, None, True)
        d_kvsrc, d_kvdst, d_x1s, d_hsrc, d_hdst, d_sqkv = DB(), DB(), DB(), DB(), DB(), DB()

        def collective(idx, src, dst, bsrc, bdst):
            if not S.enabled:
                return
            S._deps("pool", [bsrc], [bdst])
            sem = ccsem[idx]

            def fn(e):
                return e.collective_compute("AllGather", ALU.bypass,
                                            replica_groups=[[0, 1, 2, 3], [4, 5, 6, 7]],
                                            ins=[src], outs=[dst])
            S.stream["pool"].append(("ins", fn, sem, 1))
            S.dsem.setdefault("cc", ccsem)
            dep = ("dma", ("cc", idx), 1)
            S.dlast[("cc", idx)] = dep
            S._mark(dep, [bsrc], [bdst])

        pers = Grp()
        ident = sb("ident", [128, 128], F32, pers)
        identb = sb("identb", [128, 128], BF16, pers)
        lamc = sb("lamc", [128, 4], F32, pers)
        epsc = sb("epsc", [128, 1], F32, pers)
        S.memset("dve", epsc[:], LN_EPS, [epsc])
        S.load("sp", ident[:], ident_d, [], [ident])
        S.copy("dve", identb[:], ident[:], [ident], [identb])
        with Grp() as st0:
            lq = sb("lq", [128, 256], F32, st0)
            lt = sb("lt", [128, 128], F32, st0)
            ls = sb("ls", [128, 2], F32, st0)
            le = sb("le", [128, 2], F32, st0)
            S.load("sp", lq[:], lqk.partition_broadcast(128), [], [lq])
            lqv = lq[:].rearrange("p (a b c) -> p a b c", a=2, b=2)
            S.tt("dve", lt[:].rearrange("p (a c) -> p a c", a=2), lqv[:, :, 0, :], lqv[:, :, 1, :], ALU.mult, [lq], [lt])
            S.red("dve", ls[:], lt[:].rearrange("p (a c) -> p a c", a=2), ALU.add, [lt], [ls])
            S.act(le[:], ls[:], AF.Exp, [ls], [le])
            S.tt("dve", lamc[:, 2:3], le[:, 0:1], le[:, 1:2], ALU.subtract, [le], [lamc])
            S.ts("dve", lamc[:, 0:1], lamc[:, 2:3], LAM_INIT, None, ALU.add, None, [lamc], [lamc])
            S.ts("dve", lamc[:, 1:2], lamc[:, 0:1], -1.0, None, ALU.mult, None, [lamc], [lamc])
            S.barrier()

        ps_stack = ExitStack()
        es.enter_context(ps_stack)
        PS = [psb(f"ps{i}", ps_stack) for i in range(6)]
        PB = psb("psB", ps_stack, BF16, 1024)
        PM = psb("psM", ps_stack)
        psi = [0]

        def nps():
            b = PS[psi[0] % 6]
            psi[0] += 1
            return b

        st_ab = Grp()
        a_outT = sb("a_outT", [128, 8, NT], BF16, st_ab, multi=True)

        st_p12 = Grp()
        QT = sb("QT", [128, 8, TO], BF16, st_p12, multi=True)
        sbg = sb("sbg", [128, 9, 1024], BF16, st_p12, multi=True)
        ktok_s = sb("ktok_s", [NS, 1024], F32, st_p12, multi=True)
        vtok_s = sb("vtok_s", [NS, 1024], F32, st_p12, multi=True)
        qtok_s = sb("qtok_s", [NS, 1024], F32, st_p12, multi=True)

        S.enabled = STOP >= 0.1
        with Grp() as st1:
            xTb = [sb(f"xTb{i}", [128, 4, NT], BF16, st1) for i in range(4)]
            xTv = xT.rearrange("(k p) t -> p k t", p=128)
            for i in range(4):
                S.load("pool", xTb[i][:], xTv[:, 4 * i:4 * i + 4, :], [], [xTb[i]])
            W = [sb(f"W{i}", [128, 16, 512], BF16, st1) for i in range(3)]
            wcnt = [0]
            w_inv = w_in.rearrange("(k p) c -> p k c", p=128)

            def loadw(blk):
                b = W[wcnt[0] % len(W)]
                wcnt[0] += 1
                S.load("pool", b[:], w_inv[:, :, blk * 512:(blk + 1) * 512], [], [b])
                return b

            def fm(wb, c0, chunk, ps):
                t0, n = chunk
                S.mm([(ps[:, 0:n], wb[:, kt, c0:c0 + 128], xTb[kt // 4][:, kt % 4, t0:t0 + n],
                       kt == 0, kt == 15) for kt in range(16)], [wb] + xTb, [ps])

            def tm(wb, tile, ps):
                t0, n = tile
                S.mm([(ps[0:n, :], xTb[kt // 4][:, kt % 4, t0:t0 + n], wb[:, kt, :],
                       kt == 0, kt == 15) for kt in range(16)], [wb] + xTb, [ps])

            stg = [sb(f"stg{i}", [128, 512], F32, st1) for i in range(3)]
            stgb = [sb(f"stgb{i}", [128, 1024], BF16, st1) for i in range(3)]
            sc = [0, 0]

            def nstg():
                b = stg[sc[0] % 3]
                sc[0] += 1
                return b

            def nstgb():
                b = stgb[sc[1] % 3]
                sc[1] += 1
                return b

            kvs_k = kv_src[0:1024, :].rearrange("(h p) t -> p h t", p=128)
            seqA = [8, 9, 10, 11, 6, 7, 12, 13]
            wbufA = {}

            def getw(pos):
                for p_ in range(pos + 3):
                    if p_ < len(seqA) and p_ not in wbufA:
                        wbufA[p_] = loadw(seqA[p_])
                return wbufA[pos]
            getw(0)
            for bi in range(2):
                wb = getw(bi)
                for hl in range(4):
                    h = 4 * bi + hl
                    kb = nstgb()
                    for ci, ch in enumerate(CHUNKS_P):
                        ps = nps()
                        fm(wb, hl * 128, ch, ps)
                        S.copy("act", kb[:, ch[0]:ch[0] + 512], ps[:, 0:512], [ps], [kb])
                    S.load("sp", kvs_k[:, h, :], kb[:, :], [kb], [d_kvsrc])
                for ti, tile in enumerate(TILES):
                    ps = nps()
                    tm(wb, tile, ps)
                    n = tile[1]
                    if ti < 8:
                        sg_ = nstg()
                        S.copy("dve", sg_[0:n, :], ps[0:n, :], [ps], [sg_])
                        S.load("sp", knew_o[tile[0]:tile[0] + n, bi * 512:(bi + 1) * 512], sg_[0:n, :], [sg_], [])
                    else:
                        S.copy("dve", ktok_s[:, bi * 512:(bi + 1) * 512], ps[0:n, :], [ps], [ktok_s])
            S.enabled = STOP >= 0.3
            for bi in range(2):
                wb = getw(2 + bi)
                for ti, tile in enumerate(TILES):
                    ps = nps()
                    tm(wb, tile, ps)
                    n = tile[1]
                    if ti < 8:
                        sg_ = nstg()
                        S.copy("dve", sg_[0:n, :], ps[0:n, :], [ps], [sg_])
                        S.load("sp", vnew_o[tile[0]:tile[0] + n, bi * 512:(bi + 1) * 512], sg_[0:n, :], [sg_], [])
                        vb = nstgb()
                        S.copy("act", vb[:, 0:512], sg_[:, :], [sg_], [vb])
                        S.load("sp", kv_src[1024 + tile[0]:1024 + tile[0] + 128, bi * 512:(bi + 1) * 512],
                               vb[:, 0:512], [vb], [d_kvsrc])
                    else:
                        S.copy("dve", vtok_s[:, bi * 512:(bi + 1) * 512], ps[0:n, :], [ps], [vtok_s])
            S.load("sp", knew_o[TO:NT, :], ktok_s[:, :], [ktok_s], [])
            S.load("sp", vnew_o[TO:NT, :], vtok_s[:, :], [vtok_s], [])
            S.load("sp", sqkv[1], ktok_s[:, :], [ktok_s], [d_sqkv])
            S.load("sp", sqkv[2], vtok_s[:, :], [vtok_s], [d_sqkv])
            S.enabled = STOP >= 0.4
            for ci_ in range(16):
                collective(1 + ci_, kv_src[ci_ * 128:(ci_ + 1) * 128, :], kv_dst[ci_], d_kvsrc, d_kvdst)

            S.enabled = STOP >= 0.5
            for bi in range(2):
                wb = getw(4 + bi)
                for hl in range(4):
                    h = 4 * bi + hl
                    for ch in CHUNKS_P:
                        ps = nps()
                        fm(wb, hl * 128, ch, ps)
                        S.act(QT[:, h, ch[0]:ch[0] + 512], ps[:, 0:512], AF.Copy, [ps], [QT], scale=0.125)
                ps = nps()
                tm(wb, TILES[8], ps)
                S.act(qtok_s[:, bi * 512:(bi + 1) * 512], ps[0:NS, :], AF.Copy, [ps], [qtok_s], scale=0.125)
            S.load("sp", sqkv[0], qtok_s[:, :], [qtok_s], [d_sqkv])

            for bi in range(2):
                wb = getw(6 + bi)
                for ti, tile in enumerate(TILES):
                    ps = nps()
                    tm(wb, tile, ps)
                    n = tile[1]
                    S.act(sbg[0:n, ti, bi * 512:(bi + 1) * 512], ps[0:n, :], AF.Silu, [ps], [sbg])

            S.enabled = STOP >= 0.7
            S.barrier()
            free_group([W.pop(2)] + stg + stgb)
            vn = sb("vn", [128, 8, 1024], BF16, st1, multi=True)
            cvs = sb("cvs", [NS, 1024], F32, st1)
            algb = sb("algb", [128, 2, 1024], F32, st1)
            S.load("sp", algb[:, 0, :], a_ln_g.partition_broadcast(128), [], [algb])
            S.load("sp", algb[:, 1, :], a_ln_b.partition_broadcast(128), [], [algb])
            wa = [loadw(2), loadw(3)]
            lnst = [sb(f"lnst{i}", [128, 16], F32, st1) for i in range(2)]
            lnmv = [sb(f"lnmv{i}", [128, 4], F32, st1) for i in range(2)]
            lnt = [sb(f"lnt{i}", [128, 1024], F32, st1) for i in range(2)]
            for ti, tile in enumerate(TILES):
                n = tile[1]
                pA, pB = nps(), nps()
                tm(wa[0], tile, pA)
                tm(wa[1], tile, pB)
                stt_, mv, t32 = lnst[ti % 2], lnmv[ti % 2], lnt[ti % 2]
                S.op("dve", lambda e, o=stt_[0:n, 0:6], i=pA[0:n, :]: e.bn_stats(o, i), [pA], [stt_])
                S.op("dve", lambda e, o=stt_[0:n, 6:12], i=pB[0:n, :]: e.bn_stats(o, i), [pB], [stt_])
                S.op("dve", lambda e, o=mv[0:n, 0:2], i=stt_[0:n, 0:12]: e.bn_aggr(o, i), [stt_], [mv])
                S.act(mv[0:n, 2:3], mv[0:n, 1:2], AF.Sqrt, [mv], [mv], scale=1.0, bias=epsc[0:n, 0:1])
                S.op("dve", lambda e, o=mv[0:n, 2:3]: e.reciprocal(o, o), [mv], [mv])
                S.ts("dve", t32[0:n, 0:512], pA[0:n, :], mv[0:n, 0:1], mv[0:n, 2:3], ALU.subtract, ALU.mult, [pA, mv], [t32])
                S.ts("dve", t32[0:n, 512:1024], pB[0:n, :], mv[0:n, 0:1], mv[0:n, 2:3], ALU.subtract, ALU.mult, [pB, mv], [t32])
                S.tt("pool", t32[0:n, :], t32[0:n, :], algb[0:n, 0, :], ALU.mult, [t32, algb], [t32])
                if ti < 8:
                    S.tt("pool", vn[:, ti, :], t32[:, :], algb[:, 1, :], ALU.add, [t32, algb], [vn])
                else:
                    S.tt("pool", cvs[:, :], t32[0:n, :], algb[0:n, 1, :], ALU.add, [t32, algb], [cvs])
            S.load("sp", cv_o[:, :], cvs[:, :], [cvs], [])

            S.enabled = STOP >= 0.9
            wsm = sb("wsm", [128, 4, 128], F32, st1)
            wsb = sb("wsb", [128, 4, 128], BF16, st1)
            trl = sb("trl", [128, 128], F32, st1)
            bsb = sb("bsb", [128, 4, 4, 128], F32, st1)
            S.load("sp", wsm[:], wsT, [], [wsm])
            S.load("sp", trl[:], tril_d, [], [trl])
            for r in range(4):
                S.load("sp", bsb[:, :, r, :], bs.rearrange("o (g i) -> o g i", g=4).partition_broadcast(128), [], [bsb])
            for g in range(4):
                S.tt("dve", wsb[:, g, :], wsm[:, g, :], trl[:, :], ALU.mult, [wsm, trl], [wsb])
            w0b = sb("w0b", [NS, 2, 1024], F32, st1)
            S.load("sp", w0b[:, 0, :], w00e.partition_broadcast(NS), [], [w0b])
            S.load("sp", w0b[:, 1, :], b0e.partition_broadcast(NS), [], [w0b])
            ms = sb("ms", [NS, 1024], F32, st1)
            S.tt("dve", ms[:, :], cvs[:, :], w0b[:, 0, :], ALU.mult, [cvs, w0b], [ms])
            S.tt("dve", ms[:, :], ms[:, :], w0b[:, 1, :], ALU.add, [ms, w0b], [ms])
            aos = sb("aos", [NS, 1024], BF16, st1, multi=True)
            sgt = [sb(f"sgt{i}", [128, 512], BF16, st1) for i in range(2)]
            t1 = [sb(f"t1_{i}", [128, 512], F32, st1) for i in range(2)]
            cnt = 0
            for bi in range(2):
                wu = loadw(bi)
                wg = loadw(4 + bi)
                for fl in range(4):
                    ft = 4 * bi + fl
                    g = ft // 2
                    for ci, ch in enumerate(CHUNKS_P):
                        pU, pG, pM = nps(), nps(), nps()
                        fm(wu, fl * 128, ch, pU)
                        fm(wg, fl * 128, ch, pG)
                        S.mm([(pM[:, i * 128:(i + 1) * 128], vn[:, 4 * ci + i, ft * 128:(ft + 1) * 128],
                               wsb[:, g, :], True, True) for i in range(4)], [vn, wsb], [pM])
                        sg_, tt_ = sgt[cnt % 2], t1[cnt % 2]
                        cnt += 1
                        S.act(sg_[:, :], pG[:, :], AF.Silu, [pG], [sg_])
                        S.tt("dve", tt_[:, :], pM[:, :], bsb[:, g, :, :].rearrange("p r i -> p (r i)"), ALU.add, [pM, bsb], [tt_])
                        S.tt("dve", tt_[:, :], tt_[:, :], pU[:, :], ALU.mult, [tt_, pU], [tt_])
                        S.tt("pool", a_outT[:, ft, ch[0]:ch[0] + 512], tt_[:, :], sg_[:, :], ALU.mult, [tt_, sg_], [a_outT])
                pU, pG = nps(), nps()
                tm(wu, TILES[8], pU)
                tm(wg, TILES[8], pG)
                sg_, tt_ = sgt[cnt % 2], t1[cnt % 2]
                cnt += 1
                S.act(sg_[0:NS, :], pG[0:NS, :], AF.Silu, [pG], [sg_])
                S.tt("dve", tt_[0:NS, :], pU[0:NS, :], ms[:, bi * 512:(bi + 1) * 512], ALU.mult, [pU, ms], [tt_])
                S.tt("dve", aos[:, bi * 512:(bi + 1) * 512], tt_[0:NS, :], sg_[0:NS, :], ALU.mult, [tt_, sg_], [aos])
            pT = PB
            S.tr([(pT[:, ft * NS:(ft + 1) * NS], aos[:, ft * 128:(ft + 1) * 128], identb[0:NS, 0:NS]) for ft in range(8)],
                 [aos, identb], [pT])
            S.copy("dve", a_outT[:, :, TO:NT], pT[:, 0:8 * NS].rearrange("p (f s) -> p f s", f=8), [pT], [a_outT])
            S.barrier()

        S.enabled = STOP >= 2
        o_samp = sb("o_samp", [NS, 1024], F32, st_p12)
        with Grp() as st2:
            ptb = sb("ptb", [128, 256], I32, st2)
            iob = sb("iob", [128, 256], I32, st2)
            idx = sb("idx", [128, 256], I32, st2)
            S.load("sp", ptb[:], ptab.partition_broadcast(128), [], [ptb])
            S.load("sp", iob[:], iota_i, [], [iob])
            S.op("dve", lambda e: e.tensor_single_scalar(idx[:], ptb[:], 7, ALU.logical_shift_left), [ptb], [idx])
            S.tt("dve", idx[:], idx[:], iob[:], ALU.bitwise_or, [idx, iob], [idx])
            pat = sb("pat", [16, 16, 8, 16], F32, st2)
            patoe = sb("patoe", [16, 2], F32, st2)
            lamv = sb("lamv", [16, 2], F32, st2)
            ones = sb("ones", [128, 1], F32, st2)
            S.load("sp", pat[:].rearrange("p a b c -> p (a b c)"), pat_d, [], [pat])
            S.load("sp", patoe[:], patoe_d, [], [patoe])
            S.memset("dve", ones[:], 1.0, [ones])
            S.ts("dve", lamv[:, 1:2], patoe[:, 1:2], lamc[0:16, 0:1], None, ALU.mult, None, [patoe, lamc], [lamv])
            S.tt("dve", lamv[:, 0:1], patoe[:, 0:1], lamv[:, 1:2], ALU.subtract, [patoe, lamv], [lamv])
            KB = [sb(f"KB{i}", [128, 1024], F32, st2) for i in range(3)]
            VB = [sb(f"VB{i}", [128, 1024], F32, st2) for i in range(3)]
            tmpb = [sb(f"tmpb{i}", [128, 1024], F32, st2) for i in range(2)]
            qbc = [sb(f"qbc{i}", [128, 1024], F32, st2) for i in range(2)]
            ksf = [sb(f"ksf{i}", [1, 1024], F32, st2) for i in range(2)]
            vsf = [sb(f"vsf{i}", [1, 1024], F32, st2) for i in range(2)]
            Sb = [sb(f"Sb{i}", [128, 16, 17], F32, st2) for i in range(2)]
            Eb = [sb(f"Eb{i}", [128, 16, 17], F32, st2) for i in range(2)]
            esum = [sb(f"esum{i}", [128, 16], F32, st2) for i in range(2)]
            rinv = [sb(f"rinv{i}", [16, 2], F32, st2) for i in range(2)]
            csel = [sb(f"csel{i}", [16, 8, 16], F32, st2) for i in range(2)]
            ofs = [sb(f"ofs{i}", [16, 1024], F32, st2) for i in range(2)]
            for i in range(2):
                S.memset("dve", Sb[i][:, :, 16:17], NEG, [Sb[i]])
            pO = [PS[0], PS[1]]
            pF = [[PS[2], PS[3]], [PS[4], PS[5]]]
            pL = [PM, PM]
            kc = 0
            vc = 0

            def k_page(s, j):
                nonlocal kc
                qb, Sx = qbc[s % 2], Sb[s % 2]
                kb = KB[kc % 3]
                tb = tmpb[kc % 2]
                kc += 1
                col = s * 16 + j
                S.dma("pool", lambda e, o=kb[:, :], c=col: e.indirect_dma_start(
                    out=o, out_offset=None, in_=cache_k,
                    in_offset=bass.IndirectOffsetOnAxis(ap=idx[:, c:c + 1], axis=0)), [idx], [kb])
                S.tt("dve", tb[:, :], kb[:, :], qb[:, :], ALU.mult, [kb, qb], [tb])
                S.red("dve", Sx[:, :, j], tb[:, :].rearrange("p (a d) -> p a d", d=64), ALU.add, [tb], [Sx])

            def k_finish(s):
                nonlocal kc
                qb, kf, Sx, Ex = qbc[s % 2], ksf[s % 2], Sb[s % 2], Eb[s % 2]
                tb = tmpb[kc % 2]
                kc += 1
                S.tt("dve", tb[0:1, :], kf[0:1, :], qb[0:1, :], ALU.mult, [kf, qb], [tb])
                S.red("dve", Sx[0:1, :, 16], tb[0:1, :].rearrange("p (a d) -> p a d", d=64), ALU.add, [tb], [Sx])
                S.act(Ex[:], Sx[:], AF.Exp, [Sx], [Ex])
                es_ = esum[s % 2]
                S.red("dve", es_[:, :], Ex[:], ALU.add, [Ex], [es_])
                pl = pL[s % 2]
                S.mm([(pl[0:16, 0:1], es_[:, :], ones[:, 0:1], True, True)], [es_, ones], [pl])
                ri = rinv[s % 2]
                S.op("dve", lambda e, o=ri[:, 0:1], i=pl[0:16, 0:1]: e.reciprocal(o, i), [pl], [ri])
                S.tt("dve", ri[:, 1:2], ri[:, 0:1], lamv[:, 0:1], ALU.mult, [ri, lamv], [ri])
                S.ts("dve", csel[s % 2][:], pat[:, s, :, :], ri[:, 1:2], None, ALU.mult, None, [pat, ri], [csel[s % 2]])

            def v_page(s, j):
                nonlocal vc
                Ex, pf = Eb[s % 2], pF[s % 2]
                vb = VB[vc % 3]
                vc += 1
                col = s * 16 + j
                S.dma("pool", lambda e, o=vb[:, :], c=col: e.indirect_dma_start(
                    out=o, out_offset=None, in_=cache_v,
                    in_offset=bass.IndirectOffsetOnAxis(ap=idx[:, c:c + 1], axis=0)), [idx], [vb])
                S.mm([(pf[hf][0:16, :], Ex[:, :, j], vb[:, hf * 512:(hf + 1) * 512], j == 0, False)
                      for hf in range(2)], [Ex, vb], pf)

            def v_finish(s):
                Ex, pf, vf = Eb[s % 2], pF[s % 2], vsf[s % 2]
                S.mm([(pf[hf][0:16, :], Ex[0:1, :, 16], vf[0:1, hf * 512:(hf + 1) * 512], False, True)
                      for hf in range(2)], [Ex, vf], pf)
                cs, of_ = csel[s % 2], ofs[s % 2]
                for hf in range(2):
                    S.copy("act", of_[:, hf * 512:(hf + 1) * 512], pf[hf][0:16, :], [pf[hf]], [of_])
                S.mm([(pO[h // 4][0:16, (h % 4) * 128:(h % 4 + 1) * 128], cs[:, h, :], of_[:, h * 128:(h + 1) * 128],
                       s == 0 and h % 4 == 0, s == NS - 1) for h in range(8)], [cs, of_], pO)

            for s in range(NS + 1):
                if s < NS:
                    S.load("sp", qbc[s % 2][:], sqkv[0][s:s + 1, :].partition_broadcast(128), [d_sqkv], [qbc[s % 2]])
                    S.load("sp", ksf[s % 2][:], sqkv[1][s:s + 1, :], [d_sqkv], [ksf[s % 2]])
                for j in range(16):
                    if s < NS:
                        k_page(s, j)
                    if s >= 1:
                        v_page(s - 1, j)
                if s >= 1:
                    v_finish(s - 1)
                if s < NS:
                    S.load("sp", vsf[s % 2][:], sqkv[2][s:s + 1, :], [d_sqkv], [vsf[s % 2]])
                    k_finish(s)
            for hf in range(2):
                S.copy("act", o_samp[:, hf * 512:(hf + 1) * 512], pO[hf][0:16, :], [pO[hf]], [o_samp])
            S.barrier()

        S.enabled = STOP >= 3
        b_outT = sb("b_outT", [128, 8, NT], BF16, st_ab, multi=True)
        with Grp() as st3:
            KTh = [sb(f"KTh{i}", [128, 4, 1024], BF16, st3) for i in range(2)]
            Vh = [sb(f"Vh{i}", [128, 32, 129], BF16, st3) for i in range(2)]
            for i in range(2):
                S.memset("dve", Vh[i][:, :, 128:129], 1.0, [Vh[i]])
            amk = sb("amk", [128, 4, 128], F32, st3)
            S.load("sp", amk[:], amask_d, [], [amk])
            sgb = sb("sgb", [128, 128], F32, st3)
            S.load("sp", sgb[:], subln.partition_broadcast(128), [], [sgb])
            S.ts("dve", sgb[:], sgb[:], 1.0 - LAM_INIT, None, ALU.mult, None, [sgb], [sgb])
            PT = [sb(f"PT{i}", [128, 512], BF16, st3) for i in range(3)]
            Osb2 = [sb(f"Osb{i}", [128, 2, 8, 129], F32, st3, multi=True) for i in range(2)]
            rcp2 = [sb(f"rcp{i}", [128, 2, 8], F32, st3) for i in range(2)]
            bo = sb("bo", [128, 9, 1024], BF16, st3, multi=True)
            od = [sb(f"od{i}", [128, 128], F32, st3) for i in range(2)]
            sq_ = [sb(f"sq{i}", [128, 128], F32, st3) for i in range(2)]
            ssb = [sb(f"ssb{i}", [128, 2], F32, st3) for i in range(2)]
            pST = [PS[0], PS[1], PS[2]]
            pOA = [PS[3], PS[4], PS[5]]
            stc = 0
            ptc = 0
            ec = 0

            def epilogue(src_ap_fn, n, ti, h, reads):
                nonlocal ec
                o_, q_, s_ = od[ec % 2], sq_[ec % 2], ssb[ec % 2]
                ec += 1
                src_ap_fn(o_)
                S.act(q_[0:n, :], o_[0:n, :], AF.Square, [o_], [q_, s_], accum=s_[0:n, 0:1])
                S.act(s_[0:n, 1:2], s_[0:n, 0:1], AF.Sqrt, [s_], [s_], scale=1.0 / 128.0, bias=epsc[0:n, 0:1])
                S.op("dve", lambda e, o=s_[0:n, 1:2]: e.reciprocal(o, o), [s_], [s_])
                S.stt("dve", o_[0:n, :], o_[0:n, :], s_[0:n, 1:2], sgb[0:n, :], ALU.mult, ALU.mult, [o_, s_, sgb], [o_])
                S.tt("dve", bo[0:n, ti, h * 128:(h + 1) * 128], o_[0:n, :], sbg[0:n, ti, h * 128:(h + 1) * 128],
                     ALU.mult, [o_, sbg], [bo])

            for h in range(8):
                kt_, vh_ = KTh[h % 2], Vh[h % 2]
                S.load("sp", kt_[:], kv_dst[h].rearrange("(r x) c -> x r c", r=4), [d_kvdst], [kt_])
                for m_ in range(8):
                    for r in range(4):
                        S.load("sp", vh_[:, r * 8 + m_, 0:128],
                               kv_dst[8 + m_][r * 128:(r + 1) * 128, h * 128:(h + 1) * 128], [d_kvdst], [vh_])
                Osb, rcp = Osb2[h % 2], rcp2[h % 2]
                for mp in range(2):
                    rows = slice(mp * 64, mp * 64 + 64)
                    steps = []
                    for mq in range(8):
                        for r in range(4):
                            c0 = mq * 128
                            while c0 < TO:
                                n = min(512, TO - c0)
                                steps.append((mq, r, c0, n))
                                c0 += n

                    def qk(st):
                        nonlocal stc
                        mq, r, c0, n = st
                        pst = pST[stc % 3]
                        stc += 1
                        S.mm([(pst[:, 0:n], kt_[rows, r, mq * 128:(mq + 1) * 128], QT[rows, h, c0:c0 + n], True, True)],
                             [kt_, QT], [pst])
                        return pst

                    def rest(st, pst):
                        nonlocal ptc
                        mq, r, c0, n = st
                        blk = r * 8 + mq
                        pt = PT[ptc % 3]
                        ptc += 1
                        S.act(pt[:, 0:n], pst[:, 0:n], AF.Exp, [pst], [pt])
                        if c0 == mq * 128:
                            S.tt("dve", pt[:, 0:128], pt[:, 0:128], amk[:, r, :], ALU.mult, [pt, amk], [pt])
                        mms = []
                        for i in range(n // 128):
                            m = (c0 // 128) + i
                            ob = pOA[m // 3]
                            oc = (m % 3) * 129
                            mms.append((ob[:, oc:oc + 129], pt[:, i * 128:(i + 1) * 128], vh_[:, blk, :],
                                        (mq == 0 and r == 0 and m % 3 == 0), (mq == m and r == 3)))
                        S.mm(mms, [pt, vh_], pOA)

                    pend = None
                    for st in steps:
                        pst = qk(st)
                        if pend is not None:
                            rest(*pend)
                        pend = (st, pst)
                    rest(*pend)
                    for bk in range(3):
                        nm = 3 if bk < 2 else 2
                        S.copy("act", Osb[:, mp, 3 * bk:3 * bk + nm, :].rearrange("p m c -> p (m c)"),
                               pOA[bk][:, 0:nm * 129], [pOA[bk]], [Osb])
                S.op("dve", lambda e, r_=rcp[:, :, :], o_=Osb[:, :, :, 128]: e.reciprocal(r_, o_), [Osb], [rcp])
                S.ts("dve", rcp[:, 1, :], rcp[:, 1, :], lamc[:, 1:2], None, ALU.mult, None, [rcp, lamc], [rcp])
                for m in range(8):
                    def src(o_, m=m):
                        S.ts("dve", o_[:, :], Osb[:, 0, m, 0:128], rcp[:, 0, m:m + 1], None, ALU.mult, None, [Osb, rcp], [o_])
                        S.stt("dve", o_[:, :], Osb[:, 1, m, 0:128], rcp[:, 1, m:m + 1], o_[:, :], ALU.mult, ALU.add,
                              [Osb, rcp, o_], [o_])
                    epilogue(src, 128, m, h, [])
            for h in range(8):
                def src(o_, h=h):
                    S.copy("dve", o_[0:NS, :], o_samp[:, h * 128:(h + 1) * 128], [o_samp], [o_])
                epilogue(src, NS, 8, h, [])
            pTb = [PB, PB]
            for m in range(8):
                p = pTb[m % 2]
                S.tr([(p[:, h * 128:(h + 1) * 128], bo[:, m, h * 128:(h + 1) * 128], identb[:, :]) for h in range(8)],
                     [bo, identb], [p])
                S.copy("dve" if m % 2 else "act", b_outT[:, :, m * 128:(m + 1) * 128],
                       p[:, :].rearrange("p (h t) -> p h t", h=8), [p], [b_outT])
            p = pTb[0]
            S.tr([(p[:, h * NS:(h + 1) * NS], bo[0:NS, 8, h * 128:(h + 1) * 128], identb[0:NS, 0:NS]) for h in range(8)],
                 [bo, identb], [p])
            S.copy("dve", b_outT[:, :, TO:NT], p[:, 0:8 * NS].rearrange("p (h s) -> p h s", h=8), [p], [b_outT])
            S.barrier()
        st_p12.close()

        S.enabled = STOP >= 4
        def out_proj_ln(stk, layer, lhs_fn, lhs_bufs, w_dram, xres_dram, xres_dep, sink):
            Wo = [sb(f"Wo{layer}_{i}", [128, 16, 512], BF16, stk) for i in range(4)]
            wv_ = w_dram.rearrange("(k p) c -> p k c", p=128)
            for cb in range(4):
                S.load("pool", Wo[cb][:], wv_[:, :, cb * 512:(cb + 1) * 512], [], [Wo[cb]])
            gb = sb(f"gb{layer}", [128, 2, D], F32, stk)
            S.load("sp", gb[:, 0, :], ln_g[layer:layer + 1, :].partition_broadcast(128), [], [gb])
            S.load("sp", gb[:, 1, :], ln_b[layer:layer + 1, :].partition_broadcast(128), [], [gb])
            xr = [sb(f"xr{layer}_{i}", [128, D], F32, stk) for i in range(2)]
            y0 = [sb(f"y0{layer}_{i}", [128, D], F32, stk) for i in range(2)]
            st_ = [sb(f"lst{layer}_{i}", [128, 24], F32, stk) for i in range(2)]
            mv_ = [sb(f"lmv{layer}_{i}", [128, 4], F32, stk) for i in range(2)]
            for ti, (t0, n) in enumerate(TILES):
                x_, y_, s_, m_ = xr[ti % 2], y0[ti % 2], st_[ti % 2], mv_[ti % 2]
                S.load("sp", x_[0:n, :], xres_dram[t0:t0 + n, :], xres_dep, [x_])
                for cb in range(4):
                    ps = nps()
                    S.mm([(ps[0:n, :], lhs_fn(kt, t0, n), Wo[cb][:, kt, :], kt == 0, kt == 15) for kt in range(16)],
                         lhs_bufs + [Wo[cb]], [ps])
                    S.stt("dve", y_[0:n, cb * 512:(cb + 1) * 512], x_[0:n, cb * 512:(cb + 1) * 512], ALPHA, ps[0:n, :],
                          ALU.mult, ALU.add, [x_, ps], [y_])
                    S.op("dve", lambda e, o=s_[0:n, cb * 6:cb * 6 + 6], i=y_[0:n, cb * 512:(cb + 1) * 512]: e.bn_stats(o, i),
                         [y_], [s_])
                S.op("dve", lambda e, o=m_[0:n, 0:2], i=s_[0:n, 0:24]: e.bn_aggr(o, i), [s_], [m_])
                S.act(m_[0:n, 2:3], m_[0:n, 1:2], AF.Sqrt, [m_], [m_], scale=1.0, bias=epsc[0:n, 0:1])
                S.op("dve", lambda e, o=m_[0:n, 2:3]: e.reciprocal(o, o), [m_], [m_])
                S.ts("dve", y_[0:n, :], y_[0:n, :], m_[0:n, 0:1], m_[0:n, 2:3], ALU.subtract, ALU.mult, [y_, m_], [y_])
                S.tt("pool", y_[0:n, :], y_[0:n, :], gb[0:n, 0, :], ALU.mult, [y_, gb], [y_])
                S.tt("pool", y_[0:n, :], y_[0:n, :], gb[0:n, 1, :], ALU.add, [y_, gb], [y_])
                sink(ti, t0, n, y_)

        st_x1 = Grp()
        x1T = sb("x1T", [128, 16, NX], BF16, st_x1, multi=True)

        with Grp() as st4:
            x1b = [sb(f"x1b{i}", [128, D], BF16, st4) for i in range(2)]
            pTc = [PB, PB]
            pcnt = [0]

            def lhs0(kt, t0, n):
                return a_outT[:, kt, t0:t0 + n] if kt < 8 else b_outT[:, kt - 8, t0:t0 + n]

            def sink0(ti, t0, n, y_):
                S.load("sp", x1s[t0:t0 + n, :], y_[0:n, :], [y_], [d_x1s])
                xb = x1b[ti % 2]
                S.copy("act", xb[0:n, :], y_[0:n, :], [y_], [xb])
                for half in range(2):
                    p = pTc[pcnt[0] % 2]
                    pcnt[0] += 1
                    S.tr([(p[:, k * 128:k * 128 + n], xb[0:n, (half * 8 + k) * 128:(half * 8 + k + 1) * 128], identb[0:n, 0:n])
                          for k in range(8)], [xb, identb], [p])
                    S.copy("dve" if half else "act", x1T[:, half * 8:half * 8 + 8, t0:t0 + n],
                           p[:, :].rearrange("p (k t) -> p k t", k=8)[:, :, 0:n], [p], [x1T])

            out_proj_ln(st4, 0, lhs0, [a_outT, b_outT], w_out, xtok, [], sink0)
            hsv = halo_src.rearrange("(k p) (m c) -> p k m c", p=128, m=8)
            for m in range(8):
                S.load("sp", hsv[:, :, m, :], x1T[:, :, m * 128 + 113:m * 128 + 128], [x1T], [d_hsrc])
            collective(0, halo_src, halo_dst, d_hsrc, d_hdst)
            S.barrier()
        st_ab.close()

        S.enabled = STOP >= 5
        with Grp() as st5:
            st5w = Grp()
            Wc = [sb(f"Wc{i}", [128, 16, 512], BF16, st5w) for i in range(3)]
            wcc = [0]
            wcv = w_in_c.rearrange("(k p) c -> p k c", p=128)

            def loadwc(blk):
                b = Wc[wcc[0] % 3]
                wcc[0] += 1
                S.load("pool", b[:], wcv[:, :, blk * 512:(blk + 1) * 512], [], [b])
                return b
            wbufC = {}

            def getwc(pos):
                for p_ in range(pos + 3):
                    if p_ < 8 and p_ not in wbufC:
                        wbufC[p_] = loadwc(p_)
                return wbufC[pos]
            getwc(0)
            Wg = sb("Wg", [128, 16, 512], BF16, st5w)
            S.load("pool", Wg[:], w_grp.rearrange("(k p) c -> p k c", p=128), [], [Wg])
            with Grp() as st5h:
                Hb = sb("Hb", [128, 16, 4, NH], BF16, st5h)
                hdv = halo_dst.rearrange("(r k p) c -> p k r c", r=4, p=128)
                for r in range(4):
                    S.load("sp", Hb[:, :, r, :], hdv[:, :, r, :], [d_hdst], [Hb])
                selb = sb("selb", [128, 8], F32, st5h)
                S.load("sp", selb[:], sel_d.partition_broadcast(128), [], [selb])
                hA = sb("hA", [128, 16, NH], F32, st5h)
                hB = sb("hB", [128, 16, NH], F32, st5h)
                for (acc, off) in ((hA, 0), (hB, 4)):
                    S.ts("dve", acc[:], Hb[:, :, 0, :], selb[:, off:off + 1], None, ALU.mult, None, [Hb, selb], [acc])
                    for r in range(1, 4):
                        S.stt("dve", acc[:], Hb[:, :, r, :], selb[:, off + r:off + r + 1], acc[:], ALU.mult, ALU.add,
                              [Hb, selb, acc], [acc])
                S.copy("dve", x1T[:, :, NT:NT + 15], hA[:, :, 0:15], [hA], [x1T])
                S.tt("dve", x1T[:, :, NT + 15:NX], hA[:, :, 15:NH], hB[:, :, 0:NH - 15], ALU.add, [hA, hB], [x1T])
                S.barrier()

            pooledT = sb("pooledT", [128, 16, NT], BF16, st5, multi=True)
            sgT = sb("sgT", [128, 16, NT], BF16, st5, multi=True)
            with Grp() as st5a:
                cbs = sb("cbs", [128, 2, 16], F32, st5a)
                S.load("sp", cbs[:, 0, :], c_bT, [], [cbs])
                S.load("sp", cbs[:, 1, :], c_sT, [], [cbs])
                rcb = sb("rcb", [128, 4, 16], F32, st5a)
                S.load("sp", rcb[:].rearrange("p g c -> p (g c)"), rc_d.partition_broadcast(128), [], [rcb])
                hpx = [sb(f"hpx{i}", [128, 8, 143], F32, st5a) for i in range(2)]
                wnA = [sb(f"wnA{i}", [128, 8, 143], F32, st5a) for i in range(1)] * 2
                wnB = [sb(f"wnB{i}", [128, 8, 143], F32, st5a) for i in range(1)] * 2
                hps = [sb(f"hps{i}", [128, NS, 16], F32, st5a) for i in range(2)]
                wns = [sb(f"wns{i}", [128, NS], F32, st5a) for i in range(2)]
                fx = [sb(f"fx{i}", [128, 16], F32, st5a) for i in range(2)]
                stg5 = [sb(f"stg5_{i}", [128, 512], F32, st5a) for i in range(2)]
                s5 = [0]
                hTv = histT.rearrange("(k p) s c -> p k s c", p=128)

                def fm1(wb, c0, chunk, ps):
                    t0, n = chunk
                    S.mm([(ps[:, 0:n], wb[:, kt, c0:c0 + 128], x1T[:, kt, t0:t0 + n], kt == 0, kt == 15)
                          for kt in range(16)], [wb, x1T], [ps])

                def tm1(wb, tile, ps):
                    t0, n = tile
                    S.mm([(ps[0:n, :], x1T[:, kt, t0:t0 + n], wb[:, kt, :], kt == 0, kt == 15)
                          for kt in range(16)], [wb, x1T], [ps])

                S.load("sp", pools_o[:, 0:14, :], hist[:, 1:15, :], [], [])
                for b in range(4):
                    wb = getwc(b)
                    w = 2 ** (b + 1)
                    ps = nps()
                    tm1(wb, TILES[7], ps)
                    sg_ = stg5[s5[0] % 2]
                    s5[0] += 1
                    S.copy("act", sg_[:, :], ps[:, :], [ps], [sg_])
                    S.load("sp", poolp_o[:, b * 512:(b + 1) * 512], sg_[113:128, :], [sg_], [])
                    ps = nps()
                    tm1(wb, TILES[8], ps)
                    sg_ = stg5[s5[0] % 2]
                    s5[0] += 1
                    S.copy("act", sg_[0:NS, :], ps[0:NS, :], [ps], [sg_])
                    S.load("sp", pools_o[:, 14, b * 512:(b + 1) * 512], sg_[0:NS, :], [sg_], [])
                    for cl in range(4):
                        ct = 4 * b + cl
                        hx, wA, wB, hs, ws_, fx_ = hpx[ct % 2], wnA[ct % 2], wnB[ct % 2], hps[ct % 2], wns[ct % 2], fx[ct % 2]
                        for ci, ch in enumerate(CHUNKS_P):
                            ps = nps()
                            fm1(wb, cl * 128, ch, ps)
                            S.copy("act", hx[:, 4 * ci:4 * ci + 4, 15:143], ps[:, :].rearrange("p (m t) -> p m t", m=4), [ps], [hx])
                        ps = nps()
                        fm1(wb, cl * 128, (NT, NH), ps)
                        S.copy("act", hx[:, :, 0:15], ps[:, 0:NH].rearrange("p (m c) -> p m c", m=8), [ps], [hx])
                        ps = nps()
                        fm1(wb, cl * 128, (TO, NS), ps)
                        S.load("sp", hs[:, :, 0:15], hTv[:, ct, :, :], [], [hs])
                        S.copy("act", hs[:, :, 15], ps[:, 0:NS], [ps], [hs])
                        cur, lo, step = hx, 0, 1
                        bufs = [wA, wB]
                        bi_ = 0
                        while step < w:
                            nxt = bufs[bi_ % 2]
                            bi_ += 1
                            nlo = lo + step
                            S.tt("dve", nxt[:, :, nlo:143], cur[:, :, nlo:143], cur[:, :, nlo - step:143 - step], ALU.add,
                                 [cur], [nxt])
                            cur, lo, step = nxt, nlo, step * 2
                        S.stt("dve", pooledT[:, ct, 0:TO].rearrange("p (m t) -> p m t", m=8), cur[:, :, 15:143], 1.0 / w,
                              hx[:, :, 15:143], ALU.mult, ALU.subtract, [cur, hx], [pooledT])
                        S.tt("dve", fx_[:, :], cur[:, 0, 15:31], rcb[:, b, :], ALU.mult, [cur, rcb], [fx_])
                        S.tt("dve", pooledT[:, ct, 0:16], fx_[:, :], hx[:, 0, 15:31], ALU.subtract, [fx_, hx], [pooledT])
                        S.red("dve", ws_[:, :], hs[:, :, 16 - w:16], ALU.add, [hs], [ws_])
                        S.stt("dve", pooledT[:, ct, TO:NT], ws_[:, :], 1.0 / w, hs[:, :, 15], ALU.mult, ALU.subtract,
                              [ws_, hs], [pooledT])
                for b in range(4, 8):
                    wb = getwc(b)
                    for cl in range(4):
                        ct = 4 * (b - 4) + cl
                        for ch in CHUNKS_PS:
                            ps = nps()
                            fm1(wb, cl * 128, ch, ps)
                            S.act(sgT[:, ct, ch[0]:ch[0] + ch[1]], ps[:, 0:ch[1]], AF.Silu, [ps], [sgT])
                t5 = [sb(f"t5_{i}", [128, 512], F32, st5a) for i in range(2)]
                c5 = 0
                for g in range(4):
                    for el in range(4):
                        et = 4 * g + el
                        for ch in CHUNKS_PS:
                            t0, n = ch
                            ps = nps()
                            S.mm([(ps[:, 0:n], Wg[:, 4 * g + kl, el * 128:(el + 1) * 128], pooledT[:, 4 * g + kl, t0:t0 + n],
                                   kl == 0, kl == 3) for kl in range(4)], [Wg, pooledT], [ps])
                            tt_ = t5[c5 % 2]
                            c5 += 1
                            S.ts("dve", tt_[:, 0:n], ps[:, 0:n], cbs[:, 0, et:et + 1], cbs[:, 1, et:et + 1], ALU.add, ALU.mult,
                                 [ps, cbs], [tt_])
                            S.tt("pool", sgT[:, et, t0:t0 + n], tt_[:, 0:n], sgT[:, et, t0:t0 + n], ALU.mult, [tt_, sgT], [sgT])
                S.barrier()

            free_group([pooledT])
            st5w.close()
            st_x1.close()
            with Grp() as st5b:
                def lhs1(kt, t0, n):
                    return sgT[:, kt, t0:t0 + n]

                def sink1(ti, t0, n, y_):
                    S.load("sp", y_o[t0:t0 + n, :], y_[0:n, :], [y_], [])

                out_proj_ln(st5b, 1, lhs1, [sgT], w_out_c, x1s, [d_x1s], sink1)
                S.barrier()
        st_x1.close()

        S.enabled = True
        S.barrier(final=True)
        import sys as _sys
        print("ticks", S.tick, "dma", S.dcount, file=_sys.stderr)

        with nc.Block() as block:
            @block.tensor
            def _(e):
                S.replay("pe", e)

            @block.scalar
            def _(e):
                S.replay("act", e)

            @block.vector
            def _(e):
                S.replay("dve", e)

            @block.gpsimd
            def _(e):
                S.replay("pool", e)

            @block.sync
            def _(e):
                S.replay("sp", e)
    return nc


_PROG = None


def _own_tokens(qi):
    return np.concatenate([np.arange((4 * m + qi) * 128, (4 * m + qi + 1) * 128) for m in range(8)])


def kernel(x_prompt, x_sample, cache_k, cache_v, state_pool, page_table, ln_g, ln_b,
           w_in_ab, a_ln_g, a_ln_b, a_w_s, a_b_s, b_lq1, b_lk1, b_lq2, b_lk2, b_subln_g,
           w_out_ab, w_in_c, c_w_grp, c_b_grp, c_scale, w_out_c):
    global _PROG
    f = lambda a: np.ascontiguousarray(np.asarray(a))
    x_prompt, x_sample = np.asarray(x_prompt), np.asarray(x_sample)
    global NPOOL
    NPOOL = np.asarray(cache_k).shape[1]
    ck = f(cache_k).reshape(NPOOL * 128, 1024)
    cvv = f(cache_v).reshape(NPOOL * 128, 1024)
    page_table = np.asarray(page_table).astype(np.int32)
    state_pool = np.asarray(state_pool)
    a_w_s, a_b_s = np.asarray(a_w_s), np.asarray(a_b_s)

    shared = {
        "cache_k": ck, "cache_v": cvv,
        "w_in": f(np.asarray(w_in_ab)[0]), "w_out": f(np.asarray(w_out_ab)[0]),
        "w_in_c": f(np.asarray(w_in_c)[0]), "w_grp": f(np.asarray(c_w_grp)[0].reshape(2048, 512)),
        "w_out_c": f(np.asarray(w_out_c)[0]),
        "a_ln_g": f(np.asarray(a_ln_g)[0:1]), "a_ln_b": f(np.asarray(a_ln_b)[0:1]),
        "wsT": f(a_w_s[0].transpose(2, 0, 1)),
        "bs": f(a_b_s[0].reshape(1, 512)),
        "w00e": f(np.repeat(a_w_s[0, :, 0, 0], 256)[None, :]),
        "b0e": f(np.repeat(a_b_s[0, :, 0], 256)[None, :]),
        "lqk": f(np.concatenate([np.asarray(b_lq1)[0], np.asarray(b_lk1)[0],
                                 np.asarray(b_lq2)[0], np.asarray(b_lk2)[0]])[None, :]),
        "subln": f(np.asarray(b_subln_g)[0:1]),
        "ln_g": f(ln_g), "ln_b": f(ln_b),
        "c_bT": f(np.asarray(c_b_grp)[0].reshape(16, 128).T),
        "c_sT": f(np.asarray(c_scale)[0].reshape(16, 128).T),
        "ident": np.eye(128, dtype=np.float32),
        "tril": np.triu(np.ones((128, 128), np.float32)),
        "iota_i": np.broadcast_to(np.arange(128, dtype=np.int32)[:, None], (128, 256)).copy(),
    }
    pat = np.zeros((16, 16, 8, 16), np.float32)
    for hm in range(16):
        for s in range(16):
            pat[hm, s, hm // 2, s] = 1.0
    shared["pat"] = pat.reshape(16, -1)
    patoe = np.zeros((16, 2), np.float32)
    patoe[0::2, 0] = 1.0
    patoe[1::2, 1] = 1.0
    shared["patoe"] = patoe

    in_maps = []
    for c in range(NCORES):
        b, qi = c // 4, c % 4
        tok = _own_tokens(qi)
        xp = x_prompt[b][tok]
        xs = x_sample[16 * c:16 * c + 16, 0]
        xt = np.concatenate([xp, xs], 0)
        amask = np.zeros((128, 4, 128), np.float32)
        for r in range(4):
            if r < qi:
                amask[:, r, :] = 1.0
            elif r == qi:
                amask[:, r, :] = np.triu(np.ones((128, 128), np.float32))
        rc = np.zeros((4, 16), np.float32)
        for g, w in enumerate((2, 4, 8, 16)):
            for t in range(16):
                cnt = min(t + 1, w) if qi == 0 else w
                rc[g, t] = 1.0 / cnt
        sel = np.zeros((1, 8), np.float32)
        if qi > 0:
            sel[0, qi - 1] = 1.0
        else:
            sel[0, 4 + 3] = 1.0
        m = dict(shared)
        m.update({
            "xT": f(xt.T), "xtok": f(xt),
            "ptab": f(page_table[16 * c:16 * c + 16].reshape(1, 256)),
            "hist": f(state_pool[0, 16 * c:16 * c + 16]),
            "histT": f(state_pool[0, 16 * c:16 * c + 16].transpose(2, 0, 1)),
            "amask": amask, "rc": rc.reshape(1, 64), "sel": sel,
        })
        in_maps.append(m)

    if _PROG is None:
        _PROG = build_program()
    res = run_bass_kernel_spmd(_PROG, in_maps, core_ids=list(range(NCORES)))
    R = res.results

    y_prompt = np.zeros((2, 4096, 2048), np.float32)
    y_sample = np.zeros((128, 1, 2048), np.float32)
    k_p = np.zeros((1, 2, 4096, 8, 128), np.float32)
    v_p = np.zeros((1, 2, 4096, 8, 128), np.float32)
    k_s = np.zeros((1, 128, 1, 8, 128), np.float32)
    v_s = np.zeros((1, 128, 1, 8, 128), np.float32)
    cv_s = np.zeros((1, 128, 1, 1024), np.float32)
    pool_p = np.zeros((1, 2, 15, 2048), np.float32)
    pool_s = np.zeros((1, 128, 15, 2048), np.float32)
    for c in range(NCORES):
        b, qi = c // 4, c % 4
        tok = _own_tokens(qi)
        r = R[c]
        y_prompt[b, tok] = r["y"][:TO]
        y_sample[16 * c:16 * c + 16, 0] = r["y"][TO:]
        k_p[0, b, tok] = r["knew"][:TO].reshape(TO, 8, 128)
        v_p[0, b, tok] = r["vnew"][:TO].reshape(TO, 8, 128)
        k_s[0, 16 * c:16 * c + 16, 0] = r["knew"][TO:].reshape(NS, 8, 128)
        v_s[0, 16 * c:16 * c + 16, 0] = r["vnew"][TO:].reshape(NS, 8, 128)
        cv_s[0, 16 * c:16 * c + 16, 0] = r["cv"]
        if qi == 3:
            pool_p[0, b] = r["poolp"]
        pool_s[0, 16 * c:16 * c + 16] = r["pools"]
    return (y_prompt, y_sample, k_p, v_p, k_s, v_s, cv_s, pool_p, pool_s)
```

```python
import math
import numpy as np
import ml_dtypes
import concourse.bass as bass
import concourse.mybir as mybir
from concourse.bass_utils import run_bass_kernel_spmd

F32 = mybir.dt.float32
BF16 = mybir.dt.bfloat16
I32 = mybir.dt.int32
ALU = mybir.AluOpType
AF = mybir.ActivationFunctionType
AX = mybir.AxisListType

NCORES = 8
D = 2048
TO = 1024
NS = 16
NT = TO + NS
NH = 120
NX = NT + NH
ALPHA = 4 ** 0.25
LN_EPS = 1e-5
LAM_INIT = 0.8 - 0.6 * math.exp(0.0)
NEG = -1e30
CHUNKS_P = [(0, 512), (512, 512)]
CHUNKS_PS = [(0, 512), (512, 512), (1024, 16)]
TILES = [(i * 128, 128) for i in range(8)] + [(1024, 16)]


class Buf:
    def __init__(self, t, multi=False):
        self.t = t
        self.multi = multi
        self.lw = []
        self.rd = []

    def __getitem__(self, k):
        return self.t[k]


class Sched:
    ENG = ["pe", "act", "dve", "pool", "sp"]

    def __init__(self, nc, esem, dsem):
        self.nc = nc
        self.stream = {e: [] for e in self.ENG}
        self.tick = {e: 0 for e in self.ENG}
        self.seen = {e: {} for e in self.ENG}
        self.esem = esem
        self.dsem = dsem
        self.dcount = {q: 0 for q in dsem}
        self.dval = {}
        self.dlast = {}
        self.ccn = 0
        self.enabled = True

    def _need(self, eng, dep):
        if dep is None:
            return
        if dep[0] == "eng":
            _, name, tick = dep
            if name == eng and name == "pe":
                return
            key = ("eng", name)
            if self.seen[eng].get(key, 0) >= tick:
                return
            self.seen[eng][key] = tick
            self.stream[eng].append(("wait", self.esem[name], tick))
        else:
            _, key, val = dep
            if self.seen[eng].get(key, 0) >= val:
                return
            self.seen[eng][key] = val
            q, k = key
            self.stream[eng].append(("wait", self.dsem[q][k], val))

    def _deps(self, eng, reads, writes):
        best = {}
        for b in reads:
            for d in b.lw:
                k = d[:2]
                if k not in best or best[k][2] < d[2]:
                    best[k] = d
        for b in writes:
            for d in b.rd:
                k = d[:2]
                if k not in best or best[k][2] < d[2]:
                    best[k] = d
            if not b.multi:
                for d in b.lw:
                    k = d[:2]
                    if k not in best or best[k][2] < d[2]:
                        best[k] = d
        for d in best.values():
            self._need(eng, d)

    def _mark(self, dep, reads, writes):
        for b in reads:
            b.rd.append(dep)
        for b in writes:
            if b.multi:
                b.lw.append(dep)
            else:
                b.lw = [dep]
                b.rd = []

    def op(self, eng, fn, reads=(), writes=()):
        if not self.enabled:
            return None
        self._deps(eng, reads, writes)
        self.tick[eng] += 1
        dep = ("eng", eng, self.tick[eng])
        self.stream[eng].append(("ins", fn, self.esem[eng], 1))
        self._mark(dep, reads, writes)
        return dep

    def dma(self, q, fn, reads=(), writes=()):
        if not self.enabled:
            return None
        self._deps(q, reads, writes)
        k = self.dcount[q] % len(self.dsem[q])
        self.dcount[q] += 1
        key = (q, k)
        self._need(q, self.dlast.get(key))
        val = self.dval.get(key, 0) + 16
        self.dval[key] = val
        dep = ("dma", key, val)
        self.dlast[key] = dep
        self.stream[q].append(("ins", fn, self.dsem[q][k], 16))
        self._mark(dep, reads, writes)
        return dep

    def barrier(self, final=False):
        if not self.enabled:
            return
        deps = [("eng", e, self.tick[e]) for e in ("pe", "act", "dve", "pool") if self.tick[e] > 0]
        deps += [d for k, d in self.dlast.items() if final or k[0] != "cc"]
        for e in self.ENG:
            for d in deps:
                if d[0] == "eng" and d[1] == e:
                    continue
                self._need(e, d)

    def replay(self, eng, e):
        for item in self.stream[eng]:
            if item[0] == "wait":
                e.wait_ge(item[1], item[2])
            else:
                ins = item[1](e)
                ins.then_inc(item[2], item[3])

    def mm(self, mms, reads, writes):
        mms = list(mms)

        def fn(e):
            ins = None
            for (o, l, r, st, sp) in mms:
                ins = e.matmul(o, l, r, start=st, stop=sp)
            return ins
        return self.op("pe", fn, reads, writes)

    def tr(self, trs, reads, writes):
        trs = list(trs)

        def fn(e):
            ins = None
            for (o, i, idn) in trs:
                ins = e.transpose(o, i, idn)
            return ins
        return self.op("pe", fn, reads, writes)

    def act(self, out, in_, func, reads, writes, scale=None, bias=None, accum=None):
        kw = {}
        if scale is not None:
            kw["scale"] = scale
        if bias is not None:
            kw["bias"] = bias
        if accum is not None:
            kw["accum_out"] = accum
        return self.op("act", lambda e: e.activation(out, in_, func, **kw), reads, writes)

    def copy(self, eng, out, in_, reads, writes):
        if eng == "act":
            return self.op("act", lambda e: e.copy(out, in_), reads, writes)
        return self.op(eng, lambda e: e.tensor_copy(out, in_), reads, writes)

    def tt(self, eng, out, a, b, op, reads, writes):
        return self.op(eng, lambda e: e.tensor_tensor(out, a, b, op), reads, writes)

    def ts(self, eng, out, a, s1, s2, op0, op1, reads, writes):
        if op1 is None:
            return self.op(eng, lambda e: e.tensor_scalar(out, a, s1, None, op0), reads, writes)
        return self.op(eng, lambda e: e.tensor_scalar(out, a, s1, s2, op0, op1), reads, writes)

    def stt(self, eng, out, a, s, b, op0, op1, reads, writes):
        return self.op(eng, lambda e: e.scalar_tensor_tensor(out, a, s, b, op0, op1), reads, writes)

    def red(self, eng, out, in_, op, reads, writes, axis=AX.X):
        return self.op(eng, lambda e: e.tensor_reduce(out, in_, axis, op), reads, writes)

    def memset(self, eng, ap, val, writes):
        return self.op(eng, lambda e: e.memset(ap, val), (), writes)

    def load(self, q, out, in_, reads, writes):
        return self.dma(q, lambda e: e.dma_start(out=out, in_=in_), reads, writes)


STOP = 99
NPOOL = 2560


def build_program():
    nc = bass.Bass("TRN2", target_bir_lowering=False)

    def din(name, shape, dt=F32):
        return nc.dram_tensor(name, list(shape), dt, kind="ExternalInput").ap()

    def dout(name, shape, dt=F32):
        return nc.dram_tensor(name, list(shape), dt, kind="ExternalOutput").ap()

    def dint(name, shape, dt):
        return nc.dram_tensor(name, list(shape), dt, kind="Internal").ap()

    xT = din("xT", [D, NT])
    xtok = din("xtok", [NT, D])
    cache_k = din("cache_k", [NPOOL * 128, 1024])
    cache_v = din("cache_v", [NPOOL * 128, 1024])
    ptab = din("ptab", [1, 256], I32)
    iota_i = din("iota_i", [128, 256], I32)
    hist = din("hist", [NS, 15, D])
    histT = din("histT", [D, NS, 15])
    w_in = din("w_in", [D, 7168])
    w_out = din("w_out", [D, D])
    w_in_c = din("w_in_c", [D, 4096])
    w_grp = din("w_grp", [D, 512])
    w_out_c = din("w_out_c", [D, D])
    a_ln_g = din("a_ln_g", [1, 1024])
    a_ln_b = din("a_ln_b", [1, 1024])
    wsT = din("wsT", [128, 4, 128])
    bs = din("bs", [1, 512])
    w00e = din("w00e", [1, 1024])
    b0e = din("b0e", [1, 1024])
    lqk = din("lqk", [1, 256])
    subln = din("subln", [1, 128])
    ln_g = din("ln_g", [2, D])
    ln_b = din("ln_b", [2, D])
    c_bT = din("c_bT", [128, 16])
    c_sT = din("c_sT", [128, 16])
    ident_d = din("ident", [128, 128])
    tril_d = din("tril", [128, 128])
    amask_d = din("amask", [128, 4, 128])
    rc_d = din("rc", [1, 64])
    sel_d = din("sel", [1, 8])
    pat_d = din("pat", [16, 16 * 8 * 16])
    patoe_d = din("patoe", [16, 2])

    y_o = dout("y", [NT, D])
    knew_o = dout("knew", [NT, 1024])
    vnew_o = dout("vnew", [NT, 1024])
    cv_o = dout("cv", [NS, 1024])
    poolp_o = dout("poolp", [15, D])
    pools_o = dout("pools", [NS, 15, D])

    kv_src = dint("kv_src", [2048, 1024], BF16)
    kv_dst = [dint(f"kv_dst{i}", [4 * 128, 1024], BF16) for i in range(16)]
    x1s = dint("x1s", [NT, D], F32)
    halo_src = dint("halo_src", [D, NH], BF16)
    halo_dst = dint("halo_dst", [4 * D, NH], BF16)
    sqkv = [dint(f"sqkv{i}", [NS, 1024], F32) for i in range(3)]

    from contextlib import ExitStack
    es = ExitStack()
    with es:
        esem = {e: es.enter_context(nc.semaphore("es_" + e)) for e in ("pe", "act", "dve", "pool")}
        dsem = {"sp": [es.enter_context(nc.semaphore(f"dsp{i}")) for i in range(24)],
                "pool": [es.enter_context(nc.semaphore(f"dpl{i}")) for i in range(16)]}
        ccsem = [es.enter_context(nc.semaphore(f"cc{i}")) for i in range(17)]
        S = Sched(nc, esem, dsem)

        free_list = [[16512, 229344]]
        nalloc = [0]

        def sb(name, shape, dt, grp, multi=False):
            esz = 2 if dt == BF16 else 4
            nb = esz
            for d_ in shape[1:]:
                nb *= d_
            nb = (nb + 63) // 64 * 64
            for fr in free_list:
                if fr[1] - fr[0] >= nb:
                    off = fr[0]
                    fr[0] += nb
                    break
            else:
                raise RuntimeError(f"SBUF arena full allocating {name} {shape} ({nb} B); free={free_list}")
            nalloc[0] += 1
            b = Buf(nc.alloc_sbuf_tensor_at(f"{name}_{nalloc[0]}", list(shape), dt, offset=off), multi)
            b.rng = (off, off + nb)
            grp.append(b)
            return b

        def free_group(grp):
            for b in grp:
                free_list.append(list(b.rng))
            del grp[:]
            free_list.sort()
            i = 0
            while i + 1 < len(free_list):
                if free_list[i][1] >= free_list[i + 1][0]:
                    free_list[i][1] = max(free_list[i][1], free_list[i + 1][1])
                    del free_list[i + 1]
                else:
                    i += 1
            free_list[:] = [f_ for f_ in free_list if f_[1] > f_[0]]

        class Grp(list):
            def __enter__(self):
                return self

            def __exit__(self, *a):
                free_group(self)
                return False

            def close(self):
                free_group(self)

        def psb(name, stack, dt=F32, n=512):
            return Buf(stack.enter_context(nc.psum_tensor(name, [128, n], dt)))

        class DB(Buf):
            def __init__(self):
                Buf.__init__(self, None, True)
        d_kvsrc, d_kvdst, d_x1s, d_hsrc, d_hdst, d_sqkv = DB(), DB(), DB(), DB(), DB(), DB()

        def collective(idx, src, dst, bsrc, bdst):
            if not S.enabled:
                return
            S._deps("pool", [bsrc], [bdst])
            sem = ccsem[idx]

            def fn(e):
                return e.collective_compute("AllGather", ALU.bypass,
                                            replica_groups=[[0, 1, 2, 3], [4, 5, 6, 7]],
                                            ins=[src], outs=[dst])
            S.stream["pool"].append(("ins", fn, sem, 1))
            S.dsem.setdefault("cc", ccsem)
            dep = ("dma", ("cc", idx), 1)
            S.dlast[("cc", idx)] = dep
            S._mark(dep, [bsrc], [bdst])

        pers = Grp()
        ident = sb("ident", [128, 128], F32, pers)
        identb = sb("identb", [128, 128], BF16, pers)
        lamc = sb("lamc", [128, 4], F32, pers)
        epsc = sb("epsc", [128, 1], F32, pers)
        S.memset("dve", epsc[:], LN_EPS, [epsc])
        S.load("sp", ident[:], ident_d, [], [ident])
        S.copy("dve", identb[:], ident[:], [ident], [identb])
        with Grp() as st0:
            lq = sb("lq", [128, 256], F32, st0)
            lt = sb("lt", [128, 128], F32, st0)
            ls = sb("ls", [128, 2], F32, st0)
            le = sb("le", [128, 2], F32, st0)
            S.load("sp", lq[:], lqk.partition_broadcast(128), [], [lq])
            lqv = lq[:].rearrange("p (a b c) -> p a b c", a=2, b=2)
            S.tt("dve", lt[:].rearrange("p (a c) -> p a c", a=2), lqv[:, :, 0, :], lqv[:, :, 1, :], ALU.mult, [lq], [lt])
            S.red("dve", ls[:], lt[:].rearrange("p (a c) -> p a c", a=2), ALU.add, [lt], [ls])
            S.act(le[:], ls[:], AF.Exp, [ls], [le])
            S.tt("dve", lamc[:, 2:3], le[:, 0:1], le[:, 1:2], ALU.subtract, [le], [lamc])
            S.ts("dve", lamc[:, 0:1], lamc[:, 2:3], LAM_INIT, None, ALU.add, None, [lamc], [lamc])
            S.ts("dve", lamc[:, 1:2], lamc[:, 0:1], -1.0, None, ALU.mult, None, [lamc], [lamc])
            S.barrier()

        ps_stack = ExitStack()
        es.enter_context(ps_stack)
        PS = [psb(f"ps{i}", ps_stack) for i in range(6)]
        PB = psb("psB", ps_stack, BF16, 1024)
        PM = psb("psM", ps_stack)
        psi = [0]

        def nps():
            b = PS[psi[0] % 6]
            psi[0] += 1
            return b

        st_ab = Grp()
        a_outT = sb("a_outT", [128, 8, NT], BF16, st_ab, multi=True)

        st_p12 = Grp()
        QT = sb("QT", [128, 8, TO], BF16, st_p12, multi=True)
        sbg = sb("sbg", [128, 9, 1024], BF16, st_p12, multi=True)
        ktok_s = sb("ktok_s", [NS, 1024], F32, st_p12, multi=True)
        vtok_s = sb("vtok_s", [NS, 1024], F32, st_p12, multi=True)
        qtok_s = sb("qtok_s", [NS, 1024], F32, st_p12, multi=True)

        S.enabled = STOP >= 0.1
        with Grp() as st1:
            xTb = [sb(f"xTb{i}", [128, 4, NT], BF16, st1) for i in range(4)]
            xTv = xT.rearrange("(k p) t -> p k t", p=128)
            for i in range(4):
                S.load("pool", xTb[i][:], xTv[:, 4 * i:4 * i + 4, :], [], [xTb[i]])
            W = [sb(f"W{i}", [128, 16, 512], BF16, st1) for i in range(3)]
            wcnt = [0]
            w_inv = w_in.rearrange("(k p) c -> p k c", p=128)

            def loadw(blk):
                b = W[wcnt[0] % len(W)]
                wcnt[0] += 1
                S.load("pool", b[:], w_inv[:, :, blk * 512:(blk + 1) * 512], [], [b])
                return b

            def fm(wb, c0, chunk, ps):
                t0, n = chunk
                S.mm([(ps[:, 0:n], wb[:, kt, c0:c0 + 128], xTb[kt // 4][:, kt % 4, t0:t0 + n],
                       kt == 0, kt == 15) for kt in range(16)], [wb] + xTb, [ps])

            def tm(wb, tile, ps):
                t0, n = tile
                S.mm([(ps[0:n, :], xTb[kt // 4][:, kt % 4, t0:t0 + n], wb[:, kt, :],
                       kt == 0, kt == 15) for kt in range(16)], [wb] + xTb, [ps])

            stg = [sb(f"stg{i}", [128, 512], F32, st1) for i in range(3)]
            stgb = [sb(f"stgb{i}", [128, 1024], BF16, st1) for i in range(3)]
            sc = [0, 0]

            def nstg():
                b = stg[sc[0] % 3]
                sc[0] += 1
                return b

            def nstgb():
                b = stgb[sc[1] % 3]
                sc[1] += 1
                return b

            kvs_k = kv_src[0:1024, :].rearrange("(h p) t -> p h t", p=128)
            seqA = [8, 9, 10, 11, 6, 7, 12, 13]
            wbufA = {}

            def getw(pos):
                for p_ in range(pos + 3):
                    if p_ < len(seqA) and p_ not in wbufA:
                        wbufA[p_] = loadw(seqA[p_])
                return wbufA[pos]
            getw(0)
            for bi in range(2):
                wb = getw(bi)
                for hl in range(4):
                    h = 4 * bi + hl
                    kb = nstgb()
                    for ci, ch in enumerate(CHUNKS_P):
                        ps = nps()
                        fm(wb, hl * 128, ch, ps)
                        S.copy("act", kb[:, ch[0]:ch[0] + 512], ps[:, 0:512], [ps], [kb])
                    S.load("sp", kvs_k[:, h, :], kb[:, :], [kb], [d_kvsrc])
                for ti, tile in enumerate(TILES):
                    ps = nps()
                    tm(wb, tile, ps)
                    n = tile[1]
                    if ti < 8:
                        sg_ = nstg()
                        S.copy("dve", sg_[0:n, :], ps[0:n, :], [ps], [sg_])
                        S.load("sp", knew_o[tile[0]:tile[0] + n, bi * 512:(bi + 1) * 512], sg_[0:n, :], [sg_], [])
                    else:
                        S.copy("dve", ktok_s[:, bi * 512:(bi + 1) * 512], ps[0:n, :], [ps], [ktok_s])
            S.enabled = STOP >= 0.3
            for bi in range(2):
                wb = getw(2 + bi)
                for ti, tile in enumerate(TILES):
                    ps = nps()
                    tm(wb, tile, ps)
                    n = tile[1]
                    if ti < 8:
                        sg_ = nstg()
                        S.copy("dve", sg_[0:n, :], ps[0:n, :], [ps], [sg_])
                        S.load("sp", vnew_o[tile[0]:tile[0] + n, bi * 512:(bi + 1) * 512], sg_[0:n, :], [sg_], [])
                        vb = nstgb()
                        S.copy("act", vb[:, 0:512], sg_[:, :], [sg_], [vb])
                        S.load("sp", kv_src[1024 + tile[0]:1024 + tile[0] + 128, bi * 512:(bi + 1) * 512],
                               vb[:, 0:512], [vb], [d_kvsrc])
                    else:
                        S.copy("dve", vtok_s[:, bi * 512:(bi + 1) * 512], ps[0:n, :], [ps], [vtok_s])
            S.load("sp", knew_o[TO:NT, :], ktok_s[:, :], [ktok_s], [])
            S.load("sp", vnew_o[TO:NT, :], vtok_s[:, :], [vtok_s], [])
            S.load("sp", sqkv[1], ktok_s[:, :], [ktok_s], [d_sqkv])
            S.load("sp", sqkv[2], vtok_s[:, :], [vtok_s], [d_sqkv])
            S.enabled = STOP >= 0.4
            for ci_ in range(16):
                collective(1 + ci_, kv_src[ci_ * 128:(ci_ + 1) * 128, :], kv_dst[ci_], d_kvsrc, d_kvdst)

            S.enabled = STOP >= 0.5
            for bi in range(2):
                wb = getw(4 + bi)
                for hl in range(4):
                    h = 4 * bi + hl
                    for ch in CHUNKS_P:
                        ps = nps()
                        fm(wb, hl * 128, ch, ps)
                        S.act(QT[:, h, ch[0]:ch[0] + 512], ps[:, 0:512], AF.Copy, [ps], [QT], scale=0.125)
                ps = nps()
                tm(wb, TILES[8], ps)
                S.act(qtok_s[:, bi * 512:(bi + 1) * 512], ps[0:NS, :], AF.Copy, [ps], [qtok_s], scale=0.125)
            S.load("sp", sqkv[0], qtok_s[:, :], [qtok_s], [d_sqkv])

            for bi in range(2):
                wb = getw(6 + bi)
                for ti, tile in enumerate(TILES):
                    ps = nps()
                    tm(wb, tile, ps)
                    n = tile[1]
                    S.act(sbg[0:n, ti, bi * 512:(bi + 1) * 512], ps[0:n, :], AF.Silu, [ps], [sbg])

            S.enabled = STOP >= 0.7
            S.barrier()
            free_group([W.pop(2)] + stg + stgb)
            vn = sb("vn", [128, 8, 1024], BF16, st1, multi=True)
            cvs = sb("cvs", [NS, 1024], F32, st1)
            algb = sb("algb", [128, 2, 1024], F32, st1)
            S.load("sp", algb[:, 0, :], a_ln_g.partition_broadcast(128), [], [algb])
            S.load("sp", algb[:, 1, :], a_ln_b.partition_broadcast(128), [], [algb])
            wa = [loadw(2), loadw(3)]
            lnst = [sb(f"lnst{i}", [128, 16], F32, st1) for i in range(2)]
            lnmv = [sb(f"lnmv{i}", [128, 4], F32, st1) for i in range(2)]
            lnt = [sb(f"lnt{i}", [128, 1024], F32, st1) for i in range(2)]
            for ti, tile in enumerate(TILES):
                n = tile[1]
                pA, pB = nps(), nps()
                tm(wa[0], tile, pA)
                tm(wa[1], tile, pB)
                stt_, mv, t32 = lnst[ti % 2], lnmv[ti % 2], lnt[ti % 2]
                S.op("dve", lambda e, o=stt_[0:n, 0:6], i=pA[0:n, :]: e.bn_stats(o, i), [pA], [stt_])
                S.op("dve", lambda e, o=stt_[0:n, 6:12], i=pB[0:n, :]: e.bn_stats(o, i), [pB], [stt_])
                S.op("dve", lambda e, o=mv[0:n, 0:2], i=stt_[0:n, 0:12]: e.bn_aggr(o, i), [stt_], [mv])
                S.act(mv[0:n, 2:3], mv[0:n, 1:2], AF.Sqrt, [mv], [mv], scale=1.0, bias=epsc[0:n, 0:1])
                S.op("dve", lambda e, o=mv[0:n, 2:3]: e.reciprocal(o, o), [mv], [mv])
                S.ts("dve", t32[0:n, 0:512], pA[0:n, :], mv[0:n, 0:1], mv[0:n, 2:3], ALU.subtract, ALU.mult, [pA, mv], [t32])
                S.ts("dve", t32[0:n, 512:1024], pB[0:n, :], mv[0:n, 0:1], mv[0:n, 2:3], ALU.subtract, ALU.mult, [pB, mv], [t32])
                S.tt("pool", t32[0:n, :], t32[0:n, :], algb[0:n, 0, :], ALU.mult, [t32, algb], [t32])
                if ti < 8:
                    S.tt("pool", vn[:, ti, :], t32[:, :], algb[:, 1, :], ALU.add, [t32, algb], [vn])
                else:
                    S.tt("pool", cvs[:, :], t32[0:n, :], algb[0:n, 1, :], ALU.add, [t32, algb], [cvs])
            S.load("sp", cv_o[:, :], cvs[:, :], [cvs], [])

            S.enabled = STOP >= 0.9
            wsm = sb("wsm", [128, 4, 128], F32, st1)
            wsb = sb("wsb", [128, 4, 128], BF16, st1)
            trl = sb("trl", [128, 128], F32, st1)
            bsb = sb("bsb", [128, 4, 4, 128], F32, st1)
            S.load("sp", wsm[:], wsT, [], [wsm])
            S.load("sp", trl[:], tril_d, [], [trl])
            for r in range(4):
                S.load("sp", bsb[:, :, r, :], bs.rearrange("o (g i) -> o g i", g=4).partition_broadcast(128), [], [bsb])
            for g in range(4):
                S.tt("dve", wsb[:, g, :], wsm[:, g, :], trl[:, :], ALU.mult, [wsm, trl], [wsb])
            w0b = sb("w0b", [NS, 2, 1024], F32, st1)
            S.load("sp", w0b[:, 0, :], w00e.partition_broadcast(NS), [], [w0b])
            S.load("sp", w0b[:, 1, :], b0e.partition_broadcast(NS), [], [w0b])
            ms = sb("ms", [NS, 1024], F32, st1)
            S.tt("dve", ms[:, :], cvs[:, :], w0b[:, 0, :], ALU.mult, [cvs, w0b], [ms])
            S.tt("dve", ms[:, :], ms[:, :], w0b[:, 1, :], ALU.add, [ms, w0b], [ms])
            aos = sb("aos", [NS, 1024], BF16, st1, multi=True)
            sgt = [sb(f"sgt{i}", [128, 512], BF16, st1) for i in range(2)]
            t1 = [sb(f"t1_{i}", [128, 512], F32, st1) for i in range(2)]
            cnt = 0
            for bi in range(2):
                wu = loadw(bi)
                wg = loadw(4 + bi)
                for fl in range(4):
                    ft = 4 * bi + fl
                    g = ft // 2
                    for ci, ch in enumerate(CHUNKS_P):
                        pU, pG, pM = nps(), nps(), nps()
                        fm(wu, fl * 128, ch, pU)
                        fm(wg, fl * 128, ch, pG)
                        S.mm([(pM[:, i * 128:(i + 1) * 128], vn[:, 4 * ci + i, ft * 128:(ft + 1) * 128],
                               wsb[:, g, :], True, True) for i in range(4)], [vn, wsb], [pM])
                        sg_, tt_ = sgt[cnt % 2], t1[cnt % 2]
                        cnt += 1
                        S.act(sg_[:, :], pG[:, :], AF.Silu, [pG], [sg_])
                        S.tt("dve", tt_[:, :], pM[:, :], bsb[:, g, :, :].rearrange("p r i -> p (r i)"), ALU.add, [pM, bsb], [tt_])
                        S.tt("dve", tt_[:, :], tt_[:, :], pU[:, :], ALU.mult, [tt_, pU], [tt_])
                        S.tt("pool", a_outT[:, ft, ch[0]:ch[0] + 512], tt_[:, :], sg_[:, :], ALU.mult, [tt_, sg_], [a_outT])
                pU, pG = nps(), nps()
                tm(wu, TILES[8], pU)
                tm(wg, TILES[8], pG)
                sg_, tt_ = sgt[cnt % 2], t1[cnt % 2]
                cnt += 1
                S.act(sg_[0:NS, :], pG[0:NS, :], AF.Silu, [pG], [sg_])
                S.tt("dve", tt_[0:NS, :], pU[0:NS, :], ms[:, bi * 512:(bi + 1) * 512], ALU.mult, [pU, ms], [tt_])
                S.tt("dve", aos[:, bi * 512:(bi + 1) * 512], tt_[0:NS, :], sg_[0:NS, :], ALU.mult, [tt_, sg_], [aos])
            pT = PB
            S.tr([(pT[:, ft * NS:(ft + 1) * NS], aos[:, ft * 128:(ft + 1) * 128], identb[0:NS, 0:NS]) for ft in range(8)],
                 [aos, identb], [pT])
            S.copy("dve", a_outT[:, :, TO:NT], pT[:, 0:8 * NS].rearrange("p (f s) -> p f s", f=8), [pT], [a_outT])
            S.barrier()

        S.enabled = STOP >= 2
        o_samp = sb("o_samp", [NS, 1024], F32, st_p12)
        with Grp() as st2:
            ptb = sb("ptb", [128, 256], I32, st2)
            iob = sb("iob", [128, 256], I32, st2)
            idx = sb("idx", [128, 256], I32, st2)
            S.load("sp", ptb[:], ptab.partition_broadcast(128), [], [ptb])
            S.load("sp", iob[:], iota_i, [], [iob])
            S.op("dve", lambda e: e.tensor_single_scalar(idx[:], ptb[:], 7, ALU.logical_shift_left), [ptb], [idx])
            S.tt("dve", idx[:], idx[:], iob[:], ALU.bitwise_or, [idx, iob], [idx])
            pat = sb("pat", [16, 16, 8, 16], F32, st2)
            patoe = sb("patoe", [16, 2], F32, st2)
            lamv = sb("lamv", [16, 2], F32, st2)
            ones = sb("ones", [128, 1], F32, st2)
            S.load("sp", pat[:].rearrange("p a b c -> p (a b c)"), pat_d, [], [pat])
            S.load("sp", patoe[:], patoe_d, [], [patoe])
            S.memset("dve", ones[:], 1.0, [ones])
            S.ts("dve", lamv[:, 1:2], patoe[:, 1:2], lamc[0:16, 0:1], None, ALU.mult, None, [patoe, lamc], [lamv])
            S.tt("dve", lamv[:, 0:1], patoe[:, 0:1], lamv[:, 1:2], ALU.subtract, [patoe, lamv], [lamv])
            KB = [sb(f"KB{i}", [128, 1024], F32, st2) for i in range(5)]
            VB = [sb(f"VB{i}", [128, 1024], F32, st2) for i in range(5)]
            tmpb = [sb(f"tmpb{i}", [128, 1024], F32, st2) for i in range(2)]
            qbc = [sb(f"qbc{i}", [128, 1024], F32, st2) for i in range(2)]
            ksf = [sb(f"ksf{i}", [1, 1024], F32, st2) for i in range(2)]
            vsf = [sb(f"vsf{i}", [1, 1024], F32, st2) for i in range(2)]
            Sb = [sb(f"Sb{i}", [128, 16, 17], F32, st2) for i in range(2)]
            Eb = [sb(f"Eb{i}", [128, 16, 17], F32, st2) for i in range(2)]
            esum = [sb(f"esum{i}", [128, 16], F32, st2) for i in range(2)]
            rinv = [sb(f"rinv{i}", [16, 2], F32, st2) for i in range(2)]
            csel = [sb(f"csel{i}", [16, 8, 16], F32, st2) for i in range(2)]
            ofs = [sb(f"ofs{i}", [16, 1024], F32, st2) for i in range(2)]
            for i in range(2):
                S.memset("dve", Sb[i][:, :, 16:17], NEG, [Sb[i]])
            pO = [PS[0], PS[1]]
            pF = [[PS[2], PS[3]], [PS[4], PS[5]]]
            pL = [PM, PM]
            kc = 0
            vc = 0

            def k_page(s, j):
                nonlocal kc
                qb, Sx = qbc[s % 2], Sb[s % 2]
                kb = KB[kc % len(KB)]
                tb = tmpb[kc % 2]
                kc += 1
                col = s * 16 + j
                S.dma("pool", lambda e, o=kb[:, :], c=col: e.indirect_dma_start(
                    out=o, out_offset=None, in_=cache_k,
                    in_offset=bass.IndirectOffsetOnAxis(ap=idx[:, c:c + 1], axis=0)), [idx], [kb])
                S.tt("dve", tb[:, :], kb[:, :], qb[:, :], ALU.mult, [kb, qb], [tb])
                S.red("dve", Sx[:, :, j], tb[:, :].rearrange("p (a d) -> p a d", d=64), ALU.add, [tb], [Sx])

            def k_finish(s):
                nonlocal kc
                qb, kf, Sx, Ex = qbc[s % 2], ksf[s % 2], Sb[s % 2], Eb[s % 2]
                tb = tmpb[kc % 2]
                kc += 1
                S.tt("dve", tb[0:1, :], kf[0:1, :], qb[0:1, :], ALU.mult, [kf, qb], [tb])
                S.red("dve", Sx[0:1, :, 16], tb[0:1, :].rearrange("p (a d) -> p a d", d=64), ALU.add, [tb], [Sx])
                S.act(Ex[:], Sx[:], AF.Exp, [Sx], [Ex])
                es_ = esum[s % 2]
                S.red("dve", es_[:, :], Ex[:], ALU.add, [Ex], [es_])
                pl = pL[s % 2]
                S.mm([(pl[0:16, 0:1], es_[:, :], ones[:, 0:1], True, True)], [es_, ones], [pl])
                ri = rinv[s % 2]
                S.op("dve", lambda e, o=ri[:, 0:1], i=pl[0:16, 0:1]: e.reciprocal(o, i), [pl], [ri])
                S.tt("dve", ri[:, 1:2], ri[:, 0:1], lamv[:, 0:1], ALU.mult, [ri, lamv], [ri])
                S.ts("dve", csel[s % 2][:], pat[:, s, :, :], ri[:, 1:2], None, ALU.mult, None, [pat, ri], [csel[s % 2]])

            def v_page(s, j):
                nonlocal vc
                Ex, pf = Eb[s % 2], pF[s % 2]
                vb = VB[vc % len(VB)]
                vc += 1
                col = s * 16 + j
                S.dma("pool", lambda e, o=vb[:, :], c=col: e.indirect_dma_start(
                    out=o, out_offset=None, in_=cache_v,
                    in_offset=bass.IndirectOffsetOnAxis(ap=idx[:, c:c + 1], axis=0)), [idx], [vb])
                S.mm([(pf[hf][0:16, :], Ex[:, :, j], vb[:, hf * 512:(hf + 1) * 512], j == 0, False)
                      for hf in range(2)], [Ex, vb], pf)

            def v_finish(s):
                Ex, pf, vf = Eb[s % 2], pF[s % 2], vsf[s % 2]
                S.mm([(pf[hf][0:16, :], Ex[0:1, :, 16], vf[0:1, hf * 512:(hf + 1) * 512], False, True)
                      for hf in range(2)], [Ex, vf], pf)
                cs, of_ = csel[s % 2], ofs[s % 2]
                for hf in range(2):
                    S.copy("act", of_[:, hf * 512:(hf + 1) * 512], pf[hf][0:16, :], [pf[hf]], [of_])
                S.mm([(pO[h // 4][0:16, (h % 4) * 128:(h % 4 + 1) * 128], cs[:, h, :], of_[:, h * 128:(h + 1) * 128],
                       s == 0 and h % 4 == 0, s == NS - 1) for h in range(8)], [cs, of_], pO)

            for s in range(NS + 1):
                if s < NS:
                    S.load("sp", qbc[s % 2][:], sqkv[0][s:s + 1, :].partition_broadcast(128), [d_sqkv], [qbc[s % 2]])
                    S.load("sp", ksf[s % 2][:], sqkv[1][s:s + 1, :], [d_sqkv], [ksf[s % 2]])
                for j in range(16):
                    if s < NS:
                        k_page(s, j)
                    if s >= 1:
                        v_page(s - 1, j)
                if s >= 1:
                    v_finish(s - 1)
                if s < NS:
                    S.load("sp", vsf[s % 2][:], sqkv[2][s:s + 1, :], [d_sqkv], [vsf[s % 2]])
                    k_finish(s)
            for hf in range(2):
                S.copy("act", o_samp[:, hf * 512:(hf + 1) * 512], pO[hf][0:16, :], [pO[hf]], [o_samp])
            S.barrier()

        S.enabled = STOP >= 3
        b_outT = sb("b_outT", [128, 8, NT], BF16, st_ab, multi=True)
        with Grp() as st3:
            KTh = [sb(f"KTh{i}", [128, 4, 1024], BF16, st3) for i in range(2)]
            Vh = [sb(f"Vh{i}", [128, 32, 129], BF16, st3) for i in range(2)]
            for i in range(2):
                S.memset("dve", Vh[i][:, :, 128:129], 1.0, [Vh[i]])
            amk = sb("amk", [128, 4, 128], F32, st3)
            S.load("sp", amk[:], amask_d, [], [amk])
            sgb = sb("sgb", [128, 128], F32, st3)
            S.load("sp", sgb[:], subln.partition_broadcast(128), [], [sgb])
            S.ts("dve", sgb[:], sgb[:], 1.0 - LAM_INIT, None, ALU.mult, None, [sgb], [sgb])
            PT = [sb(f"PT{i}", [128, 512], BF16, st3) for i in range(4)]
            Osb2 = [sb(f"Osb{i}", [128, 2, 8, 129], F32, st3, multi=True) for i in range(2)]
            rcp2 = [sb(f"rcp{i}", [128, 2, 8], F32, st3) for i in range(2)]
            bo = sb("bo", [128, 9, 1024], BF16, st3, multi=True)
            od = [sb(f"od{i}", [128, 128], F32, st3) for i in range(2)]
            sq_ = [sb(f"sq{i}", [128, 128], F32, st3) for i in range(2)]
            ssb = [sb(f"ssb{i}", [128, 2], F32, st3) for i in range(2)]
            pST = [PS[0], PS[1], PS[2]]
            pOA = [PS[3], PS[4], PS[5]]
            stc = 0
            ptc = 0
            ec = 0

            def epilogue(src_ap_fn, n, ti, h, reads):
                nonlocal ec
                o_, q_, s_ = od[ec % 2], sq_[ec % 2], ssb[ec % 2]
                ec += 1
                src_ap_fn(o_)
                S.act(q_[0:n, :], o_[0:n, :], AF.Square, [o_], [q_, s_], accum=s_[0:n, 0:1])
                S.act(s_[0:n, 1:2], s_[0:n, 0:1], AF.Sqrt, [s_], [s_], scale=1.0 / 128.0, bias=epsc[0:n, 0:1])
                S.op("dve", lambda e, o=s_[0:n, 1:2]: e.reciprocal(o, o), [s_], [s_])
                S.stt("dve", o_[0:n, :], o_[0:n, :], s_[0:n, 1:2], sgb[0:n, :], ALU.mult, ALU.mult, [o_, s_, sgb], [o_])
                S.tt("dve", bo[0:n, ti, h * 128:(h + 1) * 128], o_[0:n, :], sbg[0:n, ti, h * 128:(h + 1) * 128],
                     ALU.mult, [o_, sbg], [bo])

            for h in range(8):
                kt_, vh_ = KTh[h % 2], Vh[h % 2]
                S.load("sp", kt_[:], kv_dst[h].rearrange("(r x) c -> x r c", r=4), [d_kvdst], [kt_])
                for m_ in range(8):
                    for r in range(4):
                        S.load("sp", vh_[:, r * 8 + m_, 0:128],
                               kv_dst[8 + m_][r * 128:(r + 1) * 128, h * 128:(h + 1) * 128], [d_kvdst], [vh_])
                Osb, rcp = Osb2[h % 2], rcp2[h % 2]
                for mp in range(2):
                    rows = slice(mp * 64, mp * 64 + 64)
                    steps = []
                    for mq in range(8):
                        for r in range(4):
                            c0 = mq * 128
                            while c0 < TO:
                                n = min(512, TO - c0)
                                steps.append((mq, r, c0, n))
                                c0 += n

                    def qk(st):
                        nonlocal stc
                        mq, r, c0, n = st
                        pst = pST[stc % 3]
                        stc += 1
                        S.mm([(pst[:, 0:n], kt_[rows, r, mq * 128:(mq + 1) * 128], QT[rows, h, c0:c0 + n], True, True)],
                             [kt_, QT], [pst])
                        return pst

                    def rest(st, pst):
                        nonlocal ptc
                        mq, r, c0, n = st
                        blk = r * 8 + mq
                        pt = PT[ptc % len(PT)]
                        ptc += 1
                        S.act(pt[:, 0:n], pst[:, 0:n], AF.Exp, [pst], [pt])
                        if c0 == mq * 128:
                            S.tt("dve", pt[:, 0:128], pt[:, 0:128], amk[:, r, :], ALU.mult, [pt, amk], [pt])
                        mms = []
                        for i in range(n // 128):
                            m = (c0 // 128) + i
                            ob = pOA[m // 3]
                            oc = (m % 3) * 129
                            mms.append((ob[:, oc:oc + 129], pt[:, i * 128:(i + 1) * 128], vh_[:, blk, :],
                                        (mq == 0 and r == 0 and m % 3 == 0), (mq == m and r == 3)))
                        S.mm(mms, [pt, vh_], pOA)

                    pend = None
                    for st in steps:
                        pst = qk(st)
                        if pend is not None:
                            rest(*pend)
                        pend = (st, pst)
                    rest(*pend)
                    for bk in range(3):
                        nm = 3 if bk < 2 else 2
                        S.copy("act", Osb[:, mp, 3 * bk:3 * bk + nm, :].rearrange("p m c -> p (m c)"),
                               pOA[bk][:, 0:nm * 129], [pOA[bk]], [Osb])
                S.op("dve", lambda e, r_=rcp[:, :, :], o_=Osb[:, :, :, 128]: e.reciprocal(r_, o_), [Osb], [rcp])
                S.ts("dve", rcp[:, 1, :], rcp[:, 1, :], lamc[:, 1:2], None, ALU.mult, None, [rcp, lamc], [rcp])
                for m in range(8):
                    def src(o_, m=m):
                        S.ts("dve", o_[:, :], Osb[:, 0, m, 0:128], rcp[:, 0, m:m + 1], None, ALU.mult, None, [Osb, rcp], [o_])
                        S.stt("dve", o_[:, :], Osb[:, 1, m, 0:128], rcp[:, 1, m:m + 1], o_[:, :], ALU.mult, ALU.add,
                              [Osb, rcp, o_], [o_])
                    epilogue(src, 128, m, h, [])
            for h in range(8):
                def src(o_, h=h):
                    S.copy("dve", o_[0:NS, :], o_samp[:, h * 128:(h + 1) * 128], [o_samp], [o_])
                epilogue(src, NS, 8, h, [])
            pTb = [PB, PB]
            for m in range(8):
                p = pTb[m % 2]
                S.tr([(p[:, h * 128:(h + 1) * 128], bo[:, m, h * 128:(h + 1) * 128], identb[:, :]) for h in range(8)],
                     [bo, identb], [p])
                S.copy("dve" if m % 2 else "act", b_outT[:, :, m * 128:(m + 1) * 128],
                       p[:, :].rearrange("p (h t) -> p h t", h=8), [p], [b_outT])
            p = pTb[0]
            S.tr([(p[:, h * NS:(h + 1) * NS], bo[0:NS, 8, h * 128:(h + 1) * 128], identb[0:NS, 0:NS]) for h in range(8)],
                 [bo, identb], [p])
            S.copy("dve", b_outT[:, :, TO:NT], p[:, 0:8 * NS].rearrange("p (h s) -> p h s", h=8), [p], [b_outT])
            S.barrier()
        st_p12.close()

        S.enabled = STOP >= 4
        def out_proj_ln(stk, layer, lhs_fn, lhs_bufs, w_dram, xres_dram, xres_dep, sink):
            Wo = [sb(f"Wo{layer}_{i}", [128, 16, 512], BF16, stk) for i in range(4)]
            wv_ = w_dram.rearrange("(k p) c -> p k c", p=128)
            for cb in range(4):
                S.load("pool", Wo[cb][:], wv_[:, :, cb * 512:(cb + 1) * 512], [], [Wo[cb]])
            gb = sb(f"gb{layer}", [128, 2, D], F32, stk)
            S.load("sp", gb[:, 0, :], ln_g[layer:layer + 1, :].partition_broadcast(128), [], [gb])
            S.load("sp", gb[:, 1, :], ln_b[layer:layer + 1, :].partition_broadcast(128), [], [gb])
            xr = [sb(f"xr{layer}_{i}", [128, D], F32, stk) for i in range(2)]
            y0 = [sb(f"y0{layer}_{i}", [128, D], F32, stk) for i in range(2)]
            st_ = [sb(f"lst{layer}_{i}", [128, 24], F32, stk) for i in range(2)]
            mv_ = [sb(f"lmv{layer}_{i}", [128, 4], F32, stk) for i in range(2)]
            for ti, (t0, n) in enumerate(TILES):
                x_, y_, s_, m_ = xr[ti % 2], y0[ti % 2], st_[ti % 2], mv_[ti % 2]
                S.load("sp", x_[0:n, :], xres_dram[t0:t0 + n, :], xres_dep, [x_])
                for cb in range(4):
                    ps = nps()
                    S.mm([(ps[0:n, :], lhs_fn(kt, t0, n), Wo[cb][:, kt, :], kt == 0, kt == 15) for kt in range(16)],
                         lhs_bufs + [Wo[cb]], [ps])
                    S.stt("dve", y_[0:n, cb * 512:(cb + 1) * 512], x_[0:n, cb * 512:(cb + 1) * 512], ALPHA, ps[0:n, :],
                          ALU.mult, ALU.add, [x_, ps], [y_])
                    S.op("dve", lambda e, o=s_[0:n, cb * 6:cb * 6 + 6], i=y_[0:n, cb * 512:(cb + 1) * 512]: e.bn_stats(o, i),
                         [y_], [s_])
                S.op("dve", lambda e, o=m_[0:n, 0:2], i=s_[0:n, 0:24]: e.bn_aggr(o, i), [s_], [m_])
                S.act(m_[0:n, 2:3], m_[0:n, 1:2], AF.Sqrt, [m_], [m_], scale=1.0, bias=epsc[0:n, 0:1])
                S.op("dve", lambda e, o=m_[0:n, 2:3]: e.reciprocal(o, o), [m_], [m_])
                S.ts("dve", y_[0:n, :], y_[0:n, :], m_[0:n, 0:1], m_[0:n, 2:3], ALU.subtract, ALU.mult, [y_, m_], [y_])
                S.tt("pool", y_[0:n, :], y_[0:n, :], gb[0:n, 0, :], ALU.mult, [y_, gb], [y_])
                S.tt("pool", y_[0:n, :], y_[0:n, :], gb[0:n, 1, :], ALU.add, [y_, gb], [y_])
                sink(ti, t0, n, y_)

        st_x1 = Grp()
        x1T = sb("x1T", [128, 16, NX], BF16, st_x1, multi=True)

        with Grp() as st4:
            x1b = [sb(f"x1b{i}", [128, D], BF16, st4) for i in range(2)]
            pTc = [PB, PB]
            pcnt = [0]

            def lhs0(kt, t0, n):
                return a_outT[:, kt, t0:t0 + n] if kt < 8 else b_outT[:, kt - 8, t0:t0 + n]

            def sink0(ti, t0, n, y_):
                S.load("sp", x1s[t0:t0 + n, :], y_[0:n, :], [y_], [d_x1s])
                xb = x1b[ti % 2]
                S.copy("act", xb[0:n, :], y_[0:n, :], [y_], [xb])
                for half in range(2):
                    p = pTc[pcnt[0] % 2]
                    pcnt[0] += 1
                    S.tr([(p[:, k * 128:k * 128 + n], xb[0:n, (half * 8 + k) * 128:(half * 8 + k + 1) * 128], identb[0:n, 0:n])
                          for k in range(8)], [xb, identb], [p])
                    S.copy("dve" if half else "act", x1T[:, half * 8:half * 8 + 8, t0:t0 + n],
                           p[:, :].rearrange("p (k t) -> p k t", k=8)[:, :, 0:n], [p], [x1T])

            out_proj_ln(st4, 0, lhs0, [a_outT, b_outT], w_out, xtok, [], sink0)
            hsv = halo_src.rearrange("(k p) (m c) -> p k m c", p=128, m=8)
            for m in range(8):
                S.load("sp", hsv[:, :, m, :], x1T[:, :, m * 128 + 113:m * 128 + 128], [x1T], [d_hsrc])
            collective(0, halo_src, halo_dst, d_hsrc, d_hdst)
            S.barrier()
        st_ab.close()

        S.enabled = STOP >= 5
        with Grp() as st5:
            st5w = Grp()
            Wc = [sb(f"Wc{i}", [128, 16, 512], BF16, st5w) for i in range(3)]
            wcc = [0]
            wcv = w_in_c.rearrange("(k p) c -> p k c", p=128)

            def loadwc(blk):
                b = Wc[wcc[0] % 3]
                wcc[0] += 1
                S.load("pool", b[:], wcv[:, :, blk * 512:(blk + 1) * 512], [], [b])
                return b
            wbufC = {}

            def getwc(pos):
                for p_ in range(pos + 3):
                    if p_ < 8 and p_ not in wbufC:
                        wbufC[p_] = loadwc(p_)
                return wbufC[pos]
            getwc(0)
            Wg = sb("Wg", [128, 16, 512], BF16, st5w)
            S.load("pool", Wg[:], w_grp.rearrange("(k p) c -> p k c", p=128), [], [Wg])
            with Grp() as st5h:
                Hb = sb("Hb", [128, 16, 4, NH], BF16, st5h)
                hdv = halo_dst.rearrange("(r k p) c -> p k r c", r=4, p=128)
                for r in range(4):
                    S.load("sp", Hb[:, :, r, :], hdv[:, :, r, :], [d_hdst], [Hb])
                selb = sb("selb", [128, 8], F32, st5h)
                S.load("sp", selb[:], sel_d.partition_broadcast(128), [], [selb])
                hA = sb("hA", [128, 16, NH], F32, st5h)
                hB = sb("hB", [128, 16, NH], F32, st5h)
                for (acc, off) in ((hA, 0), (hB, 4)):
                    S.ts("dve", acc[:], Hb[:, :, 0, :], selb[:, off:off + 1], None, ALU.mult, None, [Hb, selb], [acc])
                    for r in range(1, 4):
                        S.stt("dve", acc[:], Hb[:, :, r, :], selb[:, off + r:off + r + 1], acc[:], ALU.mult, ALU.add,
                              [Hb, selb, acc], [acc])
                S.copy("dve", x1T[:, :, NT:NT + 15], hA[:, :, 0:15], [hA], [x1T])
                S.tt("dve", x1T[:, :, NT + 15:NX], hA[:, :, 15:NH], hB[:, :, 0:NH - 15], ALU.add, [hA, hB], [x1T])
                S.barrier()

            pooledT = sb("pooledT", [128, 16, NT], BF16, st5, multi=True)
            sgT = sb("sgT", [128, 16, NT], BF16, st5, multi=True)
            with Grp() as st5a:
                cbs = sb("cbs", [128, 2, 16], F32, st5a)
                S.load("sp", cbs[:, 0, :], c_bT, [], [cbs])
                S.load("sp", cbs[:, 1, :], c_sT, [], [cbs])
                rcb = sb("rcb", [128, 4, 16], F32, st5a)
                S.load("sp", rcb[:].rearrange("p g c -> p (g c)"), rc_d.partition_broadcast(128), [], [rcb])
                hpx = [sb(f"hpx{i}", [128, 8, 143], F32, st5a) for i in range(2)]
                wnA = [sb(f"wnA{i}", [128, 8, 143], F32, st5a) for i in range(1)] * 2
                wnB = [sb(f"wnB{i}", [128, 8, 143], F32, st5a) for i in range(1)] * 2
                hps = [sb(f"hps{i}", [128, NS, 16], F32, st5a) for i in range(2)]
                wns = [sb(f"wns{i}", [128, NS], F32, st5a) for i in range(2)]
                fx = [sb(f"fx{i}", [128, 16], F32, st5a) for i in range(2)]
                stg5 = [sb(f"stg5_{i}", [128, 512], F32, st5a) for i in range(2)]
                s5 = [0]
                hTv = histT.rearrange("(k p) s c -> p k s c", p=128)

                def fm1(wb, c0, chunk, ps):
                    t0, n = chunk
                    S.mm([(ps[:, 0:n], wb[:, kt, c0:c0 + 128], x1T[:, kt, t0:t0 + n], kt == 0, kt == 15)
                          for kt in range(16)], [wb, x1T], [ps])

                def tm1(wb, tile, ps):
                    t0, n = tile
                    S.mm([(ps[0:n, :], x1T[:, kt, t0:t0 + n], wb[:, kt, :], kt == 0, kt == 15)
                          for kt in range(16)], [wb, x1T], [ps])

                S.load("sp", pools_o[:, 0:14, :], hist[:, 1:15, :], [], [])
                for b in range(4):
                    wb = getwc(b)
                    w = 2 ** (b + 1)
                    ps = nps()
                    tm1(wb, TILES[7], ps)
                    sg_ = stg5[s5[0] % 2]
                    s5[0] += 1
                    S.copy("act", sg_[:, :], ps[:, :], [ps], [sg_])
                    S.load("sp", poolp_o[:, b * 512:(b + 1) * 512], sg_[113:128, :], [sg_], [])
                    ps = nps()
                    tm1(wb, TILES[8], ps)
                    sg_ = stg5[s5[0] % 2]
                    s5[0] += 1
                    S.copy("act", sg_[0:NS, :], ps[0:NS, :], [ps], [sg_])
                    S.load("sp", pools_o[:, 14, b * 512:(b + 1) * 512], sg_[0:NS, :], [sg_], [])
                    for cl in range(4):
                        ct = 4 * b + cl
                        hx, wA, wB, hs, ws_, fx_ = hpx[ct % 2], wnA[ct % 2], wnB[ct % 2], hps[ct % 2], wns[ct % 2], fx[ct % 2]
                        for ci, ch in enumerate(CHUNKS_P):
                            ps = nps()
                            fm1(wb, cl * 128, ch, ps)
                            S.copy("act", hx[:, 4 * ci:4 * ci + 4, 15:143], ps[:, :].rearrange("p (m t) -> p m t", m=4), [ps], [hx])
                        ps = nps()
                        fm1(wb, cl * 128, (NT, NH), ps)
                        S.copy("act", hx[:, :, 0:15], ps[:, 0:NH].rearrange("p (m c) -> p m c", m=8), [ps], [hx])
                        ps = nps()
                        fm1(wb, cl * 128, (TO, NS), ps)
                        S.load("sp", hs[:, :, 0:15], hTv[:, ct, :, :], [], [hs])
                        S.copy("act", hs[:, :, 15], ps[:, 0:NS], [ps], [hs])
                        cur, lo, step = hx, 0, 1
                        bufs = [wA, wB]
                        bi_ = 0
                        while step < w:
                            nxt = bufs[bi_ % 2]
                            bi_ += 1
                            nlo = lo + step
                            S.tt("dve", nxt[:, :, nlo:143], cur[:, :, nlo:143], cur[:, :, nlo - step:143 - step], ALU.add,
                                 [cur], [nxt])
                            cur, lo, step = nxt, nlo, step * 2
                        S.stt("dve", pooledT[:, ct, 0:TO].rearrange("p (m t) -> p m t", m=8), cur[:, :, 15:143], 1.0 / w,
                              hx[:, :, 15:143], ALU.mult, ALU.subtract, [cur, hx], [pooledT])
                        S.tt("dve", fx_[:, :], cur[:, 0, 15:31], rcb[:, b, :], ALU.mult, [cur, rcb], [fx_])
                        S.tt("dve", pooledT[:, ct, 0:16], fx_[:, :], hx[:, 0, 15:31], ALU.subtract, [fx_, hx], [pooledT])
                        S.red("dve", ws_[:, :], hs[:, :, 16 - w:16], ALU.add, [hs], [ws_])
                        S.stt("dve", pooledT[:, ct, TO:NT], ws_[:, :], 1.0 / w, hs[:, :, 15], ALU.mult, ALU.subtract,
                              [ws_, hs], [pooledT])
                for b in range(4, 8):
                    wb = getwc(b)
                    for cl in range(4):
                        ct = 4 * (b - 4) + cl
                        for ch in CHUNKS_PS:
                            ps = nps()
                            fm1(wb, cl * 128, ch, ps)
                            S.act(sgT[:, ct, ch[0]:ch[0] + ch[1]], ps[:, 0:ch[1]], AF.Silu, [ps], [sgT])
                t5 = [sb(f"t5_{i}", [128, 512], F32, st5a) for i in range(2)]
                c5 = 0
                for g in range(4):
                    for el in range(4):
                        et = 4 * g + el
                        for ch in CHUNKS_PS:
                            t0, n = ch
                            ps = nps()
                            S.mm([(ps[:, 0:n], Wg[:, 4 * g + kl, el * 128:(el + 1) * 128], pooledT[:, 4 * g + kl, t0:t0 + n],
                                   kl == 0, kl == 3) for kl in range(4)], [Wg, pooledT], [ps])
                            tt_ = t5[c5 % 2]
                            c5 += 1
                            S.ts("dve", tt_[:, 0:n], ps[:, 0:n], cbs[:, 0, et:et + 1], cbs[:, 1, et:et + 1], ALU.add, ALU.mult,
                                 [ps, cbs], [tt_])
                            S.tt("pool", sgT[:, et, t0:t0 + n], tt_[:, 0:n], sgT[:, et, t0:t0 + n], ALU.mult, [tt_, sgT], [sgT])
                S.barrier()

            free_group([pooledT])
            st5w.close()
            st_x1.close()
            with Grp() as st5b:
                def lhs1(kt, t0, n):
                    return sgT[:, kt, t0:t0 + n]

                def sink1(ti, t0, n, y_):
                    S.load("sp", y_o[t0:t0 + n, :], y_[0:n, :], [y_], [])

                out_proj_ln(st5b, 1, lhs1, [sgT], w_out_c, x1s, [d_x1s], sink1)
                S.barrier()
        st_x1.close()

        S.enabled = True
        S.barrier(final=True)
        import sys as _sys
        print("ticks", S.tick, "dma", S.dcount, file=_sys.stderr)

        with nc.Block() as block:
            @block.tensor
            def _(e):
                S.replay("pe", e)

            @block.scalar
            def _(e):
                S.replay("act", e)

            @block.vector
            def _(e):
                S.replay("dve", e)

            @block.gpsimd
            def _(e):
                S.replay("pool", e)

            @block.sync
            def _(e):
                S.replay("sp", e)
    return nc


_PROG = None


def _own_tokens(qi):
    return np.concatenate([np.arange((4 * m + qi) * 128, (4 * m + qi + 1) * 128) for m in range(8)])


def kernel(x_prompt, x_sample, cache_k, cache_v, state_pool, page_table, ln_g, ln_b,
           w_in_ab, a_ln_g, a_ln_b, a_w_s, a_b_s, b_lq1, b_lk1, b_lq2, b_lk2, b_subln_g,
           w_out_ab, w_in_c, c_w_grp, c_b_grp, c_scale, w_out_c):
    global _PROG
    f = lambda a: np.ascontiguousarray(np.asarray(a))
    x_prompt, x_sample = np.asarray(x_prompt), np.asarray(x_sample)
    global NPOOL
    NPOOL = np.asarray(cache_k).shape[1]
    ck = f(cache_k).reshape(NPOOL * 128, 1024)
    cvv = f(cache_v).reshape(NPOOL * 128, 1024)
    page_table = np.asarray(page_table).astype(np.int32)
    state_pool = np.asarray(state_pool)
    a_w_s, a_b_s = np.asarray(a_w_s), np.asarray(a_b_s)

    shared = {
        "cache_k": ck, "cache_v": cvv,
        "w_in": f(np.asarray(w_in_ab)[0]), "w_out": f(np.asarray(w_out_ab)[0]),
        "w_in_c": f(np.asarray(w_in_c)[0]), "w_grp": f(np.asarray(c_w_grp)[0].reshape(2048, 512)),
        "w_out_c": f(np.asarray(w_out_c)[0]),
        "a_ln_g": f(np.asarray(a_ln_g)[0:1]), "a_ln_b": f(np.asarray(a_ln_b)[0:1]),
        "wsT": f(a_w_s[0].transpose(2, 0, 1)),
        "bs": f(a_b_s[0].reshape(1, 512)),
        "w00e": f(np.repeat(a_w_s[0, :, 0, 0], 256)[None, :]),
        "b0e": f(np.repeat(a_b_s[0, :, 0], 256)[None, :]),
        "lqk": f(np.concatenate([np.asarray(b_lq1)[0], np.asarray(b_lk1)[0],
                                 np.asarray(b_lq2)[0], np.asarray(b_lk2)[0]])[None, :]),
        "subln": f(np.asarray(b_subln_g)[0:1]),
        "ln_g": f(ln_g), "ln_b": f(ln_b),
        "c_bT": f(np.asarray(c_b_grp)[0].reshape(16, 128).T),
        "c_sT": f(np.asarray(c_scale)[0].reshape(16, 128).T),
        "ident": np.eye(128, dtype=np.float32),
        "tril": np.triu(np.ones((128, 128), np.float32)),
        "iota_i": np.broadcast_to(np.arange(128, dtype=np.int32)[:, None], (128, 256)).copy(),
    }
    pat = np.zeros((16, 16, 8, 16), np.float32)
    for hm in range(16):
        for s in range(16):
            pat[hm, s, hm // 2, s] = 1.0
    shared["pat"] = pat.reshape(16, -1)
    patoe = np.zeros((16, 2), np.float32)
    patoe[0::2, 0] = 1.0
    patoe[1::2, 1] = 1.0
    shared["patoe"] = patoe

    in_maps = []
    for c in range(NCORES):
        b, qi = c // 4, c % 4
        tok = _own_tokens(qi)
        xp = x_prompt[b][tok]
        xs = x_sample[16 * c:16 * c + 16, 0]
        xt = np.concatenate([xp, xs], 0)
        amask = np.zeros((128, 4, 128), np.float32)
        for r in range(4):
            if r < qi:
                amask[:, r, :] = 1.0
            elif r == qi:
                amask[:, r, :] = np.triu(np.ones((128, 128), np.float32))
        rc = np.zeros((4, 16), np.float32)
        for g, w in enumerate((2, 4, 8, 16)):
            for t in range(16):
                cnt = min(t + 1, w) if qi == 0 else w
                rc[g, t] = 1.0 / cnt
        sel = np.zeros((1, 8), np.float32)
        if qi > 0:
            sel[0, qi - 1] = 1.0
        else:
            sel[0, 4 + 3] = 1.0
        m = dict(shared)
        m.update({
            "xT": f(xt.T), "xtok": f(xt),
            "ptab": f(page_table[16 * c:16 * c + 16].reshape(1, 256)),
            "hist": f(state_pool[0, 16 * c:16 * c + 16]),
            "histT": f(state_pool[0, 16 * c:16 * c + 16].transpose(2, 0, 1)),
            "amask": amask, "rc": rc.reshape(1, 64), "sel": sel,
        })
        in_maps.append(m)

    if _PROG is None:
        _PROG = build_program()
    res = run_bass_kernel_spmd(_PROG, in_maps, core_ids=list(range(NCORES)))
    R = res.results

    y_prompt = np.zeros((2, 4096, 2048), np.float32)
    y_sample = np.zeros((128, 1, 2048), np.float32)
    k_p = np.zeros((1, 2, 4096, 8, 128), np.float32)
    v_p = np.zeros((1, 2, 4096, 8, 128), np.float32)
    k_s = np.zeros((1, 128, 1, 8, 128), np.float32)
    v_s = np.zeros((1, 128, 1, 8, 128), np.float32)
    cv_s = np.zeros((1, 128, 1, 1024), np.float32)
    pool_p = np.zeros((1, 2, 15, 2048), np.float32)
    pool_s = np.zeros((1, 128, 15, 2048), np.float32)
    for c in range(NCORES):
        b, qi = c // 4, c % 4
        tok = _own_tokens(qi)
        r = R[c]
        y_prompt[b, tok] = r["y"][:TO]
        y_sample[16 * c:16 * c + 16, 0] = r["y"][TO:]
        k_p[0, b, tok] = r["knew"][:TO].reshape(TO, 8, 128)
        v_p[0, b, tok] = r["vnew"][:TO].reshape(TO, 8, 128)
        k_s[0, 16 * c:16 * c + 16, 0] = r["knew"][TO:].reshape(NS, 8, 128)
        v_s[0, 16 * c:16 * c + 16, 0] = r["vnew"][TO:].reshape(NS, 8, 128)
        cv_s[0, 16 * c:16 * c + 16, 0] = r["cv"]
        if qi == 3:
            pool_p[0, b] = r["poolp"]
        pool_s[0, 16 * c:16 * c + 16] = r["pools"]
    return (y_prompt, y_sample, k_p, v_p, k_s, v_s, cv_s, pool_p, pool_s)
```
